# Optimizing a Trainium2 kernel written in Bass

```python
import math
import jax, jax.numpy as jnp
from jax import lax
import numpy as np

D_MODEL = 2048
BATCH = 4
SEQ = 4096
DEPTH = 1

N_META = 16
ROPE_THETA = 10000.0
NORM_EPS = 1e-6
DA_HEADS = 4
DA_HEAD_DIM = 128
DA_V_DIM = 2 * DA_HEAD_DIM
DA_Q_WIDTH = DA_HEADS * 2 * DA_HEAD_DIM
DA_V_WIDTH = DA_HEADS * DA_V_DIM
Q_BLOCK = 128
RET_HEADS = 4
RET_QK_DIM = 256
RET_V_DIM = 256
RET_QK_WIDTH = RET_HEADS * RET_QK_DIM
RET_V_WIDTH = RET_HEADS * RET_V_DIM
CHUNK = 128
D_FF = 4 * D_MODEL
IN_WIDTH = 2 * DA_Q_WIDTH + DA_V_WIDTH + 2 * RET_QK_WIDTH + 2 * RET_V_WIDTH + 2 * D_MODEL

kernel_name = 'hybrid_diffattn_retention_gated_block'


def _rms(x):
    xf = x.astype(jnp.float32)
    return xf * lax.rsqrt(jnp.mean(xf * xf, axis=-1, keepdims=True) + NORM_EPS)


def rmsnorm(x, g):
    return (_rms(x) * g.astype(jnp.float32)).astype(x.dtype)


def rope(x, pos):
    d = x.shape[-1]
    half = d // 2
    inv = ROPE_THETA ** (-jnp.arange(half, dtype=jnp.float32) / half)
    ang = pos.astype(jnp.float32)[:, None] * inv[None, :]
    cos = jnp.cos(ang).astype(x.dtype)
    sin = jnp.sin(ang).astype(x.dtype)
    x1, x2 = x[..., :half], x[..., half:]
    return jnp.concatenate([x1 * cos - x2 * sin, x2 * cos + x1 * sin], axis=-1)


def diff_attention(q, k, v, lam, lambda_init, subln_g):
    B, H, _, L, dh = q.shape
    nb = -(-L // Q_BLOCK)
    Lq = nb * Q_BLOCK
    qp = jnp.pad(q, ((0, 0), (0, 0), (0, 0), (0, Lq - L), (0, 0)))
    qb = qp.reshape(B, H, 2, nb, Q_BLOCK, dh).transpose(3, 0, 1, 2, 4, 5)
    scale = dh ** -0.5
    kpos = jnp.arange(L)

    def block(args):
        qblk, i = args
        s = jnp.einsum('bhcqd,bhckd->bhcqk', qblk, k).astype(jnp.float32) * scale
        qpos = i * Q_BLOCK + jnp.arange(Q_BLOCK)
        mask = kpos[None, :] <= qpos[:, None]
        p = jax.nn.softmax(jnp.where(mask, s, -jnp.inf), axis=-1)
        a = p[:, :, 0] - lam * p[:, :, 1]
        return jnp.einsum('bhqk,bhkd->bhqd', a.astype(v.dtype), v)

    o = lax.map(block, (qb, jnp.arange(nb)))
    o = o.transpose(1, 2, 0, 3, 4).reshape(B, H, Lq, v.shape[-1])[:, :, :L]
    return rmsnorm(o, subln_g) * (1.0 - lambda_init)


def retention(q, k, v, log_gamma):
    in_dtype = v.dtype
    q, k, v = (t.astype(jnp.float32) for t in (q, k, v))
    B, H, L, dk = q.shape
    dv = v.shape[-1]
    pad = (-L) % CHUNK
    padw = ((0, 0), (0, 0), (pad, 0), (0, 0))
    n = (L + pad) // CHUNK

    def chunks(t):
        t = jnp.pad(t, padw)
        return t.reshape(B, H, n, CHUNK, t.shape[-1]).transpose(2, 0, 1, 3, 4)

    qc, kc, vc = chunks(q), chunks(k), chunks(v)
    idx = jnp.arange(CHUNK, dtype=jnp.float32)
    lg = log_gamma[:, None, None]
    rel = idx[:, None] - idx[None, :]
    decay_mask = jnp.where(rel >= 0, jnp.exp(lg * jnp.maximum(rel, 0.0)), 0.0)
    xi = jnp.exp(log_gamma[:, None] * (idx[None, :] + 1.0))
    zeta = jnp.exp(log_gamma[:, None] * (CHUNK - 1.0 - idx[None, :]))
    chunk_decay = jnp.exp(log_gamma * CHUNK)

    def step(R, inp):
        qi, ki, vi = inp
        s = jnp.einsum('bhqd,bhkd->bhqk', qi, ki) * decay_mask
        inner = jnp.einsum('bhqk,bhkd->bhqd', s, vi)
        cross = jnp.einsum('bhqd,bhde->bhqe', qi, R) * xi[None, :, :, None]
        R = chunk_decay[None, :, None, None] * R + jnp.einsum(
            'bhkd,bhke->bhde', ki, vi * zeta[None, :, :, None])
        return R, inner + cross

    R0 = jnp.zeros((B, H, dk, dv), jnp.float32)
    _, o = lax.scan(step, R0, (qc, kc, vc))
    o = o.transpose(1, 2, 0, 3, 4).reshape(B, H, n * CHUNK, dv)[:, :, pad:]
    return o.astype(in_dtype)


def hybrid_layer(h, pos, lambda_init, norm1_g, w_in, lam_q1, lam_k1, lam_q2, lam_k2,
                 da_subln_g, w_pa, w_pr, w_o, norm2_g, w_up, w_down):
    B, L, D = h.shape
    xn = rmsnorm(h, norm1_g)
    proj = xn @ w_in
    sizes = [DA_Q_WIDTH, DA_Q_WIDTH, DA_V_WIDTH, RET_QK_WIDTH, RET_QK_WIDTH,
             RET_V_WIDTH, RET_V_WIDTH, D_MODEL, D_MODEL]
    cuts = [sum(sizes[:i + 1]) for i in range(len(sizes) - 1)]
    da_q, da_k, da_v, r_q, r_k, r_v, r_g, g_a, g_r = jnp.split(proj, cuts, axis=-1)

    q = da_q.reshape(B, L, DA_HEADS, 2, DA_HEAD_DIM).transpose(0, 2, 3, 1, 4)
    k = da_k.reshape(B, L, DA_HEADS, 2, DA_HEAD_DIM).transpose(0, 2, 3, 1, 4)
    v = da_v.reshape(B, L, DA_HEADS, DA_V_DIM).transpose(0, 2, 1, 3)
    q, k = rope(q, pos), rope(k, pos)
    lam = (jnp.exp(jnp.sum(lam_q1.astype(jnp.float32) * lam_k1.astype(jnp.float32)))
           - jnp.exp(jnp.sum(lam_q2.astype(jnp.float32) * lam_k2.astype(jnp.float32)))
           + lambda_init)
    oa = diff_attention(q, k, v, lam, lambda_init, da_subln_g)
    y_a = oa.transpose(0, 2, 1, 3).reshape(B, L, DA_V_WIDTH) @ w_pa

    rq = rope(r_q.reshape(B, L, RET_HEADS, RET_QK_DIM).transpose(0, 2, 1, 3), pos)
    rk = rope(r_k.reshape(B, L, RET_HEADS, RET_QK_DIM).transpose(0, 2, 1, 3), pos) * (RET_QK_DIM ** -0.5)
    rv = r_v.reshape(B, L, RET_HEADS, RET_V_DIM).transpose(0, 2, 1, 3)
    log_gamma = jnp.log(1.0 - 2.0 ** (-5.0 - jnp.arange(RET_HEADS, dtype=jnp.float32)))
    orr = retention(rq, rk, rv, log_gamma)
    orr = _rms(orr).astype(h.dtype)
    orr = orr.transpose(0, 2, 1, 3).reshape(B, L, RET_V_WIDTH) * jax.nn.silu(r_g)
    y_r = orr @ w_pr

    merged = jax.nn.sigmoid(g_a) * y_a + jax.nn.sigmoid(g_r) * y_r
    h = h + merged @ w_o

    hn = rmsnorm(h, norm2_g)
    h = h + jnp.square(jax.nn.relu(hn @ w_up)) @ w_down
    return h


def setup_inputs(seed: int = 0) -> dict:
    key = jax.random.key(seed)
    ks = jax.random.split(key, 16)
    f32 = jnp.float32

    def w(k, shape, fan_in):
        return jax.random.normal(k, shape, f32) * fan_in ** -0.5

    def gain(k, shape):
        return 1.0 + 0.02 * jax.random.normal(k, shape, f32)

    return {
        'x': jax.random.normal(ks[0], (BATCH, SEQ, D_MODEL), f32),
        'meta_tokens': jax.random.normal(ks[1], (N_META, D_MODEL), f32),
        'norm1_g': gain(ks[2], (DEPTH, D_MODEL)),
        'w_in': w(ks[3], (DEPTH, D_MODEL, IN_WIDTH), D_MODEL),
        'lam_q1': 0.1 * jax.random.normal(ks[4], (DEPTH, DA_HEAD_DIM), f32),
        'lam_k1': 0.1 * jax.random.normal(ks[5], (DEPTH, DA_HEAD_DIM), f32),
        'lam_q2': 0.1 * jax.random.normal(ks[6], (DEPTH, DA_HEAD_DIM), f32),
        'lam_k2': 0.1 * jax.random.normal(ks[7], (DEPTH, DA_HEAD_DIM), f32),
        'da_subln_g': gain(ks[8], (DEPTH, DA_V_DIM)),
        'w_pa': w(ks[9], (DEPTH, DA_V_WIDTH, D_MODEL), DA_V_WIDTH),
        'w_pr': w(ks[10], (DEPTH, RET_V_WIDTH, D_MODEL), RET_V_WIDTH),
        'w_o': w(ks[11], (DEPTH, D_MODEL, D_MODEL), D_MODEL),
        'norm2_g': gain(ks[12], (DEPTH, D_MODEL)),
        'w_up': w(ks[13], (DEPTH, D_MODEL, D_FF), D_MODEL),
        'w_down': w(ks[14], (DEPTH, D_FF, D_MODEL), D_FF),
        'normf_g': gain(ks[15], (D_MODEL,)),
    }


def reference(x, meta_tokens, norm1_g, w_in, lam_q1, lam_k1, lam_q2, lam_k2, da_subln_g,
              w_pa, w_pr, w_o, norm2_g, w_up, w_down, normf_g):
    B, S, D = x.shape
    meta = jnp.broadcast_to(meta_tokens.astype(x.dtype)[None], (B, N_META, D))
    h = jnp.concatenate([meta, x], axis=1)
    pos = jnp.arange(N_META + S)
    for l in range(DEPTH):
        lambda_init = 0.8 - 0.6 * math.exp(-0.3 * l)
        h = hybrid_layer(h, pos, lambda_init, norm1_g[l], w_in[l], lam_q1[l], lam_k1[l],
                         lam_q2[l], lam_k2[l], da_subln_g[l], w_pa[l], w_pr[l], w_o[l],
                         norm2_g[l], w_up[l], w_down[l])
    h = rmsnorm(h, normf_g)
    return h[:, N_META:]
```

```python
import contextlib
import math
import numpy as np
import concourse.bass as bass
import concourse.mybir as mybir
from concourse.bass_utils import run_bass_kernel_spmd

F32 = mybir.dt.float32
BF16 = mybir.dt.bfloat16
AF = mybir.ActivationFunctionType
ALU = mybir.AluOpType

D = 2048
NPRE = 2176
NOWN = 2048
NTOK = NPRE + NOWN
NCH = NTOK // 128
NPCH = NPRE // 128
DFF = 8192
INW = 11264
EPS = 1e-6
LAMBDA_INIT = 0.8 - 0.6 * math.exp(-0.0)
EPOCH = 24000

C_ID, C_PM, C_MK = 0, 128, 256
C_DT = 384
C_XI = C_DT + 512
C_ZE = C_XI + 2048
C_VA = C_ZE + 4
C_GS = C_VA + 33
C_LV = C_GS + 256
C_NH = C_LV + 512
NCST = C_NH + 1

ENGS = ("pe", "act", "dve", "pool", "sp")
BLK = {"pe": "tensor", "act": "scalar", "dve": "vector", "pool": "gpsimd", "sp": "sync"}


class Buf:
    __slots__ = ("name", "w", "r")

    def __init__(self, name):
        self.name = name
        self.w = None
        self.r = {}


class Op:
    __slots__ = ("eng", "fn", "idx", "deps", "signal", "sigcount", "is_dma", "sem", "val")


class DSem:
    def __init__(self, h):
        self.h = h
        self.count = 0
        self.last = None


class Prog:
    def __init__(self, nc, stack):
        self.nc = nc
        self.stack = stack
        self.streams = {e: [] for e in ENGS}
        self.sigtotal = {e: 0 for e in ENGS}
        self.engsems = {e: [] for e in ENGS}
        self.waited = {e: {} for e in ENGS}
        self.bufs = []
        self.nsem = 0

    def new_sem(self, name):
        self.nsem += 1
        return self.stack.enter_context(self.nc.semaphore(name))

    def dsem(self, name):
        return DSem(self.new_sem("d_" + name))

    def buf(self, name):
        b = Buf(name)
        self.bufs.append(b)
        return b

    def op(self, eng, fn, reads=(), writes=(), dsem=None, ndma=1):
        o = Op()
        o.eng = eng
        o.fn = fn
        o.idx = len(self.streams[eng])
        o.signal = False
        o.sigcount = None
        o.is_dma = dsem is not None
        o.sem = None
        o.val = None
        deps = []

        def add(t, raw):
            if t is None:
                return
            if (not t.is_dma) and t.eng == eng and not o.is_dma:
                if (not raw) or (o.idx - t.idx) > 2:
                    return
            deps.append(t)

        for b in reads:
            add(b.w, True)
        for b in writes:
            add(b.w, False)
            for t in b.r.values():
                add(t, False)
        if dsem is not None:
            add(dsem.last, False)
            dsem.count += 16 * ndma
            o.sem = dsem
            o.val = dsem.count
            dsem.last = o
        for b in reads:
            b.r[("dma", id(o)) if o.is_dma else eng] = o
        for b in writes:
            b.w = o
            b.r = {}
        uniq = []
        for t in deps:
            if not any(t is u for u in uniq):
                uniq.append(t)
                if not t.is_dma:
                    t.signal = True
        o.deps = uniq
        self.streams[eng].append(o)
        return o

    def pe(self, fn, reads=(), writes=()):
        return self.op("pe", fn, reads, writes)

    def act(self, fn, reads=(), writes=()):
        return self.op("act", fn, reads, writes)

    def dve(self, fn, reads=(), writes=()):
        return self.op("dve", fn, reads, writes)

    def pool(self, fn, reads=(), writes=()):
        return self.op("pool", fn, reads, writes)

    def dma(self, eng, dsem, pairs, reads=(), writes=(), **kw):
        h16 = dsem.h

        def fn(h):
            ins = None
            for (o_, i_) in pairs:
                ins = h.dma_start(out=o_, in_=i_, **kw).then_inc(h16, 16)
            return ins
        return self.op(eng, fn, reads, writes, dsem=dsem, ndma=len(pairs))

    def _engsem(self, e, epoch):
        lst = self.engsems[e]
        while len(lst) <= epoch:
            lst.append(self.new_sem("e_%s_%d" % (e, len(lst))))
        return lst[epoch]

    def _emit_stream(self, h, e, ops):
        waited = self.waited[e]
        for o in ops:
            for t in o.deps:
                if t.is_dma:
                    key = ("d", id(t.sem))
                    g = t.val
                    sem, val = t.sem.h, t.val
                else:
                    key = ("e", t.eng)
                    g = t.sigcount
                    ep = (g - 1) // EPOCH
                    sem, val = self._engsem(t.eng, ep), (g - 1) % EPOCH + 1
                if waited.get(key, 0) >= g:
                    continue
                waited[key] = g
                h.wait_ge(sem, val)
            ins = o.fn(h)
            if (not o.is_dma) and o.signal:
                g = o.sigcount
                ins.then_inc(self._engsem(e, (g - 1) // EPOCH), 1)

    def emit_stage(self):
        for e in ENGS:
            for o in self.streams[e]:
                if o.signal and not o.is_dma:
                    self.sigtotal[e] += 1
                    o.sigcount = self.sigtotal[e]
        for e in ENGS:
            if self.sigtotal[e] > 0:
                self._engsem(e, (self.sigtotal[e] - 1) // EPOCH)
        streams = self.streams
        with self.nc.Block() as block:
            for e in ENGS:
                ops = streams[e]
                if not ops:
                    continue

                def body(h, e=e, ops=ops):
                    self._emit_stream(h, e, ops)
                getattr(block, BLK[e])(body)
        self.streams = {e: [] for e in ENGS}
        for b in self.bufs:
            if b.w is not None and not b.w.is_dma:
                b.w = None
            b.r = {k: t for k, t in b.r.items() if t.is_dma}


def mm_group(p, out_ap, pairs, reads, writes, start=True, stop=True):
    def fn(h):
        ins = None
        n = len(pairs)
        for i, (l, r) in enumerate(pairs):
            ins = h.matmul(out_ap, l, r, start=(start and i == 0), stop=(stop and i == n - 1))
        return ins
    return p.pe(fn, reads, writes)


def build(debug=False, upto=99):
    nc = bass.Bass("TRN2", target_bir_lowering=False)

    def dram(name, shape, dt, kind):
        return nc.dram_tensor(name, shape, dt, kind=kind).ap()

    xin = dram("xin", [NTOK, D], F32, "ExternalInput")
    ropeda = dram("ropeda", [128, 2, NTOK], F32, "ExternalInput")
    roper = dram("roper", [128, 2, NTOK], F32, "ExternalInput")
    cst_d = dram("cst", [128, NCST], F32, "ExternalInput")
    gb_d = dram("gb", [128, 3, D], F32, "ExternalInput")
    w_in = dram("w_in", [D, INW], F32, "ExternalInput")
    w_pa = dram("w_pa", [1024, D], F32, "ExternalInput")
    w_pr = dram("w_pr", [1024, D], F32, "ExternalInput")
    w_o = dram("w_o", [D, D], F32, "ExternalInput")
    w_up = dram("w_up", [D, DFF], F32, "ExternalInput")
    w_down = dram("w_down", [DFF, D], F32, "ExternalInput")
    y = dram("y", [NOWN, D], F32, "ExternalOutput")
    sk = "ExternalOutput" if debug else "Internal"
    xnT_d = dram("xnT_d", [128, 16, NTOK], BF16, sk)
    oaT_d = dram("oaT_d", [128, 8, NOWN], BF16, sk)
    orrT_d = dram("orrT_d", [128, 8, NOWN], BF16, sk)
    mT_d = dram("mT_d", [128, 16, NOWN], BF16, sk)
    h1_d = dram("h1_d", [NOWN, D], F32, sk)
    hnT_d = dram("hnT_d", [128, 16, NOWN], BF16, sk)
    wupb_d = dram("wupb_d", [D, DFF], BF16, "Internal")
    wdnb_d = dram("wdnb_d", [DFF, D], BF16, "Internal")
    wupb_v = wupb_d.rearrange("(kc p) c -> p kc c", p=128)
    wdnb_v = wdnb_d.rearrange("(kc p) c -> p kc c", p=128)
    if debug:
        dbgKT = dram("dbgKT", [128, 2, NTOK], BF16, "ExternalOutput")
        dbgV = dram("dbgV", [128, NCH, 257], BF16, "ExternalOutput")
        dbgQT = dram("dbgQT", [128, 2, 512], BF16, "ExternalOutput")
        dbgO = dram("dbgO", [128, 4, 256], F32, "ExternalOutput")
        dbgS = dram("dbgS", [128, 4, 8], F32, "ExternalOutput")

    w_in_v = w_in.rearrange("(kc p) c -> p kc c", p=128)
    w_pa_v = w_pa.rearrange("(kc p) c -> p kc c", p=128)
    w_pr_v = w_pr.rearrange("(kc p) c -> p kc c", p=128)
    w_o_v = w_o.rearrange("(kc p) c -> p kc c", p=128)
    w_up_v = w_up.rearrange("(kc p) c -> p kc c", p=128)
    w_down_v = w_down.rearrange("(kc p) c -> p kc c", p=128)

    with contextlib.ExitStack() as gs:
        p = Prog(nc, gs)

        def sb(stack, name, shape, dt):
            return stack.enter_context(nc.sbuf_tensor("sb_" + name, shape, dt))

        PS = [gs.enter_context(nc.psum_tensor("ps%d" % i, [128, 512], F32)) for i in range(8)]
        PSB = [p.buf("ps%d" % i) for i in range(8)]
        cst = sb(gs, "cst", [128, NCST], F32)
        cstb = sb(gs, "cstb", [128, 384], BF16)
        neglam = sb(gs, "neglam", [128, 1], F32)
        gs8 = sb(gs, "gs8", [128, 256], F32)
        B_cst = p.buf("cst")
        B_cstb = p.buf("cstb")
        B_neglam = p.buf("neglam")
        B_gs8 = p.buf("gs8")
        ident = cstb[:, C_ID:C_ID + 128]
        permm = cstb[:, C_PM:C_PM + 128]
        maskT = cstb[:, C_MK:C_MK + 128]
        negh = cst[:, C_NH:C_NH + 1]

        S_c = p.dsem("cst")
        S_pre = p.dsem("pre")
        B_wpre = p.buf("wpre")
        S_cb = p.dsem("cstb")

        B_xnT = [p.buf("xnT%d" % i) for i in range(NCH)]
        B_oaT = {}
        B_orrT = {}
        B_mT = {}
        B_h1 = [p.buf("h1_%d" % i) for i in range(16)]
        B_hnT = [p.buf("hnT%d" % i) for i in range(16)]
        B_y = [p.buf("y%d" % i) for i in range(16)]

        def gbuf(dct, key, name):
            if key not in dct:
                dct[key] = p.buf("%s_%s" % (name, key))
            return dct[key]

        with contextlib.ExitStack() as s0:
            tmp = sb(s0, "l_tmp", [128, 128], F32)
            s12 = sb(s0, "l_s12", [128, 4], F32)
            B_tmp = p.buf("l_tmp")
            B_s12 = p.buf("l_s12")
            p.dma("sp", S_c, [(cst[:], cst_d[:, :])], writes=[B_cst])
            p.dma("pool", S_cb, [(cstb[:], cst_d[:, 0:384])], writes=[B_cstb])
            for i in range(2):
                p.dve(lambda h, i=i: h.scalar_tensor_tensor(
                    out=tmp[:], in0=cst[:, C_LV + 256 * i:C_LV + 256 * i + 128], scalar=1.0,
                    in1=cst[:, C_LV + 256 * i + 128:C_LV + 256 * i + 256],
                    op0=ALU.mult, op1=ALU.mult, accum_out=s12[:, i:i + 1]),
                    reads=[B_cst], writes=[B_tmp, B_s12])
            p.act(lambda h: h.activation(out=s12[:, 2:4], in_=s12[:, 0:2], func=AF.Exp),
                  reads=[B_s12], writes=[B_s12])
            p.dve(lambda h: h.tensor_tensor(out=neglam[:], in0=s12[:, 3:4], in1=s12[:, 2:3], op=ALU.subtract),
                  reads=[B_s12], writes=[B_neglam])
            p.dve(lambda h: h.tensor_scalar(out=neglam[:], in0=neglam[:], scalar1=-LAMBDA_INIT, scalar2=None, op0=ALU.add),
                  reads=[B_neglam], writes=[B_neglam])
            p.dve(lambda h: h.tensor_scalar(out=gs8[:], in0=cst[:, C_GS:C_GS + 256], scalar1=1.0 - LAMBDA_INIT, scalar2=None, op0=ALU.mult),
                  reads=[B_cst], writes=[B_gs8])
            p.emit_stage()

        def norm_transpose(src, B_src, gbt, B_gbt, xn, B_xn, st4, B_st4, dstT, B_dstT, pa, pb, defer=False):
            p.act(lambda h: h.activation(out=xn[:], in_=src, func=AF.Square, accum_out=st4[:, 0:1]),
                  reads=[B_src], writes=[B_xn, B_st4])
            p.dve(lambda h: h.tensor_scalar(out=st4[:, 1:2], in0=st4[:, 0:1], scalar1=1.0 / D, scalar2=EPS,
                                            op0=ALU.mult, op1=ALU.add), reads=[B_st4], writes=[B_st4])
            p.pool(lambda h: h.tensor_tensor(out=st4[:, 2:3], in0=st4[:, 1:2], in1=negh, op=ALU.pow),
                   reads=[B_st4, B_cst], writes=[B_st4])
            p.dve(lambda h: h.scalar_tensor_tensor(out=xn[:], in0=src, scalar=st4[:, 2:3], in1=gbt,
                                                   op0=ALU.mult, op1=ALU.mult),
                  reads=[B_src, B_st4, B_gbt], writes=[B_xn])
            def part2():
              for half, pi in ((0, pa), (1, pb)):
                pv = PS[pi][:].bitcast(BF16)

                def fn(h, half=half, pv=pv):
                    ins = None
                    for k in range(8):
                        kc = half * 8 + k
                        ins = h.transpose(pv[:, k * 128:(k + 1) * 128], xn[:, kc * 128:(kc + 1) * 128], ident)
                    return ins
                p.pe(fn, reads=[B_xn, B_cstb], writes=[PSB[pi]])
                dst = dstT[:, half * 8:(half + 1) * 8, :]
                if half == 0:
                    p.act(lambda h, pv=pv, dst=dst: h.activation(out=dst, in_=pv.rearrange("p (k t) -> p k t", k=8), func=AF.Copy),
                          reads=[], writes=[PSB[pi], B_dstT])
                else:
                    p.dve(lambda h, pv=pv, dst=dst: h.tensor_copy(out=dst, in_=pv.rearrange("p (k t) -> p k t", k=8)),
                          reads=[], writes=[PSB[pi], B_dstT])
            if defer:
                return part2
            part2()
            return None

        if upto >= 1:
            with contextlib.ExitStack() as s1:
                gbt = sb(s1, "g1", [128, D], F32)
                B_gbt = p.buf("g1")
                S_g = p.dsem("g1")
                p.dma("sp", S_g, [(gbt[:], gb_d[:, 0, :])], writes=[B_gbt])
                NB1 = 4
                xt = [sb(s1, "s1x%d" % i, [128, D], F32) for i in range(NB1)]
                B_xt = [p.buf("s1x%d" % i) for i in range(NB1)]
                S_xt = [p.dsem("s1x%d" % i) for i in range(NB1)]
                xn = [sb(s1, "s1n%d" % i, [128, D], BF16) for i in range(NB1)]
                B_xn = [p.buf("s1n%d" % i) for i in range(NB1)]
                st4 = [sb(s1, "s1s%d" % i, [128, 4], F32) for i in range(NB1)]
                B_st4 = [p.buf("s1s%d" % i) for i in range(NB1)]
                xT = [sb(s1, "s1T%d" % i, [128, 16, 128], BF16) for i in range(NB1)]
                B_xT = [p.buf("s1T%d" % i) for i in range(NB1)]
                S_xT = [p.dsem("s1T%d" % i) for i in range(NB1)]
                for c in range(min(NB1 - 1, NCH)):
                    p.dma("sp", S_xt[c % NB1], [(xt[c % NB1][:], xin[c * 128:(c + 1) * 128, :])], writes=[B_xt[c % NB1]])
                for c in range(NCH):
                    s = c % NB1
                    cn = c + NB1 - 1
                    if cn < NCH:
                        p.dma("sp", S_xt[cn % NB1], [(xt[cn % NB1][:], xin[cn * 128:(cn + 1) * 128, :])], writes=[B_xt[cn % NB1]])
                    norm_transpose(xt[s][:], B_xt[s], gbt[:], B_gbt, xn[s], B_xn[s], st4[s], B_st4[s],
                                   xT[s], B_xT[s], (c % 4) * 2, (c % 4) * 2 + 1)
                    p.dma("sp", S_xT[s], [(xnT_d[:, :, c * 128:(c + 1) * 128], xT[s][:])],
                          reads=[B_xT[s]], writes=[B_xnT[c]])
                p.emit_stage()

        pre_pairs = [(wupb_v[:, a, :], w_up_v[:, a, :]) for a in range(16)] + \
                    [(wdnb_v[:, 4 * a:4 * a + 4, :], w_down_v[:, 4 * a:4 * a + 4, :]) for a in range(16)]

        def pre_cast_one():
            if pre_pairs:
                p.dma("pool", S_pre, [pre_pairs.pop(0)], writes=[B_wpre])

        tiles = [(0, 512), (512, 512), (1024, 512), (1536, 512), (2048, 128)] + \
                [(NPRE + 512 * j, 512) for j in range(4)]

        if upto >= 2:
            with contextlib.ExitStack() as s2:
                NT = len(tiles)
                W = [sb(s2, "s2w%d" % i, [128, 16, 768], BF16) for i in range(2)]
                B_W = [p.buf("s2w%d" % i) for i in range(2)]
                S_W = [p.dsem("s2w%d" % i) for i in range(2)]
                xt = [sb(s2, "s2x%d" % i, [128, 16, 512], BF16) for i in range(2)]
                B_xt = [p.buf("s2x%d" % i) for i in range(2)]
                S_xt = [p.dsem("s2x%d" % i) for i in range(2)]
                rt = [sb(s2, "s2r%d" % i, [128, 2, 512], F32) for i in range(2)]
                B_rt = [p.buf("s2r%d" % i) for i in range(2)]
                S_rt = [p.dsem("s2r%d" % i) for i in range(2)]
                KT = [sb(s2, "s2KT%d" % i, [128, 2, NTOK], BF16) for i in range(2)]
                B_KT = [[p.buf("s2KT%d_%d" % (i, t)) for t in range(NT)] for i in range(2)]
                V = [sb(s2, "s2V%d" % i, [128, NCH, 257], BF16) for i in range(2)]
                B_V = [[p.buf("s2V%d_%d" % (i, t)) for t in range(NT)] for i in range(2)]
                QT = [sb(s2, "s2QT%d" % i, [128, 2, 512], BF16) for i in range(2)]
                B_QT = [p.buf("s2QT%d" % i) for i in range(2)]
                ksb = [sb(s2, "s2ksb%d" % i, [128, 512], BF16) for i in range(2)]
                B_ksb = [p.buf("s2ksb%d" % i) for i in range(2)]
                t1 = [sb(s2, "s2t1%d" % i, [128, 512], F32) for i in range(2)]
                B_t1 = [p.buf("s2t1%d" % i) for i in range(2)]
                t2 = [sb(s2, "s2t2%d" % i, [128, 512], F32) for i in range(2)]
                B_t2 = [p.buf("s2t2%d" % i) for i in range(2)]
                pt = [sb(s2, "s2pt%d" % i, [128, 512], BF16) for i in range(3)]
                B_pt = [p.buf("s2pt%d" % i) for i in range(3)]
                o1 = sb(s2, "s2o1", [128, 4, 256], F32)
                B_o1 = [p.buf("s2o1_%d" % i) for i in range(4)]
                sq = sb(s2, "s2sq", [128, 256], F32)
                B_sq = p.buf("s2sq")
                stt_ = sb(s2, "s2st", [128, 4, 8], F32)
                B_st = [p.buf("s2st%d" % i) for i in range(4)]
                oab = [sb(s2, "s2oab%d" % i, [128, 256], BF16) for i in range(2)]
                B_oab = [p.buf("s2oab%d" % i) for i in range(2)]
                oaT = [sb(s2, "s2oaT%d" % i, [128, 2, 512], BF16) for i in range(2)]
                B_oaTt = [p.buf("s2oaT%d" % i) for i in range(2)]
                S_oaT = [p.dsem("s2oaT%d" % i) for i in range(2)]

                for i in range(2):
                    p.dve(lambda h, i=i: h.tensor_copy(out=V[i][:, :, 256], in_=cst[:, C_VA:C_VA + NCH]),
                          reads=[B_cst], writes=B_V[i])

                def load_W(hd):
                    s = hd % 2
                    pairs = [(W[s][:, :, 256 * i:256 * (i + 1)],
                              w_in_v[:, :, 1024 * i + 256 * hd:1024 * i + 256 * (hd + 1)]) for i in range(3)]
                    p.dma("pool", S_W[s], pairs, writes=[B_W[s]])

                rr = {"proj": 0, "ks": 0, "pt": 0, "oab": 0, "oaT": 0}
                scale = 1.0 / math.sqrt(128.0)

                def tile_of(ch):
                    if ch < 16:
                        return ch // 4
                    if ch == 16:
                        return 4
                    return 5 + (ch - NPCH) // 4

                def rope_steps(Wt, B_Wt, col0, ls, n, dst, B_dst):
                    pb = rr["proj"] % 2
                    rr["proj"] += 1
                    pb2 = 1 - pb
                    ki = rr["ks"] % 2
                    rr["ks"] += 1
                    steps = []
                    for g in range(8):
                        def st(g=g):
                            pairs = [(Wt[:, kc, col0:col0 + 128], xt[ls][:, kc, 0:n]) for kc in (2 * g, 2 * g + 1)]
                            mm_group(p, PS[pb][:, 0:n], pairs, [B_Wt, B_xt[ls]], [PSB[pb]], start=(g == 0), stop=(g == 7))
                            if g == 7:
                                p.act(lambda h: h.activation(out=ksb[ki][:, 0:n], in_=PS[pb][:, 0:n], func=AF.Copy),
                                      writes=[PSB[pb], B_ksb[ki]])
                                p.dve(lambda h: h.tensor_tensor(out=t1[ki][:, 0:n], in0=PS[pb][:, 0:n], in1=rt[ls][:, 0, 0:n], op=ALU.mult),
                                      reads=[B_rt[ls]], writes=[PSB[pb], B_t1[ki]])
                        steps.append(st)

                    def st_perm():
                        mm_group(p, PS[pb2][:, 0:n], [(permm, ksb[ki][:, 0:n])], [B_cstb, B_ksb[ki]], [PSB[pb2]])
                        p.dve(lambda h: h.tensor_tensor(out=t2[ki][:, 0:n], in0=PS[pb2][:, 0:n], in1=rt[ls][:, 1, 0:n], op=ALU.mult),
                              reads=[B_rt[ls]], writes=[PSB[pb2], B_t2[ki]])
                        p.pool(lambda h: h.tensor_tensor(out=dst, in0=t1[ki][:, 0:n], in1=t2[ki][:, 0:n], op=ALU.add),
                               reads=[B_t1[ki], B_t2[ki]], writes=[B_dst])
                    steps.append(st_perm)
                    return steps

                def proj_steps(hd, ti, lidx, qp):
                    s0_, n = tiles[ti]
                    own = s0_ >= NPRE
                    hp = hd % 2
                    ws = hd % 2
                    ls = lidx % 2
                    c0 = s0_ // 128
                    nchk = n // 128
                    steps = []

                    def st_load():
                        if ti == 0 and hd + 1 < 4:
                            load_W(hd + 1)
                        p.dma("sp", S_xt[ls], [(xt[ls][:, :, 0:n], xnT_d[:, :, s0_:s0_ + n])],
                              reads=[B_xnT[c0 + i] for i in range(nchk)], writes=[B_xt[ls]])
                        p.dma("sp", S_rt[ls], [(rt[ls][:, :, 0:n], ropeda[:, :, s0_:s0_ + n])], writes=[B_rt[ls]])
                    steps.append(st_load)
                    for c in range(2):
                        steps += rope_steps(W[ws], B_W[ws], 256 + 128 * c, ls, n, KT[hp][:, c, s0_:s0_ + n], B_KT[hp][ti])
                    for cc in range(nchk):
                        pb = rr["proj"] % 2
                        rr["proj"] += 1
                        for g in range(8):
                            def st(g=g, cc=cc, pb=pb):
                                pairs = [(xt[ls][:, kc, cc * 128:(cc + 1) * 128], W[ws][:, kc, 512:768]) for kc in (2 * g, 2 * g + 1)]
                                mm_group(p, PS[pb][:, 0:256], pairs, [B_W[ws], B_xt[ls]], [PSB[pb]], start=(g == 0), stop=(g == 7))
                                if g == 7:
                                    p.dve(lambda h: h.tensor_copy(out=V[hp][:, c0 + cc, 0:256], in_=PS[pb][:, 0:256]),
                                          writes=[PSB[pb], B_V[hp][ti]])
                            steps.append(st)
                    if own:
                        for c in range(2):
                            steps += rope_steps(W[ws], B_W[ws], 128 * c, ls, n, QT[qp][:, c, :], B_QT[qp])
                    return steps

                queue = []
                lidx = 0
                qcnt = 0
                qp_of = {}
                for hd in range(4):
                    for ti in range(NT):
                        own = tiles[ti][0] >= NPRE
                        qp = qcnt % 2
                        if own:
                            qp_of[(hd, ti)] = qp
                            qcnt += 1
                        for stp in proj_steps(hd, ti, lidx, qp):
                            queue.append((hd * NT + ti, stp))
                        lidx += 1
                qpos = {"i": 0}

                def ensure(tag):
                    while qpos["i"] < len(queue) and queue[qpos["i"]][0] <= tag:
                        queue[qpos["i"]][1]()
                        qpos["i"] += 1

                def pull(k, limit_tag):
                    for _ in range(k):
                        if qpos["i"] < len(queue) and queue[qpos["i"]][0] <= limit_tag:
                            queue[qpos["i"]][1]()
                            qpos["i"] += 1

                def attention(hd, ti):
                    s0_, n = tiles[ti]
                    hp = hd % 2
                    qp = qp_of[(hd, ti)]
                    j = (s0_ - NPRE) // 512
                    limit_tag = hd * NT + ti + 1 if ti + 1 < NT else (hd + 1) * NT + 5
                    budget = {"steps": sum(1 for k_ in range(qpos["i"], len(queue)) if queue[k_][0] <= limit_tag),
                              "iters": 0}
                    nfull = NPCH + 4 * j
                    seq = [(ch, None) for ch in range(nfull)] + [(nfull + k, k) for k in range(4)]
                    budget["iters"] = 2 * len(seq)
                    for c in range(2):
                        slots = {}

                        def QK(i, c=c):
                            ch, dg = seq[i]
                            sbk = 2 + (i % 2)
                            q0 = 0 if dg is None else dg * 128
                            mm_group(p, PS[sbk][:, q0:512], [(KT[hp][:, c, ch * 128:(ch + 1) * 128], QT[qp][:, c, q0:512])],
                                     [B_KT[hp][tile_of(ch)], B_QT[qp]], [PSB[sbk]])
                            ps_ = rr["pt"] % 3
                            rr["pt"] += 1
                            slots[i] = ps_
                            p.act(lambda h: h.activation(out=pt[ps_][:, q0:512], in_=PS[sbk][:, q0:512], func=AF.Exp, scale=scale),
                                  writes=[PSB[sbk], B_pt[ps_]])
                            if dg is not None:
                                p.pool(lambda h: h.tensor_tensor(out=pt[ps_][:, q0:q0 + 128], in0=pt[ps_][:, q0:q0 + 128], in1=maskT, op=ALU.mult),
                                       reads=[B_cstb], writes=[B_pt[ps_]])

                        def AV(i, c=c):
                            ch, dg = seq[i]
                            ps_ = slots[i]
                            qb0 = 0 if dg is None else dg
                            for qb in range(qb0, 4):
                                last = (dg is not None and dg == qb)
                                mm_group(p, PS[4 + qb][:, 0:257], [(pt[ps_][:, qb * 128:(qb + 1) * 128], V[hp][:, ch, :])],
                                         [B_pt[ps_], B_V[hp][tile_of(ch)]], [PSB[4 + qb]], start=(i == 0), stop=last)
                        QK(0)
                        for i in range(len(seq)):
                            if i + 1 < len(seq):
                                QK(i + 1)
                            take = -(-budget["steps"] // max(budget["iters"], 1))
                            if c == 1 and i == 0:
                                take += 8
                            take = min(take, budget["steps"], 12)
                            pull(take, limit_tag)
                            budget["steps"] -= take
                            budget["iters"] -= 1
                            AV(i)
                        for qb in range(4):
                            ob = PS[4 + qb]
                            if c == 0:
                                p.dve(lambda h, ob=ob, qb=qb: h.reciprocal(out=stt_[:, qb, 0:1], in_=ob[:, 256:257]),
                                      writes=[PSB[4 + qb], B_st[qb]])
                                p.dve(lambda h, ob=ob, qb=qb: h.tensor_scalar(out=o1[:, qb, :], in0=ob[:, 0:256], scalar1=stt_[:, qb, 0:1],
                                                                               scalar2=None, op0=ALU.mult),
                                      reads=[B_st[qb]], writes=[PSB[4 + qb], B_o1[qb]])
                            else:
                                p.dve(lambda h, ob=ob, qb=qb: h.reciprocal(out=stt_[:, qb, 1:2], in_=ob[:, 256:257]),
                                      writes=[PSB[4 + qb], B_st[qb]])
                                p.dve(lambda h, qb=qb: h.tensor_tensor(out=stt_[:, qb, 2:3], in0=stt_[:, qb, 1:2], in1=neglam[:], op=ALU.mult),
                                      reads=[B_neglam, B_st[qb]], writes=[B_st[qb]])
                                p.dve(lambda h, ob=ob, qb=qb: h.scalar_tensor_tensor(out=o1[:, qb, :], in0=ob[:, 0:256], scalar=stt_[:, qb, 2:3],
                                                                                      in1=o1[:, qb, :], op0=ALU.mult, op1=ALU.add),
                                      reads=[B_st[qb]], writes=[PSB[4 + qb], B_o1[qb]])
                    os_ = rr["oaT"] % 2
                    rr["oaT"] += 1
                    tb = 2
                    pv = PS[tb][:].bitcast(BF16)
                    for qb in range(4):
                        ab = rr["oab"] % 2
                        rr["oab"] += 1
                        p.dve(lambda h, qb=qb: h.scalar_tensor_tensor(out=sq[:], in0=o1[:, qb, :], scalar=1.0, in1=o1[:, qb, :],
                                                                      op0=ALU.mult, op1=ALU.mult, accum_out=stt_[:, qb, 3:4]),
                              reads=[B_o1[qb]], writes=[B_sq, B_st[qb]])
                        p.dve(lambda h, qb=qb: h.tensor_scalar(out=stt_[:, qb, 4:5], in0=stt_[:, qb, 3:4], scalar1=1.0 / 256, scalar2=EPS,
                                                               op0=ALU.mult, op1=ALU.add), reads=[B_st[qb]], writes=[B_st[qb]])
                        p.pool(lambda h, qb=qb: h.tensor_tensor(out=stt_[:, qb, 5:6], in0=stt_[:, qb, 4:5], in1=negh, op=ALU.pow),
                               reads=[B_st[qb], B_cst], writes=[B_st[qb]])
                        p.dve(lambda h, qb=qb, ab=ab: h.scalar_tensor_tensor(out=oab[ab][:], in0=o1[:, qb, :], scalar=stt_[:, qb, 5:6], in1=gs8[:],
                                                                             op0=ALU.mult, op1=ALU.mult),
                              reads=[B_o1[qb], B_st[qb], B_gs8], writes=[B_oab[ab]])
                        pull(2, limit_tag)

                        def tfn(h, qb=qb, ab=ab, pv=pv):
                            ins = None
                            for b in range(2):
                                ins = h.transpose(pv[:, b * 512 + qb * 128:b * 512 + (qb + 1) * 128], oab[ab][:, b * 128:(b + 1) * 128], ident)
                            return ins
                        p.pe(tfn, reads=[B_oab[ab], B_cstb], writes=[PSB[tb]])
                    p.act(lambda h, os_=os_, pv=pv: h.activation(out=oaT[os_][:], in_=pv.rearrange("p (b t) -> p b t", b=2), func=AF.Copy),
                          writes=[PSB[tb], B_oaTt[os_]])
                    p.dma("sp", S_oaT[os_], [(oaT_d[:, 2 * hd:2 * hd + 2, j * 512:(j + 1) * 512], oaT[os_][:])],
                          reads=[B_oaTt[os_]], writes=[gbuf(B_oaT, (hd, j), "oaT")])

                load_W(0)
                for hd in range(4):
                    for ti in range(NT):
                        if tiles[ti][0] < NPRE:
                            continue
                        ensure(hd * NT + ti)
                        attention(hd, ti)
                ensure(10 ** 9)
                if debug:
                    S_dbg = p.dsem("dbg")
                    B_dbg = p.buf("dbg")
                    p.dma("sp", S_dbg, [(dbgKT[:, :, :], KT[1][:]), (dbgV[:, :, :], V[1][:]), (dbgQT[:, :, :], QT[1][:]), (dbgO[:, :, :], o1[:]), (dbgS[:, :, :], stt_[:])],
                          reads=B_KT[1] + B_V[1] + [B_QT[1]] + B_o1 + B_st, writes=[B_dbg])
                p.emit_stage()

        if upto >= 3:
            with contextlib.ExitStack() as s3:
                W = [sb(s3, "s3w%d" % i, [128, 16, 1024], BF16) for i in range(2)]
                B_W = [p.buf("s3w%d" % i) for i in range(2)]
                S_W = [p.dsem("s3w%d" % i) for i in range(2)]
                xt = [sb(s3, "s3x%d" % i, [128, 16, 512], BF16) for i in range(2)]
                B_xt = [p.buf("s3x%d" % i) for i in range(2)]
                S_xt = [p.dsem("s3x%d" % i) for i in range(2)]
                rt = [sb(s3, "s3r%d" % i, [128, 2, 512], F32) for i in range(2)]
                B_rt = [p.buf("s3r%d" % i) for i in range(2)]
                S_rt = [p.dsem("s3r%d" % i) for i in range(2)]
                ta = [sb(s3, "s3ta%d" % i, [128, 512], F32) for i in range(2)]
                B_ta = [p.buf("s3ta%d" % i) for i in range(2)]
                tb_ = [sb(s3, "s3tb%d" % i, [128, 512], F32) for i in range(2)]
                B_tb = [p.buf("s3tb%d" % i) for i in range(2)]
                rkT = [sb(s3, "s3rkT%d" % i, [128, 2, 512], BF16) for i in range(2)]
                B_rkT = [p.buf("s3rkT%d" % i) for i in range(2)]
                rqT = [sb(s3, "s3rqT%d" % i, [128, 2, 512], BF16) for i in range(2)]
                B_rqT = [p.buf("s3rqT%d" % i) for i in range(2)]
                rqX = [sb(s3, "s3rqX%d" % i, [128, 2, 512], BF16) for i in range(2)]
                B_rqX = [p.buf("s3rqX%d" % i) for i in range(2)]
                rkt = [sb(s3, "s3rkt%d" % i, [128, 4, 256], BF16) for i in range(2)]
                B_rkt = [p.buf("s3rkt%d" % i) for i in range(2)]
                rv = [sb(s3, "s3rv%d" % i, [128, 4, 256], BF16) for i in range(2)]
                B_rv = [p.buf("s3rv%d" % i) for i in range(2)]
                rvz = [sb(s3, "s3rvz%d" % i, [128, 4, 256], BF16) for i in range(2)]
                B_rvz = [p.buf("s3rvz%d" % i) for i in range(2)]
                sg = [sb(s3, "s3sg%d" % i, [128, 4, 256], BF16) for i in range(2)]
                B_sg = [p.buf("s3sg%d" % i) for i in range(2)]
                R32 = sb(s3, "s3R32", [128, 512], F32)
                B_R32 = p.buf("s3R32")
                Rb = sb(s3, "s3Rb", [128, 512], BF16)
                B_Rb = p.buf("s3Rb")
                Rb2 = sb(s3, "s3Rb2", [128, 512], BF16)
                B_Rb2 = p.buf("s3Rb2")
                sm = [sb(s3, "s3sm%d" % i, [128, 128], BF16) for i in range(2)]
                B_sm = [p.buf("s3sm%d" % i) for i in range(2)]
                osb = [sb(s3, "s3osb%d" % i, [128, 256], F32) for i in range(2)]
                B_osb = [p.buf("s3osb%d" % i) for i in range(2)]
                sq = sb(s3, "s3sq", [128, 256], F32)
                B_sq = p.buf("s3sq")
                st3 = [sb(s3, "s3st%d" % i, [128, 4], F32) for i in range(2)]
                B_st3 = [p.buf("s3st%d" % i) for i in range(2)]
                onb = [sb(s3, "s3onb%d" % i, [128, 256], BF16) for i in range(2)]
                B_onb = [p.buf("s3onb%d" % i) for i in range(2)]
                orT = [sb(s3, "s3orT%d" % i, [128, 2, 512], BF16) for i in range(2)]
                B_orTt = [p.buf("s3orT%d" % i) for i in range(2)]
                S_orT = [p.dsem("s3orT%d" % i) for i in range(2)]
                gam = [1.0 - 2.0 ** (-5.0 - hh) for hh in range(4)]
                lg = [float(np.log(np.float32(g))) for g in gam]

                def load_W3(hd):
                    s = hd % 2
                    offs = [3072, 4096, 5120, 6144]
                    pairs = [(W[s][:, :, 256 * i:256 * (i + 1)],
                              w_in_v[:, :, offs[i] + 256 * hd:offs[i] + 256 * (hd + 1)]) for i in range(4)]
                    p.dma("pool", S_W[s], pairs, writes=[B_W[s]])

                rr = {"ld": 0, "tl": 0, "ch": 0, "orT": 0}

                def rope2(Wt, B_Wt, col0, ls, n, dstT, B_dst):
                    for b in range(2):
                        pairs = [(Wt[:, kc, col0 + 128 * b:col0 + 128 * (b + 1)], xt[ls][:, kc, 0:n]) for kc in range(16)]
                        mm_group(p, PS[b][:, 0:n], pairs, [B_Wt, B_xt[ls]], [PSB[b]])
                    cr = rt[ls][:, 0, 0:n]
                    sr = rt[ls][:, 1, 0:n]
                    p.dve(lambda h: h.tensor_tensor(out=ta[0][:, 0:n], in0=PS[0][:, 0:n], in1=cr, op=ALU.mult),
                          reads=[B_rt[ls]], writes=[PSB[0], B_ta[0]])
                    p.dve(lambda h: h.tensor_tensor(out=tb_[0][:, 0:n], in0=PS[1][:, 0:n], in1=sr, op=ALU.mult),
                          reads=[B_rt[ls]], writes=[PSB[1], B_tb[0]])
                    p.pool(lambda h: h.tensor_tensor(out=dstT[:, 0, 0:n], in0=ta[0][:, 0:n], in1=tb_[0][:, 0:n], op=ALU.subtract),
                           reads=[B_ta[0], B_tb[0]], writes=[B_dst])
                    p.dve(lambda h: h.tensor_tensor(out=ta[1][:, 0:n], in0=PS[1][:, 0:n], in1=cr, op=ALU.mult),
                          reads=[B_rt[ls]], writes=[PSB[1], B_ta[1]])
                    p.dve(lambda h: h.tensor_tensor(out=tb_[1][:, 0:n], in0=PS[0][:, 0:n], in1=sr, op=ALU.mult),
                          reads=[B_rt[ls]], writes=[PSB[0], B_tb[1]])
                    p.pool(lambda h: h.tensor_tensor(out=dstT[:, 1, 0:n], in0=ta[1][:, 0:n], in1=tb_[1][:, 0:n], op=ALU.add),
                           reads=[B_ta[1], B_tb[1]], writes=[B_dst])

                Rbs = [Rb, Rb2]
                B_Rbs = [B_Rb, B_Rb2]
                cds = [float(np.exp(np.float32(lg[hh_]) * np.float32(128.0))) for hh_ in range(4)]

                def proj_units(hd, ti, idx):
                    s0_, n = tiles[ti]
                    own = s0_ >= NPRE
                    ls = idx % 2
                    tl = idx % 2
                    ws = hd % 2
                    c0 = s0_ // 128
                    nchk = n // 128
                    U = []

                    def u_load():
                        if ti == 0 and hd + 1 < 4:
                            load_W3(hd + 1)
                        p.dma("sp", S_xt[ls], [(xt[ls][:, :, 0:n], xnT_d[:, :, s0_:s0_ + n])],
                              reads=[B_xnT[c0 + i] for i in range(nchk)], writes=[B_xt[ls]])
                        p.dma("sp", S_rt[ls], [(rt[ls][:, :, 0:n], roper[:, :, s0_:s0_ + n])], writes=[B_rt[ls]])
                    U.append(u_load)
                    U.append(lambda: rope2(W[ws], B_W[ws], 256, ls, n, rkT[tl], B_rkT[tl]))

                    def u_tr():
                        pv2 = PS[2][:].bitcast(BF16)

                        def tfn(h):
                            ins = None
                            for cc in range(nchk):
                                for b in range(2):
                                    ins = h.transpose(pv2[:, cc * 256 + b * 128:cc * 256 + (b + 1) * 128],
                                                      rkT[tl][:, b, cc * 128:(cc + 1) * 128], ident)
                            return ins
                        p.pe(tfn, reads=[B_rkT[tl], B_cstb], writes=[PSB[2]])
                        p.act(lambda h: h.activation(
                            out=rkt[tl][:, 0:nchk, :], in_=pv2[:, 0:nchk * 256].rearrange("p (c d) -> p c d", c=nchk), func=AF.Copy),
                            writes=[PSB[2], B_rkt[tl]])
                    U.append(u_tr)

                    def u_rv(cc):
                        pairs = [(xt[ls][:, kc, cc * 128:(cc + 1) * 128], W[ws][:, kc, 512:768]) for kc in range(16)]
                        mm_group(p, PS[3][:, 0:256], pairs, [B_W[ws], B_xt[ls]], [PSB[3]])
                        if own:
                            p.act(lambda h: h.activation(out=rv[tl][:, cc, :], in_=PS[3][:, 0:256], func=AF.Copy),
                                  writes=[PSB[3], B_rv[tl]])
                        p.act(lambda h: h.activation(out=rvz[tl][:, cc, :], in_=PS[3][:, 0:256], func=AF.Copy,
                                                     scale=cst[:, C_ZE + hd:C_ZE + hd + 1]),
                              reads=[B_cst], writes=[PSB[3], B_rvz[tl]])
                    for cc in range(nchk):
                        U.append(lambda cc=cc: u_rv(cc))
                    if own:
                        U.append(lambda: rope2(W[ws], B_W[ws], 0, ls, n, rqT[tl], B_rqT[tl]))

                        def u_rqx():
                            for b in range(2):
                                p.pool(lambda h, b=b: h.tensor_tensor(
                                    out=rqX[tl][:, b, :], in0=rqT[tl][:, b, :],
                                    in1=cst[:, C_XI + 512 * hd:C_XI + 512 * (hd + 1)], op=ALU.mult),
                                    reads=[B_rqT[tl], B_cst], writes=[B_rqX[tl]])
                        U.append(u_rqx)

                        def u_rg(cc):
                            pairs = [(xt[ls][:, kc, cc * 128:(cc + 1) * 128], W[ws][:, kc, 768:1024]) for kc in range(16)]
                            mm_group(p, PS[3][:, 0:256], pairs, [B_W[ws], B_xt[ls]], [PSB[3]])
                            p.act(lambda h: h.activation(out=sg[tl][:, cc, :], in_=PS[3][:, 0:256], func=AF.Silu),
                                  writes=[PSB[3], B_sg[tl]])
                        for cc in range(nchk):
                            U.append(lambda cc=cc: u_rg(cc))
                    return U

                gch = {"n": 0}

                def rec_steps(hd, ti, idx):
                    s0_, n = tiles[ti]
                    own = s0_ >= NPRE
                    tl = idx % 2
                    nchk = n // 128
                    cd = cds[hd]
                    j = (s0_ - NPRE) // 512 if own else None
                    os_ = None
                    if own:
                        os_ = rr["orT"] % 2
                        rr["orT"] += 1
                    pv7 = PS[7][:].bitcast(BF16)
                    out = []
                    for cc in range(nchk):
                        g = gch["n"]
                        gch["n"] += 1
                        ci = g % 2
                        first = (ti == 0 and cc == 0)
                        csl = slice(cc * 128, (cc + 1) * 128)
                        Rprev, B_Rprev = Rbs[(g + 1) % 2], B_Rbs[(g + 1) % 2]
                        Rnew, B_Rnew = Rbs[g % 2], B_Rbs[g % 2]
                        st = {}

                        def A(ci=ci, csl=csl):
                            mm_group(p, PS[4][:, 0:128], [(rkT[tl][:, b, csl], rqT[tl][:, b, csl]) for b in range(2)],
                                     [B_rkT[tl], B_rqT[tl]], [PSB[4]])
                            p.dve(lambda h: h.tensor_tensor(out=sm[ci][:], in0=PS[4][:, 0:128],
                                                            in1=cst[:, C_DT + 128 * hd:C_DT + 128 * (hd + 1)], op=ALU.mult),
                                  reads=[B_cst], writes=[PSB[4], B_sm[ci]])

                        def Dst(cc=cc, first=first, Rnew=Rnew, B_Rnew=B_Rnew):
                            if first:
                                p.pool(lambda h: h.memset(R32[:], 0.0), writes=[B_R32])
                            for b in range(2):
                                mm_group(p, PS[6][:, b * 256:(b + 1) * 256], [(rkt[tl][:, cc, b * 128:(b + 1) * 128], rvz[tl][:, cc, :])],
                                         [B_rkt[tl], B_rvz[tl]], [PSB[6]])
                            p.dve(lambda h: h.scalar_tensor_tensor(out=R32[:], in0=R32[:], scalar=cd, in1=PS[6][:],
                                                                   op0=ALU.mult, op1=ALU.add),
                                  writes=[PSB[6], B_R32])
                            p.pool(lambda h: h.tensor_copy(out=Rnew[:], in_=R32[:]), reads=[B_R32], writes=[B_Rnew])

                        def Bst(ci=ci, cc=cc, csl=csl, first=first, Rprev=Rprev, B_Rprev=B_Rprev):
                            pairs = [(sm[ci][:], rv[tl][:, cc, :])]
                            rds = [B_sm[ci], B_rv[tl]]
                            if not first:
                                pairs += [(rqX[tl][:, b, csl], Rprev[:, b * 256:(b + 1) * 256]) for b in range(2)]
                                rds += [B_rqX[tl], B_Rprev]
                            mm_group(p, PS[5][:, 0:256], pairs, rds, [PSB[5]])
                            p.act(lambda h: h.activation(out=osb[ci][:], in_=PS[5][:, 0:256], func=AF.Copy),
                                  writes=[PSB[5], B_osb[ci]])
                            p.dve(lambda h: h.scalar_tensor_tensor(out=sq[:], in0=osb[ci][:], scalar=1.0, in1=osb[ci][:],
                                                                   op0=ALU.mult, op1=ALU.mult, accum_out=st3[ci][:, 0:1]),
                                  reads=[B_osb[ci]], writes=[B_sq, B_st3[ci]])
                            p.dve(lambda h: h.tensor_scalar(out=st3[ci][:, 1:2], in0=st3[ci][:, 0:1], scalar1=1.0 / 256, scalar2=EPS,
                                                            op0=ALU.mult, op1=ALU.add), reads=[B_st3[ci]], writes=[B_st3[ci]])
                            p.pool(lambda h: h.tensor_tensor(out=st3[ci][:, 2:3], in0=st3[ci][:, 1:2], in1=negh, op=ALU.pow),
                                   reads=[B_st3[ci], B_cst], writes=[B_st3[ci]])
                            p.dve(lambda h: h.scalar_tensor_tensor(out=onb[ci][:], in0=osb[ci][:], scalar=st3[ci][:, 2:3],
                                                                   in1=sg[tl][:, cc, :], op0=ALU.mult, op1=ALU.mult),
                                  reads=[B_osb[ci], B_st3[ci], B_sg[tl]], writes=[B_onb[ci]])

                        def Cst(ci=ci, cc=cc, last=(cc == nchk - 1)):
                            def tfn2(h):
                                ins = None
                                for b in range(2):
                                    ins = h.transpose(pv7[:, b * 512 + cc * 128:b * 512 + (cc + 1) * 128], onb[ci][:, b * 128:(b + 1) * 128], ident)
                                return ins
                            p.pe(tfn2, reads=[B_onb[ci], B_cstb], writes=[PSB[7]])
                            if last:
                                p.act(lambda h: h.activation(out=orT[os_][:], in_=pv7.rearrange("p (b t) -> p b t", b=2), func=AF.Copy),
                                      writes=[PSB[7], B_orTt[os_]])
                                p.dma("sp", S_orT[os_], [(orrT_d[:, 2 * hd:2 * hd + 2, j * 512:(j + 1) * 512], orT[os_][:])],
                                      reads=[B_orTt[os_]], writes=[gbuf(B_orrT, (hd, j), "orrT")])
                        st["A"] = A if own else None
                        st["D"] = Dst
                        st["B"] = Bst if own else None
                        st["C"] = Cst if own else None
                        out.append(st)
                    return out

                seqL = [(hd, ti) for hd in range(4) for ti in range(len(tiles))]
                load_W3(0)
                for u in proj_units(seqL[0][0], seqL[0][1], 0):
                    u()
                pendC = None
                for i, (hd, ti) in enumerate(seqL):
                    if i < 24:
                        pre_cast_one()
                    R_ = rec_steps(hd, ti, i)
                    P_ = proj_units(seqL[i + 1][0], seqL[i + 1][1], i + 1) if i + 1 < len(seqL) else []
                    pi = 0
                    nslots = 2 * len(R_)
                    slot = 0

                    for k, stp in enumerate(R_):
                        if stp["A"] is not None:
                            stp["A"]()
                        stp["D"]()
                        for half in range(2):
                            cnt = -(-(len(P_) - pi) // (nslots - slot))
                            slot += 1
                            for _ in range(cnt):
                                P_[pi]()
                                pi += 1
                            if half == 0:
                                if stp["B"] is not None:
                                    stp["B"]()
                            else:
                                if pendC is not None:
                                    pendC()
                                    pendC = None
                                pendC = stp["C"]
                    while pi < len(P_):
                        P_[pi]()
                        pi += 1
                if pendC is not None:
                    pendC()
                    pendC = None
                p.emit_stage()

        if upto >= 4:
            with contextlib.ExitStack() as s4:
                Wa = [sb(s4, "s4wa%d" % i, [128, 8, 512], BF16) for i in range(2)]
                Wr = [sb(s4, "s4wr%d" % i, [128, 8, 512], BF16) for i in range(2)]
                Wga = [sb(s4, "s4wga%d" % i, [128, 16, 512], BF16) for i in range(2)]
                Wgr = [sb(s4, "s4wgr%d" % i, [128, 16, 512], BF16) for i in range(2)]
                B_W = [p.buf("s4w%d" % i) for i in range(2)]
                S_W = [p.dsem("s4w%d" % i) for i in range(2)]
                xt = [sb(s4, "s4x%d" % i, [128, 16, 512], BF16) for i in range(2)]
                at = [sb(s4, "s4a%d" % i, [128, 8, 512], BF16) for i in range(2)]
                rt_ = [sb(s4, "s4r%d" % i, [128, 8, 512], BF16) for i in range(2)]
                B_in = [p.buf("s4in%d" % i) for i in range(2)]
                S_in = [p.dsem("s4in%d" % i) for i in range(2)]
                sa = [sb(s4, "s4sa%d" % i, [128, 512], BF16) for i in range(2)]
                sr = [sb(s4, "s4sr%d" % i, [128, 512], BF16) for i in range(2)]
                B_sa = [p.buf("s4sa%d" % i) for i in range(2)]
                B_sr = [p.buf("s4sr%d" % i) for i in range(2)]
                m1 = [sb(s4, "s4m1%d" % i, [128, 512], F32) for i in range(2)]
                m2 = [sb(s4, "s4m2%d" % i, [128, 512], F32) for i in range(2)]
                B_m1 = [p.buf("s4m1%d" % i) for i in range(2)]
                B_m2 = [p.buf("s4m2%d" % i) for i in range(2)]
                mo = [sb(s4, "s4mo%d" % i, [128, 4, 512], BF16) for i in range(2)]
                B_mo = [p.buf("s4mo%d" % i) for i in range(2)]
                S_mo = [p.dsem("s4mo%d" % i) for i in range(2)]

                def load_W4(g):
                    s = g % 2
                    pairs = [(Wa[s][:], w_pa_v[:, :, 512 * g:512 * (g + 1)]),
                             (Wr[s][:], w_pr_v[:, :, 512 * g:512 * (g + 1)]),
                             (Wga[s][:], w_in_v[:, :, 7168 + 512 * g:7168 + 512 * (g + 1)]),
                             (Wgr[s][:], w_in_v[:, :, 9216 + 512 * g:9216 + 512 * (g + 1)])]
                    p.dma("pool", S_W[s], pairs, writes=[B_W[s]])
                load_W4(0)
                it = 0
                fbc = 0

                def ld4(itx):
                    ls_ = itx % 2
                    j_ = itx % 4
                    tsl_ = slice(j_ * 512, (j_ + 1) * 512)
                    p.dma("sp", S_in[ls_], [(xt[ls_][:], xnT_d[:, :, NPRE + j_ * 512:NPRE + (j_ + 1) * 512]),
                                            (at[ls_][:], oaT_d[:, :, tsl_]), (rt_[ls_][:], orrT_d[:, :, tsl_])],
                          reads=[B_xnT[NPCH + 4 * j_ + i] for i in range(4)] + [B_oaT[(hd, j_)] for hd in range(4)] + [B_orrT[(hd, j_)] for hd in range(4)],
                          writes=[B_in[ls_]])
                ld4(0)
                for g in range(4):
                    ws = g % 2
                    if g + 1 < 4:
                        load_W4(g + 1)
                    for j in range(4):
                        ls = it % 2
                        if it + 1 < 16:
                            ld4(it + 1)
                        it += 1
                        pre_cast_one()
                        tsl = slice(j * 512, (j + 1) * 512)
                        for fb in range(4):
                            bs = (fbc % 2) * 4
                            fs = fbc % 2
                            fbc += 1
                            fsl = slice(fb * 128, (fb + 1) * 128)
                            mm_group(p, PS[bs + 0][:], [(Wa[ws][:, kc, fsl], at[ls][:, kc, :]) for kc in range(8)], [B_W[ws], B_in[ls]], [PSB[bs + 0]])
                            mm_group(p, PS[bs + 1][:], [(Wga[ws][:, kc, fsl], xt[ls][:, kc, :]) for kc in range(16)], [B_W[ws], B_in[ls]], [PSB[bs + 1]])
                            mm_group(p, PS[bs + 2][:], [(Wr[ws][:, kc, fsl], rt_[ls][:, kc, :]) for kc in range(8)], [B_W[ws], B_in[ls]], [PSB[bs + 2]])
                            mm_group(p, PS[bs + 3][:], [(Wgr[ws][:, kc, fsl], xt[ls][:, kc, :]) for kc in range(16)], [B_W[ws], B_in[ls]], [PSB[bs + 3]])
                            p.act(lambda h, fs=fs, bs=bs: h.activation(out=sa[fs][:], in_=PS[bs + 1][:], func=AF.Sigmoid), writes=[PSB[bs + 1], B_sa[fs]])
                            p.act(lambda h, fs=fs, bs=bs: h.activation(out=sr[fs][:], in_=PS[bs + 3][:], func=AF.Sigmoid), writes=[PSB[bs + 3], B_sr[fs]])
                            p.dve(lambda h, fs=fs, bs=bs: h.tensor_tensor(out=m1[fs][:], in0=PS[bs + 0][:], in1=sa[fs][:], op=ALU.mult),
                                  reads=[B_sa[fs]], writes=[PSB[bs + 0], B_m1[fs]])
                            p.dve(lambda h, fs=fs, bs=bs: h.tensor_tensor(out=m2[fs][:], in0=PS[bs + 2][:], in1=sr[fs][:], op=ALU.mult),
                                  reads=[B_sr[fs]], writes=[PSB[bs + 2], B_m2[fs]])
                            p.pool(lambda h, fs=fs, ls=ls, fb=fb: h.tensor_tensor(out=mo[ls][:, fb, :], in0=m1[fs][:], in1=m2[fs][:], op=ALU.add),
                                   reads=[B_m1[fs], B_m2[fs]], writes=[B_mo[ls]])
                        p.dma("sp", S_mo[ls], [(mT_d[:, 4 * g:4 * g + 4, tsl], mo[ls][:])], reads=[B_mo[ls]],
                              writes=[gbuf(B_mT, (g, j), "mT")])
                p.emit_stage()

        if upto >= 5:
            with contextlib.ExitStack() as s5:
                Wo = sb(s5, "s5wo", [128, 16, D], BF16)
                B_Wo = p.buf("s5wo")
                S_Wo = p.dsem("s5wo")
                g2 = sb(s5, "s5g2", [128, D], F32)
                B_g2 = p.buf("s5g2")
                S_g2 = p.dsem("s5g2")
                NB5 = 3
                mt = [sb(s5, "s5m%d" % i, [128, 16, 128], BF16) for i in range(NB5)]
                xr = [sb(s5, "s5x%d" % i, [128, D], F32) for i in range(NB5)]
                B_in = [p.buf("s5in%d" % i) for i in range(NB5)]
                S_in = [p.dsem("s5in%d" % i) for i in range(NB5)]
                h1 = [sb(s5, "s5h%d" % i, [128, D], F32) for i in range(NB5)]
                B_h1t = [p.buf("s5h%d" % i) for i in range(NB5)]
                S_h1 = [p.dsem("s5h%d" % i) for i in range(NB5)]
                xn = [sb(s5, "s5n%d" % i, [128, D], BF16) for i in range(NB5)]
                B_xn = [p.buf("s5n%d" % i) for i in range(NB5)]
                st4 = [sb(s5, "s5s%d" % i, [128, 4], F32) for i in range(NB5)]
                B_st4 = [p.buf("s5s%d" % i) for i in range(NB5)]
                hT = [sb(s5, "s5T%d" % i, [128, 16, 128], BF16) for i in range(NB5)]
                B_hT = [p.buf("s5T%d" % i) for i in range(NB5)]
                S_hT = [p.dsem("s5T%d" % i) for i in range(NB5)]
                p.dma("pool", S_Wo, [(Wo[:, :, 512 * i:512 * (i + 1)], w_o_v[:, :, 512 * i:512 * (i + 1)]) for i in range(4)], writes=[B_Wo])
                p.dma("sp", S_g2, [(g2[:], gb_d[:, 1, :])], writes=[B_g2])
                def ld5(c):
                    s = c % NB5
                    p.dma("sp", S_in[s], [(mt[s][:], mT_d[:, :, c * 128:(c + 1) * 128]),
                                          (xr[s][:], xin[NPRE + c * 128:NPRE + (c + 1) * 128, :])],
                          reads=[B_mT[(g, c // 4)] for g in range(4)], writes=[B_in[s]])
                for c in range(NB5 - 1):
                    ld5(c)
                pend5 = None
                for c in range(16):
                    s = c % NB5
                    j = c // 4
                    if c + NB5 - 1 < 16:
                        ld5(c + NB5 - 1)
                    for nb in range(4):
                        bk = 4 + (c * 4 + nb) % 4
                        nsl = slice(nb * 512, (nb + 1) * 512)
                        mm_group(p, PS[bk][:], [(mt[s][:, kc, :], Wo[:, kc, nsl]) for kc in range(16)], [B_Wo, B_in[s]], [PSB[bk]])
                        p.dve(lambda h, s=s, bk=bk, nsl=nsl: h.tensor_tensor(out=h1[s][:, nsl], in0=PS[bk][:], in1=xr[s][:, nsl], op=ALU.add),
                              reads=[B_in[s]], writes=[PSB[bk], B_h1t[s]])
                    p.dma("sp", S_h1[s], [(h1_d[c * 128:(c + 1) * 128, :], h1[s][:])], reads=[B_h1t[s]], writes=[B_h1[c]])
                    part2 = norm_transpose(h1[s][:], B_h1t[s], g2[:], B_g2, xn[s], B_xn[s], st4[s], B_st4[s], hT[s], B_hT[s],
                                           (c % 2) * 2, (c % 2) * 2 + 1, defer=True)
                    if pend5 is not None:
                        pend5()

                    def pend5(part2=part2, s=s, c=c):
                        part2()
                        p.dma("sp", S_hT[s], [(hnT_d[:, :, c * 128:(c + 1) * 128], hT[s][:])], reads=[B_hT[s]], writes=[B_hnT[c]])
                pend5()
                p.emit_stage()

        if upto >= 6:
            with contextlib.ExitStack() as s6:
                NWB = 4
                wb = [sb(s6, "s6w%d" % i, [128, 16, 512], BF16) for i in range(NWB)]
                B_wb = [p.buf("s6w%d" % i) for i in range(NWB)]
                S_wb = [p.dsem("s6w%d" % i) for i in range(NWB)]
                gf = sb(s6, "s6gf", [128, D], F32)
                B_gf = p.buf("s6gf")
                S_gf = p.dsem("s6gf")
                ht = sb(s6, "s6ht", [128, 16, 512], BF16)
                B_ht = p.buf("s6ht")
                S_ht = p.dsem("s6ht")
                uT = sb(s6, "s6uT", [128, 64, 512], BF16)
                B_uT = [p.buf("s6uT%d" % i) for i in range(16)]
                rl = [sb(s6, "s6rl%d" % i, [128, 512], F32) for i in range(2)]
                B_rl = [p.buf("s6rl%d" % i) for i in range(2)]
                hh = sb(s6, "s6hh", [128, 4, D], F32)
                B_hh = [p.buf("s6hh%d" % i) for i in range(4)]
                S_hh = p.dsem("s6hh")
                st6 = sb(s6, "s6st", [128, 4, 4], F32)
                B_st6 = [p.buf("s6st%d" % i) for i in range(4)]
                S_yo = [p.dsem("s6yo%d" % i) for i in range(4)]
                p.dma("sp", S_gf, [(gf[:], gb_d[:, 2, :])], writes=[B_gf])
                wi = 0
                ri = 0
                yi = 0
                for j in range(4):
                    if j == 0:
                        p.dma("sp", S_ht, [(ht[:], hnT_d[:, :, j * 512:(j + 1) * 512])],
                              reads=[B_hnT[4 * j + i] for i in range(4)], writes=[B_ht])
                    p.dma("sp", S_hh, [(hh[:, m, :], h1_d[(4 * j + m) * 128:(4 * j + m + 1) * 128, :]) for m in range(4)],
                          reads=[B_h1[4 * j + m] for m in range(4)], writes=B_hh)
                    for ub in range(16):
                        s = wi % NWB
                        wi += 1
                        p.dma("pool", S_wb[s], [(wb[s][:], wupb_v[:, :, ub * 512:(ub + 1) * 512])], reads=[B_wpre], writes=[B_wb[s]])
                        for q in range(4):
                            bk = (ub * 4 + q) % 4
                            r_ = ri % 2
                            ri += 1
                            mm_group(p, PS[bk][:], [(wb[s][:, kc, q * 128:(q + 1) * 128], ht[:, kc, :]) for kc in range(16)],
                                     [B_wb[s], B_ht], [PSB[bk]])
                            p.act(lambda h, r_=r_, bk=bk: h.activation(out=rl[r_][:], in_=PS[bk][:], func=AF.Relu),
                                  writes=[PSB[bk], B_rl[r_]])
                            p.dve(lambda h, r_=r_, ub=ub, q=q: h.tensor_tensor(out=uT[:, ub * 4 + q, :], in0=rl[r_][:], in1=rl[r_][:], op=ALU.mult),
                                  reads=[B_rl[r_]], writes=[B_uT[ub]])
                    if j + 1 < 4:
                        p.dma("sp", S_ht, [(ht[:], hnT_d[:, :, (j + 1) * 512:(j + 2) * 512])],
                              reads=[B_hnT[4 * (j + 1) + i] for i in range(4)], writes=[B_ht])
                    for nb in range(4):
                        nsl = slice(nb * 512, (nb + 1) * 512)
                        for gq in range(4):
                            s = wi % NWB
                            wi += 1
                            p.dma("pool", S_wb[s], [(wb[s][:], wdnb_v[:, gq * 16:(gq + 1) * 16, nsl])], reads=[B_wpre], writes=[B_wb[s]])
                            for m in range(4):
                                mm_group(p, PS[4 + m][:], [(uT[:, gq * 16 + f, m * 128:(m + 1) * 128], wb[s][:, f, :]) for f in range(16)],
                                         [B_wb[s]] + B_uT[gq * 4:(gq + 1) * 4], [PSB[4 + m]], start=(gq == 0), stop=(gq == 3))
                        for m in range(4):
                            p.dve(lambda h, m=m, nsl=nsl: h.tensor_tensor(out=hh[:, m, nsl], in0=PS[4 + m][:], in1=hh[:, m, nsl], op=ALU.add),
                                  writes=[PSB[4 + m], B_hh[m]])
                    for m in range(4):
                        ys = yi % 2
                        yi += 1
                        p.act(lambda h, m=m: h.activation(out=uT[:, 0:4, :], in_=hh[:, m, :].rearrange("p (a b) -> p a b", a=4),
                                                          func=AF.Square, accum_out=st6[:, m, 0:1]),
                              reads=[B_hh[m]], writes=[B_uT[0], B_st6[m]])
                        p.dve(lambda h, m=m: h.tensor_scalar(out=st6[:, m, 1:2], in0=st6[:, m, 0:1], scalar1=1.0 / D, scalar2=EPS,
                                                             op0=ALU.mult, op1=ALU.add), reads=[B_st6[m]], writes=[B_st6[m]])
                        p.pool(lambda h, m=m: h.tensor_tensor(out=st6[:, m, 2:3], in0=st6[:, m, 1:2], in1=negh, op=ALU.pow),
                               reads=[B_st6[m], B_cst], writes=[B_st6[m]])
                        p.dve(lambda h, m=m: h.scalar_tensor_tensor(out=hh[:, m, :], in0=hh[:, m, :], scalar=st6[:, m, 2:3], in1=gf[:],
                                                                    op0=ALU.mult, op1=ALU.mult),
                              reads=[B_st6[m], B_gf], writes=[B_hh[m]])
                        c = 4 * j + m
                        p.dma("sp", S_yo[m], [(y[c * 128:(c + 1) * 128, :], hh[:, m, :])], reads=[B_hh[m]], writes=[B_y[c]])
                p.op("sp", lambda h: h.nop(), reads=B_y)
                p.emit_stage()
        else:
            allb = [b for b in p.bufs if (b.w is not None and b.w.is_dma)]
            p.op("sp", lambda h: h.nop(), reads=allb)
            p.emit_stage()
    return nc


def _consts():
    c = np.zeros((128, NCST), np.float32)
    c[:, C_ID:C_ID + 128] = np.eye(128, dtype=np.float32)
    pm = np.zeros((128, 128), np.float32)
    for i in range(128):
        pm[(i + 64) % 128, i] = 1.0
    c[:, C_PM:C_PM + 128] = pm
    kk = np.arange(128)[:, None]
    qq = np.arange(128)[None, :]
    c[:, C_MK:C_MK + 128] = (qq >= kk).astype(np.float32)
    lg = np.log(np.float32(1.0) - np.float32(2.0) ** (-5.0 - np.arange(4, dtype=np.float32))).astype(np.float32)
    for h in range(4):
        rel = (qq - kk).astype(np.float32)
        dm = np.where(rel >= 0, np.exp(lg[h] * np.maximum(rel, 0.0)), 0.0).astype(np.float32)
        c[:, C_DT + 128 * h:C_DT + 128 * (h + 1)] = dm * np.float32(256.0 ** -0.5)
        xi = np.exp(lg[h] * (np.arange(128, dtype=np.float32) + 1.0)).astype(np.float32)
        c[:, C_XI + 512 * h:C_XI + 512 * (h + 1)] = np.tile(xi, 4)[None, :]
        ze = np.exp(lg[h] * (127.0 - np.arange(128, dtype=np.float32))).astype(np.float32)
        c[:, C_ZE + h] = ze * np.float32(256.0 ** -0.5)
    c[:, C_NH] = -0.5
    return c


def _rope_tables(pos):
    inv_a = (np.float32(10000.0) ** (-np.arange(64, dtype=np.float32) / np.float32(64))).astype(np.float32)
    ang = pos.astype(np.float32)[:, None] * inv_a[None, :]
    ca = np.cos(ang).astype(np.float32).T
    sa = np.sin(ang).astype(np.float32).T
    da = np.zeros((128, 2, NTOK), np.float32)
    da[0:64, 0] = ca
    da[64:128, 0] = ca
    da[0:64, 1] = -sa
    da[64:128, 1] = sa
    inv_r = (np.float32(10000.0) ** (-np.arange(128, dtype=np.float32) / np.float32(128))).astype(np.float32)
    angr = pos.astype(np.float32)[:, None] * inv_r[None, :]
    rr = np.zeros((128, 2, NTOK), np.float32)
    rr[:, 0] = np.cos(angr).astype(np.float32).T
    rr[:, 1] = np.sin(angr).astype(np.float32).T
    return da, rr


def _prep(x, meta_tokens, norm1_g, w_in, lam_q1, lam_k1, lam_q2, lam_k2, da_subln_g,
          w_pa, w_pr, w_o, norm2_g, w_up, w_down, normf_g):
    f = np.float32
    base = _consts()
    gb = np.zeros((128, 3, D), f)
    gb[:, 0] = np.asarray(norm1_g, f).reshape(1, D)
    gb[:, 1] = np.asarray(norm2_g, f).reshape(1, D)
    gb[:, 2] = np.asarray(normf_g, f).reshape(1, D)
    shared = {
        "gb": gb,
        "w_in": np.ascontiguousarray(np.asarray(w_in, f).reshape(D, INW)),
        "w_pa": np.ascontiguousarray(np.asarray(w_pa, f).reshape(1024, D)),
        "w_pr": np.ascontiguousarray(np.asarray(w_pr, f).reshape(1024, D)),
        "w_o": np.ascontiguousarray(np.asarray(w_o, f).reshape(D, D)),
        "w_up": np.ascontiguousarray(np.asarray(w_up, f).reshape(D, DFF)),
        "w_down": np.ascontiguousarray(np.asarray(w_down, f).reshape(DFF, D)),
    }
    x = np.asarray(x, f)
    meta = np.asarray(meta_tokens, f)
    in_maps = []
    for c in range(8):
        b, half = c // 2, c % 2
        xin = np.zeros((NTOK, D), f)
        pos = np.zeros((NTOK,), f)
        valid = np.zeros((NTOK,), f)
        if half == 0:
            xin[NPRE - 16:NPRE] = meta
            pos[NPRE - 16:NPRE] = np.arange(16)
            valid[NPRE - 16:NPRE] = 1
            xin[NPRE:] = x[b, 0:NOWN]
            pos[NPRE:] = 16 + np.arange(NOWN)
            valid[NPRE:] = 1
        else:
            xin[112:128] = meta
            pos[112:128] = np.arange(16)
            valid[112:128] = 1
            xin[128:NPRE] = x[b, 0:NOWN]
            pos[128:NPRE] = 16 + np.arange(NOWN)
            valid[128:NPRE] = 1
            xin[NPRE:] = x[b, NOWN:2 * NOWN]
            pos[NPRE:] = 16 + NOWN + np.arange(NOWN)
            valid[NPRE:] = 1
        cst = base.copy()
        cst[:, C_VA:C_VA + NCH] = valid.reshape(NCH, 128).T
        cst[:, C_GS:C_GS + 256] = np.asarray(da_subln_g, f).reshape(1, 256)
        for i, v in enumerate((lam_q1, lam_k1, lam_q2, lam_k2)):
            cst[:, C_LV + 128 * i:C_LV + 128 * (i + 1)] = np.asarray(v, f).reshape(1, 128)
        da, rr = _rope_tables(pos)
        m = {"xin": xin, "ropeda": da, "roper": rr, "cst": cst}
        m.update(shared)
        in_maps.append(m)
    return in_maps


def kernel(**inputs):
    in_maps = _prep(**inputs)
    nc = build()
    res = run_bass_kernel_spmd(nc, in_maps, core_ids=list(range(8)))
    out = np.zeros((4, 2 * NOWN, D), np.float32)
    for c in range(8):
        b, half = c // 2, c % 2
        out[b, half * NOWN:(half + 1) * NOWN] = res.results[c]["y"]
    return out
```

```python
import contextlib
import math
import numpy as np
import concourse.bass as bass
import concourse.mybir as mybir
from concourse.bass_utils import run_bass_kernel_spmd

F32 = mybir.dt.float32
BF16 = mybir.dt.bfloat16
AF = mybir.ActivationFunctionType
ALU = mybir.AluOpType

D = 2048
NPRE = 2176
NOWN = 2048
NTOK = NPRE + NOWN
NCH = NTOK // 128
NPCH = NPRE // 128
DFF = 8192
INW = 11264
EPS = 1e-6
LAMBDA_INIT = 0.8 - 0.6 * math.exp(-0.0)
EPOCH = 24000

C_ID, C_PM, C_MK = 0, 128, 256
C_DT = 384
C_XI = C_DT + 512
C_ZE = C_XI + 2048
C_VA = C_ZE + 4
C_GS = C_VA + 33
C_LV = C_GS + 256
C_NH = C_LV + 512
NCST = C_NH + 1

ENGS = ("pe", "act", "dve", "pool", "sp")
BLK = {"pe": "tensor", "act": "scalar", "dve": "vector", "pool": "gpsimd", "sp": "sync"}


class Buf:
    __slots__ = ("name", "w", "r")

    def __init__(self, name):
        self.name = name
        self.w = None
        self.r = {}


class Op:
    __slots__ = ("eng", "fn", "idx", "deps", "signal", "sigcount", "is_dma", "sem", "val")


class DSem:
    def __init__(self, h):
        self.h = h
        self.count = 0
        self.last = None


class Prog:
    def __init__(self, nc, stack):
        self.nc = nc
        self.stack = stack
        self.streams = {e: [] for e in ENGS}
        self.sigtotal = {e: 0 for e in ENGS}
        self.engsems = {e: [] for e in ENGS}
        self.waited = {e: {} for e in ENGS}
        self.bufs = []
        self.nsem = 0

    def new_sem(self, name):
        self.nsem += 1
        return self.stack.enter_context(self.nc.semaphore(name))

    def dsem(self, name):
        return DSem(self.new_sem("d_" + name))

    def buf(self, name):
        b = Buf(name)
        self.bufs.append(b)
        return b

    def op(self, eng, fn, reads=(), writes=(), dsem=None, ndma=1):
        o = Op()
        o.eng = eng
        o.fn = fn
        o.idx = len(self.streams[eng])
        o.signal = False
        o.sigcount = None
        o.is_dma = dsem is not None
        o.sem = None
        o.val = None
        deps = []

        def add(t, raw):
            if t is None:
                return
            if (not t.is_dma) and t.eng == eng and not o.is_dma:
                if (not raw) or (o.idx - t.idx) > 2:
                    return
            deps.append(t)

        for b in reads:
            add(b.w, True)
        for b in writes:
            add(b.w, False)
            for t in b.r.values():
                add(t, False)
        if dsem is not None:
            add(dsem.last, False)
            dsem.count += 16 * ndma
            o.sem = dsem
            o.val = dsem.count
            dsem.last = o
        for b in reads:
            b.r[("dma", id(o)) if o.is_dma else eng] = o
        for b in writes:
            b.w = o
            b.r = {}
        uniq = []
        for t in deps:
            if not any(t is u for u in uniq):
                uniq.append(t)
                if not t.is_dma:
                    t.signal = True
        o.deps = uniq
        self.streams[eng].append(o)
        return o

    def pe(self, fn, reads=(), writes=()):
        return self.op("pe", fn, reads, writes)

    def act(self, fn, reads=(), writes=()):
        return self.op("act", fn, reads, writes)

    def dve(self, fn, reads=(), writes=()):
        return self.op("dve", fn, reads, writes)

    def pool(self, fn, reads=(), writes=()):
        return self.op("pool", fn, reads, writes)

    def dma(self, eng, dsem, pairs, reads=(), writes=(), **kw):
        h16 = dsem.h

        def fn(h):
            ins = None
            for (o_, i_) in pairs:
                ins = h.dma_start(out=o_, in_=i_, **kw).then_inc(h16, 16)
            return ins
        return self.op(eng, fn, reads, writes, dsem=dsem, ndma=len(pairs))

    def _engsem(self, e, epoch):
        lst = self.engsems[e]
        while len(lst) <= epoch:
            lst.append(self.new_sem("e_%s_%d" % (e, len(lst))))
        return lst[epoch]

    def _emit_stream(self, h, e, ops):
        waited = self.waited[e]
        for o in ops:
            for t in o.deps:
                if t.is_dma:
                    key = ("d", id(t.sem))
                    g = t.val
                    sem, val = t.sem.h, t.val
                else:
                    key = ("e", t.eng)
                    g = t.sigcount
                    ep = (g - 1) // EPOCH
                    sem, val = self._engsem(t.eng, ep), (g - 1) % EPOCH + 1
                if waited.get(key, 0) >= g:
                    continue
                waited[key] = g
                h.wait_ge(sem, val)
            ins = o.fn(h)
            if (not o.is_dma) and o.signal:
                g = o.sigcount
                ins.then_inc(self._engsem(e, (g - 1) // EPOCH), 1)

    def emit_stage(self):
        for e in ENGS:
            for o in self.streams[e]:
                if o.signal and not o.is_dma:
                    self.sigtotal[e] += 1
                    o.sigcount = self.sigtotal[e]
        for e in ENGS:
            if self.sigtotal[e] > 0:
                self._engsem(e, (self.sigtotal[e] - 1) // EPOCH)
        streams = self.streams
        with self.nc.Block() as block:
            for e in ENGS:
                ops = streams[e]
                if not ops:
                    continue

                def body(h, e=e, ops=ops):
                    self._emit_stream(h, e, ops)
                getattr(block, BLK[e])(body)
        self.streams = {e: [] for e in ENGS}
        for b in self.bufs:
            if b.w is not None and not b.w.is_dma:
                b.w = None
            b.r = {k: t for k, t in b.r.items() if t.is_dma}


def mm_group(p, out_ap, pairs, reads, writes, start=True, stop=True):
    def fn(h):
        ins = None
        n = len(pairs)
        for i, (l, r) in enumerate(pairs):
            ins = h.matmul(out_ap, l, r, start=(start and i == 0), stop=(stop and i == n - 1))
        return ins
    return p.pe(fn, reads, writes)


def build(debug=False, upto=99):
    nc = bass.Bass("TRN2", target_bir_lowering=False)

    def dram(name, shape, dt, kind):
        return nc.dram_tensor(name, shape, dt, kind=kind).ap()

    xin = dram("xin", [NTOK, D], F32, "ExternalInput")
    ropeda = dram("ropeda", [128, 2, NTOK], F32, "ExternalInput")
    roper = dram("roper", [128, 2, NTOK], F32, "ExternalInput")
    cst_d = dram("cst", [128, NCST], F32, "ExternalInput")
    gb_d = dram("gb", [128, 3, D], F32, "ExternalInput")
    w_in = dram("w_in", [D, INW], F32, "ExternalInput")
    w_pa = dram("w_pa", [1024, D], F32, "ExternalInput")
    w_pr = dram("w_pr", [1024, D], F32, "ExternalInput")
    w_o = dram("w_o", [D, D], F32, "ExternalInput")
    w_up = dram("w_up", [D, DFF], F32, "ExternalInput")
    w_down = dram("w_down", [DFF, D], F32, "ExternalInput")
    y = dram("y", [NOWN, D], F32, "ExternalOutput")
    sk = "ExternalOutput" if debug else "Internal"
    xnT_d = dram("xnT_d", [128, 16, NTOK], BF16, sk)
    oaT_d = dram("oaT_d", [128, 8, NOWN], BF16, sk)
    orrT_d = dram("orrT_d", [128, 8, NOWN], BF16, sk)
    mT_d = dram("mT_d", [128, 16, NOWN], BF16, sk)
    h1_d = dram("h1_d", [NOWN, D], F32, sk)
    hnT_d = dram("hnT_d", [128, 16, NOWN], BF16, sk)
    wupb_d = dram("wupb_d", [D, DFF], BF16, "Internal")
    wdnb_d = dram("wdnb_d", [DFF, D], BF16, "Internal")
    wupb_v = wupb_d.rearrange("(kc p) c -> p kc c", p=128)
    wdnb_v = wdnb_d.rearrange("(kc p) c -> p kc c", p=128)
    if debug:
        dbgKT = dram("dbgKT", [128, 2, NTOK], BF16, "ExternalOutput")
        dbgV = dram("dbgV", [128, NCH, 257], BF16, "ExternalOutput")
        dbgQT = dram("dbgQT", [128, 2, 512], BF16, "ExternalOutput")
        dbgO = dram("dbgO", [128, 4, 256], F32, "ExternalOutput")
        dbgS = dram("dbgS", [128, 4, 8], F32, "ExternalOutput")

    w_in_v = w_in.rearrange("(kc p) c -> p kc c", p=128)
    w_pa_v = w_pa.rearrange("(kc p) c -> p kc c", p=128)
    w_pr_v = w_pr.rearrange("(kc p) c -> p kc c", p=128)
    w_o_v = w_o.rearrange("(kc p) c -> p kc c", p=128)
    w_up_v = w_up.rearrange("(kc p) c -> p kc c", p=128)
    w_down_v = w_down.rearrange("(kc p) c -> p kc c", p=128)

    with contextlib.ExitStack() as gs:
        p = Prog(nc, gs)

        def sb(stack, name, shape, dt):
            return stack.enter_context(nc.sbuf_tensor("sb_" + name, shape, dt))

        PS = [gs.enter_context(nc.psum_tensor("ps%d" % i, [128, 512], F32)) for i in range(8)]
        PSB = [p.buf("ps%d" % i) for i in range(8)]
        cst = sb(gs, "cst", [128, NCST], F32)
        cstb = sb(gs, "cstb", [128, 384], BF16)
        neglam = sb(gs, "neglam", [128, 1], F32)
        gs8 = sb(gs, "gs8", [128, 256], F32)
        B_cst = p.buf("cst")
        B_cstb = p.buf("cstb")
        B_neglam = p.buf("neglam")
        B_gs8 = p.buf("gs8")
        ident = cstb[:, C_ID:C_ID + 128]
        permm = cstb[:, C_PM:C_PM + 128]
        maskT = cstb[:, C_MK:C_MK + 128]
        negh = cst[:, C_NH:C_NH + 1]

        S_c = p.dsem("cst")
        S_pre = p.dsem("pre")
        B_wpre = p.buf("wpre")
        S_cb = p.dsem("cstb")

        B_xnT = [p.buf("xnT%d" % i) for i in range(NCH)]
        B_oaT = {}
        B_orrT = {}
        B_mT = {}
        B_h1 = [p.buf("h1_%d" % i) for i in range(16)]
        B_hnT = [p.buf("hnT%d" % i) for i in range(16)]
        B_y = [p.buf("y%d" % i) for i in range(16)]

        def gbuf(dct, key, name):
            if key not in dct:
                dct[key] = p.buf("%s_%s" % (name, key))
            return dct[key]

        with contextlib.ExitStack() as s0:
            tmp = sb(gs, "l_tmp", [128, 128], F32)
            s12 = sb(gs, "l_s12", [128, 4], F32)
            B_tmp = p.buf("l_tmp")
            B_s12 = p.buf("l_s12")
            p.dma("sp", S_c, [(cst[:], cst_d[:, :])], writes=[B_cst])
            p.dma("pool", S_cb, [(cstb[:], cst_d[:, 0:384])], writes=[B_cstb])
            for i in range(2):
                p.dve(lambda h, i=i: h.scalar_tensor_tensor(
                    out=tmp[:], in0=cst[:, C_LV + 256 * i:C_LV + 256 * i + 128], scalar=1.0,
                    in1=cst[:, C_LV + 256 * i + 128:C_LV + 256 * i + 256],
                    op0=ALU.mult, op1=ALU.mult, accum_out=s12[:, i:i + 1]),
                    reads=[B_cst], writes=[B_tmp, B_s12])
            p.act(lambda h: h.activation(out=s12[:, 2:4], in_=s12[:, 0:2], func=AF.Exp),
                  reads=[B_s12], writes=[B_s12])
            p.dve(lambda h: h.tensor_tensor(out=neglam[:], in0=s12[:, 3:4], in1=s12[:, 2:3], op=ALU.subtract),
                  reads=[B_s12], writes=[B_neglam])
            p.dve(lambda h: h.tensor_scalar(out=neglam[:], in0=neglam[:], scalar1=-LAMBDA_INIT, scalar2=None, op0=ALU.add),
                  reads=[B_neglam], writes=[B_neglam])
            p.dve(lambda h: h.tensor_scalar(out=gs8[:], in0=cst[:, C_GS:C_GS + 256], scalar1=1.0 - LAMBDA_INIT, scalar2=None, op0=ALU.mult),
                  reads=[B_cst], writes=[B_gs8])
            if upto < 1:
                p.emit_stage()

        def norm_transpose(src, B_src, gbt, B_gbt, xn, B_xn, st4, B_st4, dstT, B_dstT, pa, pb, defer=False):
            p.act(lambda h: h.activation(out=xn[:], in_=src, func=AF.Square, accum_out=st4[:, 0:1]),
                  reads=[B_src], writes=[B_xn, B_st4])
            p.dve(lambda h: h.tensor_scalar(out=st4[:, 1:2], in0=st4[:, 0:1], scalar1=1.0 / D, scalar2=EPS,
                                            op0=ALU.mult, op1=ALU.add), reads=[B_st4], writes=[B_st4])
            p.pool(lambda h: h.tensor_tensor(out=st4[:, 2:3], in0=st4[:, 1:2], in1=negh, op=ALU.pow),
                   reads=[B_st4, B_cst], writes=[B_st4])
            p.dve(lambda h: h.scalar_tensor_tensor(out=xn[:], in0=src, scalar=st4[:, 2:3], in1=gbt,
                                                   op0=ALU.mult, op1=ALU.mult),
                  reads=[B_src, B_st4, B_gbt], writes=[B_xn])
            def part2():
              for half, pi in ((0, pa), (1, pb)):
                pv = PS[pi][:].bitcast(BF16)

                def fn(h, half=half, pv=pv):
                    ins = None
                    for k in range(8):
                        kc = half * 8 + k
                        ins = h.transpose(pv[:, k * 128:(k + 1) * 128], xn[:, kc * 128:(kc + 1) * 128], ident)
                    return ins
                p.pe(fn, reads=[B_xn, B_cstb], writes=[PSB[pi]])
                dst = dstT[:, half * 8:(half + 1) * 8, :]
                if half == 0:
                    p.act(lambda h, pv=pv, dst=dst: h.activation(out=dst, in_=pv.rearrange("p (k t) -> p k t", k=8), func=AF.Copy),
                          reads=[], writes=[PSB[pi], B_dstT])
                else:
                    p.dve(lambda h, pv=pv, dst=dst: h.tensor_copy(out=dst, in_=pv.rearrange("p (k t) -> p k t", k=8)),
                          reads=[], writes=[PSB[pi], B_dstT])
            if defer:
                return part2
            part2()
            return None

        if upto >= 1:
            with contextlib.ExitStack() as s1:
                gbt = sb(s1, "g1", [128, D], F32)
                B_gbt = p.buf("g1")
                S_g = p.dsem("g1")
                p.dma("sp", S_g, [(gbt[:], gb_d[:, 0, :])], writes=[B_gbt])
                NB1 = 4
                xt = [sb(s1, "s1x%d" % i, [128, D], F32) for i in range(NB1)]
                B_xt = [p.buf("s1x%d" % i) for i in range(NB1)]
                S_xt = [p.dsem("s1x%d" % i) for i in range(NB1)]
                xn = [sb(s1, "s1n%d" % i, [128, D], BF16) for i in range(NB1)]
                B_xn = [p.buf("s1n%d" % i) for i in range(NB1)]
                st4 = [sb(s1, "s1s%d" % i, [128, 4], F32) for i in range(NB1)]
                B_st4 = [p.buf("s1s%d" % i) for i in range(NB1)]
                xT = [sb(s1, "s1T%d" % i, [128, 16, 128], BF16) for i in range(NB1)]
                B_xT = [p.buf("s1T%d" % i) for i in range(NB1)]
                S_xT = [p.dsem("s1T%d" % i) for i in range(NB1)]
                for c in range(min(NB1 - 1, NCH)):
                    p.dma("sp", S_xt[c % NB1], [(xt[c % NB1][:], xin[c * 128:(c + 1) * 128, :])], writes=[B_xt[c % NB1]])
                for c in range(NCH):
                    s = c % NB1
                    cn = c + NB1 - 1
                    if cn < NCH:
                        p.dma("sp", S_xt[cn % NB1], [(xt[cn % NB1][:], xin[cn * 128:(cn + 1) * 128, :])], writes=[B_xt[cn % NB1]])
                    norm_transpose(xt[s][:], B_xt[s], gbt[:], B_gbt, xn[s], B_xn[s], st4[s], B_st4[s],
                                   xT[s], B_xT[s], (c % 4) * 2, (c % 4) * 2 + 1)
                    p.dma("sp", S_xT[s], [(xnT_d[:, :, c * 128:(c + 1) * 128], xT[s][:])],
                          reads=[B_xT[s]], writes=[B_xnT[c]])
                p.emit_stage()

        pre_pairs = [(wupb_v[:, a, :], w_up_v[:, a, :]) for a in range(16)] + \
                    [(wdnb_v[:, 4 * a:4 * a + 4, :], w_down_v[:, 4 * a:4 * a + 4, :]) for a in range(16)]

        def pre_cast_one():
            if pre_pairs:
                p.dma("pool", S_pre, [pre_pairs.pop(0)], writes=[B_wpre])

        tiles = [(0, 512), (512, 512), (1024, 512), (1536, 512), (2048, 128)] + \
                [(NPRE + 512 * j, 512) for j in range(4)]

        if upto >= 2:
            with contextlib.ExitStack() as s2:
                NT = len(tiles)
                W = [sb(s2, "s2w%d" % i, [128, 16, 768], BF16) for i in range(2)]
                B_W = [p.buf("s2w%d" % i) for i in range(2)]
                S_W = [p.dsem("s2w%d" % i) for i in range(2)]
                xt = [sb(s2, "s2x%d" % i, [128, 16, 512], BF16) for i in range(2)]
                B_xt = [p.buf("s2x%d" % i) for i in range(2)]
                S_xt = [p.dsem("s2x%d" % i) for i in range(2)]
                rt = [sb(s2, "s2r%d" % i, [128, 2, 512], F32) for i in range(2)]
                B_rt = [p.buf("s2r%d" % i) for i in range(2)]
                S_rt = [p.dsem("s2r%d" % i) for i in range(2)]
                KT = [sb(s2, "s2KT%d" % i, [128, 2, NTOK], BF16) for i in range(2)]
                B_KT = [[p.buf("s2KT%d_%d" % (i, t)) for t in range(NT)] for i in range(2)]
                V = [sb(s2, "s2V%d" % i, [128, NCH, 257], BF16) for i in range(2)]
                B_V = [[p.buf("s2V%d_%d" % (i, t)) for t in range(NT)] for i in range(2)]
                QT = [sb(s2, "s2QT%d" % i, [128, 2, 512], BF16) for i in range(2)]
                B_QT = [p.buf("s2QT%d" % i) for i in range(2)]
                ksb = [sb(s2, "s2ksb%d" % i, [128, 512], BF16) for i in range(2)]
                B_ksb = [p.buf("s2ksb%d" % i) for i in range(2)]
                t1 = [sb(s2, "s2t1%d" % i, [128, 512], F32) for i in range(2)]
                B_t1 = [p.buf("s2t1%d" % i) for i in range(2)]
                t2 = [sb(s2, "s2t2%d" % i, [128, 512], F32) for i in range(2)]
                B_t2 = [p.buf("s2t2%d" % i) for i in range(2)]
                pt = [sb(s2, "s2pt%d" % i, [128, 512], BF16) for i in range(3)]
                B_pt = [p.buf("s2pt%d" % i) for i in range(3)]
                o1 = sb(s2, "s2o1", [128, 4, 256], F32)
                B_o1 = [p.buf("s2o1_%d" % i) for i in range(4)]
                sq = sb(s2, "s2sq", [128, 256], F32)
                B_sq = p.buf("s2sq")
                stt_ = sb(s2, "s2st", [128, 4, 8], F32)
                B_st = [p.buf("s2st%d" % i) for i in range(4)]
                oab = [sb(s2, "s2oab%d" % i, [128, 256], BF16) for i in range(2)]
                B_oab = [p.buf("s2oab%d" % i) for i in range(2)]
                oaT = [sb(s2, "s2oaT%d" % i, [128, 2, 512], BF16) for i in range(2)]
                B_oaTt = [p.buf("s2oaT%d" % i) for i in range(2)]
                S_oaT = [p.dsem("s2oaT%d" % i) for i in range(2)]

                for i in range(2):
                    p.dve(lambda h, i=i: h.tensor_copy(out=V[i][:, :, 256], in_=cst[:, C_VA:C_VA + NCH]),
                          reads=[B_cst], writes=B_V[i])

                def load_W(hd):
                    s = hd % 2
                    pairs = [(W[s][:, :, 256 * i:256 * (i + 1)],
                              w_in_v[:, :, 1024 * i + 256 * hd:1024 * i + 256 * (hd + 1)]) for i in range(3)]
                    p.dma("pool", S_W[s], pairs, writes=[B_W[s]])

                rr = {"proj": 0, "ks": 0, "pt": 0, "oab": 0, "oaT": 0}
                scale = 1.0 / math.sqrt(128.0)

                def tile_of(ch):
                    if ch < 16:
                        return ch // 4
                    if ch == 16:
                        return 4
                    return 5 + (ch - NPCH) // 4

                def rope_steps(Wt, B_Wt, col0, ls, n, dst, B_dst):
                    pb = rr["proj"] % 2
                    rr["proj"] += 1
                    pb2 = 1 - pb
                    ki = rr["ks"] % 2
                    rr["ks"] += 1
                    steps = []
                    for g in range(8):
                        def st(g=g):
                            pairs = [(Wt[:, kc, col0:col0 + 128], xt[ls][:, kc, 0:n]) for kc in (2 * g, 2 * g + 1)]
                            mm_group(p, PS[pb][:, 0:n], pairs, [B_Wt, B_xt[ls]], [PSB[pb]], start=(g == 0), stop=(g == 7))
                            if g == 7:
                                p.act(lambda h: h.activation(out=ksb[ki][:, 0:n], in_=PS[pb][:, 0:n], func=AF.Copy),
                                      writes=[PSB[pb], B_ksb[ki]])
                                p.dve(lambda h: h.tensor_tensor(out=t1[ki][:, 0:n], in0=PS[pb][:, 0:n], in1=rt[ls][:, 0, 0:n], op=ALU.mult),
                                      reads=[B_rt[ls]], writes=[PSB[pb], B_t1[ki]])
                        steps.append(st)

                    def st_perm():
                        mm_group(p, PS[pb2][:, 0:n], [(permm, ksb[ki][:, 0:n])], [B_cstb, B_ksb[ki]], [PSB[pb2]])
                        p.dve(lambda h: h.tensor_tensor(out=t2[ki][:, 0:n], in0=PS[pb2][:, 0:n], in1=rt[ls][:, 1, 0:n], op=ALU.mult),
                              reads=[B_rt[ls]], writes=[PSB[pb2], B_t2[ki]])
                        p.pool(lambda h: h.tensor_tensor(out=dst, in0=t1[ki][:, 0:n], in1=t2[ki][:, 0:n], op=ALU.add),
                               reads=[B_t1[ki], B_t2[ki]], writes=[B_dst])
                    steps.append(st_perm)
                    return steps

                def proj_steps(hd, ti, lidx, qp):
                    s0_, n = tiles[ti]
                    own = s0_ >= NPRE
                    hp = hd % 2
                    ws = hd % 2
                    ls = lidx % 2
                    c0 = s0_ // 128
                    nchk = n // 128
                    steps = []

                    def st_load():
                        if ti == 0 and hd + 1 < 4:
                            load_W(hd + 1)
                        p.dma("sp", S_xt[ls], [(xt[ls][:, :, 0:n], xnT_d[:, :, s0_:s0_ + n])],
                              reads=[B_xnT[c0 + i] for i in range(nchk)], writes=[B_xt[ls]])
                        p.dma("sp", S_rt[ls], [(rt[ls][:, :, 0:n], ropeda[:, :, s0_:s0_ + n])], writes=[B_rt[ls]])
                    steps.append(st_load)
                    for c in range(2):
                        steps += rope_steps(W[ws], B_W[ws], 256 + 128 * c, ls, n, KT[hp][:, c, s0_:s0_ + n], B_KT[hp][ti])
                    for cc in range(nchk):
                        pb = rr["proj"] % 2
                        rr["proj"] += 1
                        for g in range(8):
                            def st(g=g, cc=cc, pb=pb):
                                pairs = [(xt[ls][:, kc, cc * 128:(cc + 1) * 128], W[ws][:, kc, 512:768]) for kc in (2 * g, 2 * g + 1)]
                                mm_group(p, PS[pb][:, 0:256], pairs, [B_W[ws], B_xt[ls]], [PSB[pb]], start=(g == 0), stop=(g == 7))
                                if g == 7:
                                    p.dve(lambda h: h.tensor_copy(out=V[hp][:, c0 + cc, 0:256], in_=PS[pb][:, 0:256]),
                                          writes=[PSB[pb], B_V[hp][ti]])
                            steps.append(st)
                    if own:
                        for c in range(2):
                            steps += rope_steps(W[ws], B_W[ws], 128 * c, ls, n, QT[qp][:, c, :], B_QT[qp])
                    return steps

                queue = []
                lidx = 0
                qcnt = 0
                qp_of = {}
                for hd in range(4):
                    for ti in range(NT):
                        own = tiles[ti][0] >= NPRE
                        qp = qcnt % 2
                        if own:
                            qp_of[(hd, ti)] = qp
                            qcnt += 1
                        for stp in proj_steps(hd, ti, lidx, qp):
                            queue.append((hd * NT + ti, stp))
                        lidx += 1
                qpos = {"i": 0}

                def ensure(tag):
                    while qpos["i"] < len(queue) and queue[qpos["i"]][0] <= tag:
                        queue[qpos["i"]][1]()
                        qpos["i"] += 1

                def pull(k, limit_tag):
                    for _ in range(k):
                        if qpos["i"] < len(queue) and queue[qpos["i"]][0] <= limit_tag:
                            queue[qpos["i"]][1]()
                            qpos["i"] += 1

                def attention(hd, ti):
                    s0_, n = tiles[ti]
                    hp = hd % 2
                    qp = qp_of[(hd, ti)]
                    j = (s0_ - NPRE) // 512
                    limit_tag = hd * NT + ti + 1 if ti + 1 < NT else (hd + 1) * NT + 5
                    budget = {"steps": sum(1 for k_ in range(qpos["i"], len(queue)) if queue[k_][0] <= limit_tag),
                              "iters": 0}
                    nfull = NPCH + 4 * j
                    seq = [(ch, None) for ch in range(nfull)] + [(nfull + k, k) for k in range(4)]
                    budget["iters"] = 2 * len(seq)
                    for c in range(2):
                        slots = {}

                        def QK(i, c=c):
                            ch, dg = seq[i]
                            sbk = 2 + (i % 2)
                            q0 = 0 if dg is None else dg * 128
                            mm_group(p, PS[sbk][:, q0:512], [(KT[hp][:, c, ch * 128:(ch + 1) * 128], QT[qp][:, c, q0:512])],
                                     [B_KT[hp][tile_of(ch)], B_QT[qp]], [PSB[sbk]])
                            ps_ = rr["pt"] % 3
                            rr["pt"] += 1
                            slots[i] = ps_
                            p.act(lambda h: h.activation(out=pt[ps_][:, q0:512], in_=PS[sbk][:, q0:512], func=AF.Exp, scale=scale),
                                  writes=[PSB[sbk], B_pt[ps_]])
                            if dg is not None:
                                p.pool(lambda h: h.tensor_tensor(out=pt[ps_][:, q0:q0 + 128], in0=pt[ps_][:, q0:q0 + 128], in1=maskT, op=ALU.mult),
                                       reads=[B_cstb], writes=[B_pt[ps_]])

                        def AV(i, c=c):
                            ch, dg = seq[i]
                            ps_ = slots[i]
                            qb0 = 0 if dg is None else dg
                            for qb in range(qb0, 4):
                                last = (dg is not None and dg == qb)
                                mm_group(p, PS[4 + qb][:, 0:257], [(pt[ps_][:, qb * 128:(qb + 1) * 128], V[hp][:, ch, :])],
                                         [B_pt[ps_], B_V[hp][tile_of(ch)]], [PSB[4 + qb]], start=(i == 0), stop=last)
                        QK(0)
                        for i in range(len(seq)):
                            if i + 1 < len(seq):
                                QK(i + 1)
                            take = -(-budget["steps"] // max(budget["iters"], 1))
                            if c == 1 and i == 0:
                                take += 8
                            take = min(take, budget["steps"], 12)
                            pull(take, limit_tag)
                            budget["steps"] -= take
                            budget["iters"] -= 1
                            AV(i)
                        for qb in range(4):
                            ob = PS[4 + qb]
                            if c == 0:
                                p.dve(lambda h, ob=ob, qb=qb: h.reciprocal(out=stt_[:, qb, 0:1], in_=ob[:, 256:257]),
                                      writes=[PSB[4 + qb], B_st[qb]])
                                p.dve(lambda h, ob=ob, qb=qb: h.tensor_scalar(out=o1[:, qb, :], in0=ob[:, 0:256], scalar1=stt_[:, qb, 0:1],
                                                                               scalar2=None, op0=ALU.mult),
                                      reads=[B_st[qb]], writes=[PSB[4 + qb], B_o1[qb]])
                            else:
                                p.dve(lambda h, ob=ob, qb=qb: h.reciprocal(out=stt_[:, qb, 1:2], in_=ob[:, 256:257]),
                                      writes=[PSB[4 + qb], B_st[qb]])
                                p.dve(lambda h, qb=qb: h.tensor_tensor(out=stt_[:, qb, 2:3], in0=stt_[:, qb, 1:2], in1=neglam[:], op=ALU.mult),
                                      reads=[B_neglam, B_st[qb]], writes=[B_st[qb]])
                                p.dve(lambda h, ob=ob, qb=qb: h.scalar_tensor_tensor(out=o1[:, qb, :], in0=ob[:, 0:256], scalar=stt_[:, qb, 2:3],
                                                                                      in1=o1[:, qb, :], op0=ALU.mult, op1=ALU.add),
                                      reads=[B_st[qb]], writes=[PSB[4 + qb], B_o1[qb]])
                    os_ = rr["oaT"] % 2
                    rr["oaT"] += 1
                    tb = 2
                    pv = PS[tb][:].bitcast(BF16)
                    for qb in range(4):
                        ab = rr["oab"] % 2
                        rr["oab"] += 1
                        p.dve(lambda h, qb=qb: h.scalar_tensor_tensor(out=sq[:], in0=o1[:, qb, :], scalar=1.0, in1=o1[:, qb, :],
                                                                      op0=ALU.mult, op1=ALU.mult, accum_out=stt_[:, qb, 3:4]),
                              reads=[B_o1[qb]], writes=[B_sq, B_st[qb]])
                        p.dve(lambda h, qb=qb: h.tensor_scalar(out=stt_[:, qb, 4:5], in0=stt_[:, qb, 3:4], scalar1=1.0 / 256, scalar2=EPS,
                                                               op0=ALU.mult, op1=ALU.add), reads=[B_st[qb]], writes=[B_st[qb]])
                        p.pool(lambda h, qb=qb: h.tensor_tensor(out=stt_[:, qb, 5:6], in0=stt_[:, qb, 4:5], in1=negh, op=ALU.pow),
                               reads=[B_st[qb], B_cst], writes=[B_st[qb]])
                        p.dve(lambda h, qb=qb, ab=ab: h.scalar_tensor_tensor(out=oab[ab][:], in0=o1[:, qb, :], scalar=stt_[:, qb, 5:6], in1=gs8[:],
                                                                             op0=ALU.mult, op1=ALU.mult),
                              reads=[B_o1[qb], B_st[qb], B_gs8], writes=[B_oab[ab]])
                        pull(2, limit_tag)

                        def tfn(h, qb=qb, ab=ab, pv=pv):
                            ins = None
                            for b in range(2):
                                ins = h.transpose(pv[:, b * 512 + qb * 128:b * 512 + (qb + 1) * 128], oab[ab][:, b * 128:(b + 1) * 128], ident)
                            return ins
                        p.pe(tfn, reads=[B_oab[ab], B_cstb], writes=[PSB[tb]])
                    p.act(lambda h, os_=os_, pv=pv: h.activation(out=oaT[os_][:], in_=pv.rearrange("p (b t) -> p b t", b=2), func=AF.Copy),
                          writes=[PSB[tb], B_oaTt[os_]])
                    p.dma("sp", S_oaT[os_], [(oaT_d[:, 2 * hd:2 * hd + 2, j * 512:(j + 1) * 512], oaT[os_][:])],
                          reads=[B_oaTt[os_]], writes=[gbuf(B_oaT, (hd, j), "oaT")])

                load_W(0)
                for hd in range(4):
                    for ti in range(NT):
                        if tiles[ti][0] < NPRE:
                            continue
                        ensure(hd * NT + ti)
                        attention(hd, ti)
                ensure(10 ** 9)
                if debug:
                    S_dbg = p.dsem("dbg")
                    B_dbg = p.buf("dbg")
                    p.dma("sp", S_dbg, [(dbgKT[:, :, :], KT[1][:]), (dbgV[:, :, :], V[1][:]), (dbgQT[:, :, :], QT[1][:]), (dbgO[:, :, :], o1[:]), (dbgS[:, :, :], stt_[:])],
                          reads=B_KT[1] + B_V[1] + [B_QT[1]] + B_o1 + B_st, writes=[B_dbg])
                p.emit_stage()

        if upto >= 3:
            with contextlib.ExitStack() as s3:
                W = [sb(s3, "s3w%d" % i, [128, 16, 1024], BF16) for i in range(2)]
                B_W = [p.buf("s3w%d" % i) for i in range(2)]
                S_W = [p.dsem("s3w%d" % i) for i in range(2)]
                xt = [sb(s3, "s3x%d" % i, [128, 16, 512], BF16) for i in range(2)]
                B_xt = [p.buf("s3x%d" % i) for i in range(2)]
                S_xt = [p.dsem("s3x%d" % i) for i in range(2)]
                rt = [sb(s3, "s3r%d" % i, [128, 2, 512], F32) for i in range(2)]
                B_rt = [p.buf("s3r%d" % i) for i in range(2)]
                S_rt = [p.dsem("s3r%d" % i) for i in range(2)]
                ta = [sb(s3, "s3ta%d" % i, [128, 512], F32) for i in range(2)]
                B_ta = [p.buf("s3ta%d" % i) for i in range(2)]
                tb_ = [sb(s3, "s3tb%d" % i, [128, 512], F32) for i in range(2)]
                B_tb = [p.buf("s3tb%d" % i) for i in range(2)]
                rkT = [sb(s3, "s3rkT%d" % i, [128, 2, 512], BF16) for i in range(2)]
                B_rkT = [p.buf("s3rkT%d" % i) for i in range(2)]
                rqT = [sb(s3, "s3rqT%d" % i, [128, 2, 512], BF16) for i in range(2)]
                B_rqT = [p.buf("s3rqT%d" % i) for i in range(2)]
                rqX = [sb(s3, "s3rqX%d" % i, [128, 2, 512], BF16) for i in range(2)]
                B_rqX = [p.buf("s3rqX%d" % i) for i in range(2)]
                rkt = [sb(s3, "s3rkt%d" % i, [128, 4, 256], BF16) for i in range(2)]
                B_rkt = [p.buf("s3rkt%d" % i) for i in range(2)]
                rv = [sb(s3, "s3rv%d" % i, [128, 4, 256], BF16) for i in range(2)]
                B_rv = [p.buf("s3rv%d" % i) for i in range(2)]
                rvz = [sb(s3, "s3rvz%d" % i, [128, 4, 256], BF16) for i in range(2)]
                B_rvz = [p.buf("s3rvz%d" % i) for i in range(2)]
                sg = [sb(s3, "s3sg%d" % i, [128, 4, 256], BF16) for i in range(2)]
                B_sg = [p.buf("s3sg%d" % i) for i in range(2)]
                R32 = sb(s3, "s3R32", [128, 512], F32)
                B_R32 = p.buf("s3R32")
                Rb = sb(s3, "s3Rb", [128, 512], BF16)
                B_Rb = p.buf("s3Rb")
                Rb2 = sb(s3, "s3Rb2", [128, 512], BF16)
                B_Rb2 = p.buf("s3Rb2")
                sm = [sb(s3, "s3sm%d" % i, [128, 128], BF16) for i in range(2)]
                B_sm = [p.buf("s3sm%d" % i) for i in range(2)]
                osb = [sb(s3, "s3osb%d" % i, [128, 256], F32) for i in range(2)]
                B_osb = [p.buf("s3osb%d" % i) for i in range(2)]
                sq = sb(s3, "s3sq", [128, 256], F32)
                B_sq = p.buf("s3sq")
                st3 = [sb(s3, "s3st%d" % i, [128, 4], F32) for i in range(2)]
                B_st3 = [p.buf("s3st%d" % i) for i in range(2)]
                onb = [sb(s3, "s3onb%d" % i, [128, 256], BF16) for i in range(2)]
                B_onb = [p.buf("s3onb%d" % i) for i in range(2)]
                orT = [sb(s3, "s3orT%d" % i, [128, 2, 512], BF16) for i in range(2)]
                B_orTt = [p.buf("s3orT%d" % i) for i in range(2)]
                S_orT = [p.dsem("s3orT%d" % i) for i in range(2)]
                gam = [1.0 - 2.0 ** (-5.0 - hh) for hh in range(4)]
                lg = [float(np.log(np.float32(g))) for g in gam]

                def load_W3(hd):
                    s = hd % 2
                    offs = [3072, 4096, 5120, 6144]
                    pairs = [(W[s][:, :, 256 * i:256 * (i + 1)],
                              w_in_v[:, :, offs[i] + 256 * hd:offs[i] + 256 * (hd + 1)]) for i in range(4)]
                    p.dma("pool", S_W[s], pairs, writes=[B_W[s]])

                rr = {"ld": 0, "tl": 0, "ch": 0, "orT": 0}

                def rope2(Wt, B_Wt, col0, ls, n, dstT, B_dst):
                    for b in range(2):
                        pairs = [(Wt[:, kc, col0 + 128 * b:col0 + 128 * (b + 1)], xt[ls][:, kc, 0:n]) for kc in range(16)]
                        mm_group(p, PS[b][:, 0:n], pairs, [B_Wt, B_xt[ls]], [PSB[b]])
                    cr = rt[ls][:, 0, 0:n]
                    sr = rt[ls][:, 1, 0:n]
                    p.dve(lambda h: h.tensor_tensor(out=ta[0][:, 0:n], in0=PS[0][:, 0:n], in1=cr, op=ALU.mult),
                          reads=[B_rt[ls]], writes=[PSB[0], B_ta[0]])
                    p.dve(lambda h: h.tensor_tensor(out=tb_[0][:, 0:n], in0=PS[1][:, 0:n], in1=sr, op=ALU.mult),
                          reads=[B_rt[ls]], writes=[PSB[1], B_tb[0]])
                    p.pool(lambda h: h.tensor_tensor(out=dstT[:, 0, 0:n], in0=ta[0][:, 0:n], in1=tb_[0][:, 0:n], op=ALU.subtract),
                           reads=[B_ta[0], B_tb[0]], writes=[B_dst])
                    p.dve(lambda h: h.tensor_tensor(out=ta[1][:, 0:n], in0=PS[1][:, 0:n], in1=cr, op=ALU.mult),
                          reads=[B_rt[ls]], writes=[PSB[1], B_ta[1]])
                    p.dve(lambda h: h.tensor_tensor(out=tb_[1][:, 0:n], in0=PS[0][:, 0:n], in1=sr, op=ALU.mult),
                          reads=[B_rt[ls]], writes=[PSB[0], B_tb[1]])
                    p.pool(lambda h: h.tensor_tensor(out=dstT[:, 1, 0:n], in0=ta[1][:, 0:n], in1=tb_[1][:, 0:n], op=ALU.add),
                           reads=[B_ta[1], B_tb[1]], writes=[B_dst])

                Rbs = [Rb, Rb2]
                B_Rbs = [B_Rb, B_Rb2]
                cds = [float(np.exp(np.float32(lg[hh_]) * np.float32(128.0))) for hh_ in range(4)]

                def proj_units(hd, ti, idx):
                    s0_, n = tiles[ti]
                    own = s0_ >= NPRE
                    ls = idx % 2
                    tl = idx % 2
                    ws = hd % 2
                    c0 = s0_ // 128
                    nchk = n // 128
                    U = []

                    def u_load():
                        if ti == 0 and hd + 1 < 4:
                            load_W3(hd + 1)
                        p.dma("sp", S_xt[ls], [(xt[ls][:, :, 0:n], xnT_d[:, :, s0_:s0_ + n])],
                              reads=[B_xnT[c0 + i] for i in range(nchk)], writes=[B_xt[ls]])
                        p.dma("sp", S_rt[ls], [(rt[ls][:, :, 0:n], roper[:, :, s0_:s0_ + n])], writes=[B_rt[ls]])
                    U.append(u_load)
                    U.append(lambda: rope2(W[ws], B_W[ws], 256, ls, n, rkT[tl], B_rkT[tl]))

                    def u_tr():
                        pv2 = PS[2][:].bitcast(BF16)

                        def tfn(h):
                            ins = None
                            for cc in range(nchk):
                                for b in range(2):
                                    ins = h.transpose(pv2[:, cc * 256 + b * 128:cc * 256 + (b + 1) * 128],
                                                      rkT[tl][:, b, cc * 128:(cc + 1) * 128], ident)
                            return ins
                        p.pe(tfn, reads=[B_rkT[tl], B_cstb], writes=[PSB[2]])
                        p.act(lambda h: h.activation(
                            out=rkt[tl][:, 0:nchk, :], in_=pv2[:, 0:nchk * 256].rearrange("p (c d) -> p c d", c=nchk), func=AF.Copy),
                            writes=[PSB[2], B_rkt[tl]])
                    U.append(u_tr)

                    def u_rv(cc):
                        pairs = [(xt[ls][:, kc, cc * 128:(cc + 1) * 128], W[ws][:, kc, 512:768]) for kc in range(16)]
                        mm_group(p, PS[3][:, 0:256], pairs, [B_W[ws], B_xt[ls]], [PSB[3]])
                        if own:
                            p.act(lambda h: h.activation(out=rv[tl][:, cc, :], in_=PS[3][:, 0:256], func=AF.Copy),
                                  writes=[PSB[3], B_rv[tl]])
                        p.act(lambda h: h.activation(out=rvz[tl][:, cc, :], in_=PS[3][:, 0:256], func=AF.Copy,
                                                     scale=cst[:, C_ZE + hd:C_ZE + hd + 1]),
                              reads=[B_cst], writes=[PSB[3], B_rvz[tl]])
                    for cc in range(nchk):
                        U.append(lambda cc=cc: u_rv(cc))
                    if own:
                        U.append(lambda: rope2(W[ws], B_W[ws], 0, ls, n, rqT[tl], B_rqT[tl]))

                        def u_rqx():
                            for b in range(2):
                                p.pool(lambda h, b=b: h.tensor_tensor(
                                    out=rqX[tl][:, b, :], in0=rqT[tl][:, b, :],
                                    in1=cst[:, C_XI + 512 * hd:C_XI + 512 * (hd + 1)], op=ALU.mult),
                                    reads=[B_rqT[tl], B_cst], writes=[B_rqX[tl]])
                        U.append(u_rqx)

                        def u_rg(cc):
                            pairs = [(xt[ls][:, kc, cc * 128:(cc + 1) * 128], W[ws][:, kc, 768:1024]) for kc in range(16)]
                            mm_group(p, PS[3][:, 0:256], pairs, [B_W[ws], B_xt[ls]], [PSB[3]])
                            p.act(lambda h: h.activation(out=sg[tl][:, cc, :], in_=PS[3][:, 0:256], func=AF.Silu),
                                  writes=[PSB[3], B_sg[tl]])
                        for cc in range(nchk):
                            U.append(lambda cc=cc: u_rg(cc))
                    return U

                gch = {"n": 0}

                def rec_steps(hd, ti, idx):
                    s0_, n = tiles[ti]
                    own = s0_ >= NPRE
                    tl = idx % 2
                    nchk = n // 128
                    cd = cds[hd]
                    j = (s0_ - NPRE) // 512 if own else None
                    os_ = None
                    if own:
                        os_ = rr["orT"] % 2
                        rr["orT"] += 1
                    pv7 = PS[7][:].bitcast(BF16)
                    out = []
                    for cc in range(nchk):
                        g = gch["n"]
                        gch["n"] += 1
                        ci = g % 2
                        first = (ti == 0 and cc == 0)
                        csl = slice(cc * 128, (cc + 1) * 128)
                        Rprev, B_Rprev = Rbs[(g + 1) % 2], B_Rbs[(g + 1) % 2]
                        Rnew, B_Rnew = Rbs[g % 2], B_Rbs[g % 2]
                        st = {}

                        def A(ci=ci, csl=csl):
                            mm_group(p, PS[4][:, 0:128], [(rkT[tl][:, b, csl], rqT[tl][:, b, csl]) for b in range(2)],
                                     [B_rkT[tl], B_rqT[tl]], [PSB[4]])
                            p.dve(lambda h: h.tensor_tensor(out=sm[ci][:], in0=PS[4][:, 0:128],
                                                            in1=cst[:, C_DT + 128 * hd:C_DT + 128 * (hd + 1)], op=ALU.mult),
                                  reads=[B_cst], writes=[PSB[4], B_sm[ci]])

                        def Dst(cc=cc, first=first, Rnew=Rnew, B_Rnew=B_Rnew):
                            if first:
                                p.pool(lambda h: h.memset(R32[:], 0.0), writes=[B_R32])
                            for b in range(2):
                                mm_group(p, PS[6][:, b * 256:(b + 1) * 256], [(rkt[tl][:, cc, b * 128:(b + 1) * 128], rvz[tl][:, cc, :])],
                                         [B_rkt[tl], B_rvz[tl]], [PSB[6]])
                            p.dve(lambda h: h.scalar_tensor_tensor(out=R32[:], in0=R32[:], scalar=cd, in1=PS[6][:],
                                                                   op0=ALU.mult, op1=ALU.add),
                                  writes=[PSB[6], B_R32])
                            p.pool(lambda h: h.tensor_copy(out=Rnew[:], in_=R32[:]), reads=[B_R32], writes=[B_Rnew])

                        def Bst(ci=ci, cc=cc, csl=csl, first=first, Rprev=Rprev, B_Rprev=B_Rprev):
                            pairs = [(sm[ci][:], rv[tl][:, cc, :])]
                            rds = [B_sm[ci], B_rv[tl]]
                            if not first:
                                pairs += [(rqX[tl][:, b, csl], Rprev[:, b * 256:(b + 1) * 256]) for b in range(2)]
                                rds += [B_rqX[tl], B_Rprev]
                            mm_group(p, PS[5][:, 0:256], pairs, rds, [PSB[5]])
                            p.act(lambda h: h.activation(out=osb[ci][:], in_=PS[5][:, 0:256], func=AF.Copy),
                                  writes=[PSB[5], B_osb[ci]])
                            p.dve(lambda h: h.scalar_tensor_tensor(out=sq[:], in0=osb[ci][:], scalar=1.0, in1=osb[ci][:],
                                                                   op0=ALU.mult, op1=ALU.mult, accum_out=st3[ci][:, 0:1]),
                                  reads=[B_osb[ci]], writes=[B_sq, B_st3[ci]])
                            p.dve(lambda h: h.tensor_scalar(out=st3[ci][:, 1:2], in0=st3[ci][:, 0:1], scalar1=1.0 / 256, scalar2=EPS,
                                                            op0=ALU.mult, op1=ALU.add), reads=[B_st3[ci]], writes=[B_st3[ci]])
                            p.pool(lambda h: h.tensor_tensor(out=st3[ci][:, 2:3], in0=st3[ci][:, 1:2], in1=negh, op=ALU.pow),
                                   reads=[B_st3[ci], B_cst], writes=[B_st3[ci]])
                            p.dve(lambda h: h.scalar_tensor_tensor(out=onb[ci][:], in0=osb[ci][:], scalar=st3[ci][:, 2:3],
                                                                   in1=sg[tl][:, cc, :], op0=ALU.mult, op1=ALU.mult),
                                  reads=[B_osb[ci], B_st3[ci], B_sg[tl]], writes=[B_onb[ci]])

                        def Cst(ci=ci, cc=cc, last=(cc == nchk - 1)):
                            def tfn2(h):
                                ins = None
                                for b in range(2):
                                    ins = h.transpose(pv7[:, b * 512 + cc * 128:b * 512 + (cc + 1) * 128], onb[ci][:, b * 128:(b + 1) * 128], ident)
                                return ins
                            p.pe(tfn2, reads=[B_onb[ci], B_cstb], writes=[PSB[7]])
                            if last:
                                p.act(lambda h: h.activation(out=orT[os_][:], in_=pv7.rearrange("p (b t) -> p b t", b=2), func=AF.Copy),
                                      writes=[PSB[7], B_orTt[os_]])
                                p.dma("sp", S_orT[os_], [(orrT_d[:, 2 * hd:2 * hd + 2, j * 512:(j + 1) * 512], orT[os_][:])],
                                      reads=[B_orTt[os_]], writes=[gbuf(B_orrT, (hd, j), "orrT")])
                        st["A"] = A if own else None
                        st["D"] = Dst
                        st["B"] = Bst if own else None
                        st["C"] = Cst if own else None
                        out.append(st)
                    return out

                seqL = [(hd, ti) for hd in range(4) for ti in range(len(tiles))]
                load_W3(0)
                for u in proj_units(seqL[0][0], seqL[0][1], 0):
                    u()
                pendC = None
                for i, (hd, ti) in enumerate(seqL):
                    if i < 20:
                        pre_cast_one()
                    R_ = rec_steps(hd, ti, i)
                    P_ = proj_units(seqL[i + 1][0], seqL[i + 1][1], i + 1) if i + 1 < len(seqL) else []
                    pi = 0
                    nslots = 2 * len(R_)
                    slot = 0

                    for k, stp in enumerate(R_):
                        if stp["A"] is not None:
                            stp["A"]()
                        stp["D"]()
                        for half in range(2):
                            cnt = -(-(len(P_) - pi) // (nslots - slot))
                            slot += 1
                            for _ in range(cnt):
                                P_[pi]()
                                pi += 1
                            if half == 0:
                                if stp["B"] is not None:
                                    stp["B"]()
                            else:
                                if pendC is not None:
                                    pendC()
                                    pendC = None
                                pendC = stp["C"]
                    while pi < len(P_):
                        P_[pi]()
                        pi += 1
                if pendC is not None:
                    pendC()
                    pendC = None
                p.emit_stage()

        if upto >= 4:
            with contextlib.ExitStack() as s4:
                Wa = [sb(s4, "s4wa%d" % i, [128, 8, 512], BF16) for i in range(2)]
                Wr = [sb(s4, "s4wr%d" % i, [128, 8, 512], BF16) for i in range(2)]
                Wga = [sb(s4, "s4wga%d" % i, [128, 16, 512], BF16) for i in range(2)]
                Wgr = [sb(s4, "s4wgr%d" % i, [128, 16, 512], BF16) for i in range(2)]
                B_W = [p.buf("s4w%d" % i) for i in range(2)]
                S_W = [p.dsem("s4w%d" % i) for i in range(2)]
                xt = [sb(s4, "s4x%d" % i, [128, 16, 512], BF16) for i in range(2)]
                at = [sb(s4, "s4a%d" % i, [128, 8, 512], BF16) for i in range(2)]
                rt_ = [sb(s4, "s4r%d" % i, [128, 8, 512], BF16) for i in range(2)]
                B_in = [p.buf("s4in%d" % i) for i in range(2)]
                S_in = [p.dsem("s4in%d" % i) for i in range(2)]
                sa = [sb(s4, "s4sa%d" % i, [128, 512], BF16) for i in range(2)]
                sr = [sb(s4, "s4sr%d" % i, [128, 512], BF16) for i in range(2)]
                B_sa = [p.buf("s4sa%d" % i) for i in range(2)]
                B_sr = [p.buf("s4sr%d" % i) for i in range(2)]
                m1 = [sb(s4, "s4m1%d" % i, [128, 512], F32) for i in range(2)]
                m2 = [sb(s4, "s4m2%d" % i, [128, 512], F32) for i in range(2)]
                B_m1 = [p.buf("s4m1%d" % i) for i in range(2)]
                B_m2 = [p.buf("s4m2%d" % i) for i in range(2)]
                mo = [sb(s4, "s4mo%d" % i, [128, 4, 512], BF16) for i in range(2)]
                B_mo = [p.buf("s4mo%d" % i) for i in range(2)]
                S_mo = [p.dsem("s4mo%d" % i) for i in range(2)]

                def load_W4(g):
                    s = g % 2
                    pairs = [(Wa[s][:], w_pa_v[:, :, 512 * g:512 * (g + 1)]),
                             (Wr[s][:], w_pr_v[:, :, 512 * g:512 * (g + 1)]),
                             (Wga[s][:], w_in_v[:, :, 7168 + 512 * g:7168 + 512 * (g + 1)]),
                             (Wgr[s][:], w_in_v[:, :, 9216 + 512 * g:9216 + 512 * (g + 1)])]
                    p.dma("pool", S_W[s], pairs, writes=[B_W[s]])
                load_W4(0)
                it = 0
                fbc = 0

                def ld4(itx):
                    ls_ = itx % 2
                    j_ = itx % 4
                    tsl_ = slice(j_ * 512, (j_ + 1) * 512)
                    p.dma("sp", S_in[ls_], [(xt[ls_][:], xnT_d[:, :, NPRE + j_ * 512:NPRE + (j_ + 1) * 512]),
                                            (at[ls_][:], oaT_d[:, :, tsl_]), (rt_[ls_][:], orrT_d[:, :, tsl_])],
                          reads=[B_xnT[NPCH + 4 * j_ + i] for i in range(4)] + [B_oaT[(hd, j_)] for hd in range(4)] + [B_orrT[(hd, j_)] for hd in range(4)],
                          writes=[B_in[ls_]])
                ld4(0)
                for g in range(4):
                    ws = g % 2
                    if g + 1 < 4:
                        load_W4(g + 1)
                    for j in range(4):
                        ls = it % 2
                        if it + 1 < 16:
                            ld4(it + 1)
                        it += 1
                        if it <= 8:
                            pre_cast_one()
                        tsl = slice(j * 512, (j + 1) * 512)
                        for fb in range(4):
                            bs = (fbc % 2) * 4
                            fs = fbc % 2
                            fbc += 1
                            fsl = slice(fb * 128, (fb + 1) * 128)
                            mm_group(p, PS[bs + 0][:], [(Wa[ws][:, kc, fsl], at[ls][:, kc, :]) for kc in range(8)], [B_W[ws], B_in[ls]], [PSB[bs + 0]])
                            mm_group(p, PS[bs + 1][:], [(Wga[ws][:, kc, fsl], xt[ls][:, kc, :]) for kc in range(16)], [B_W[ws], B_in[ls]], [PSB[bs + 1]])
                            mm_group(p, PS[bs + 2][:], [(Wr[ws][:, kc, fsl], rt_[ls][:, kc, :]) for kc in range(8)], [B_W[ws], B_in[ls]], [PSB[bs + 2]])
                            mm_group(p, PS[bs + 3][:], [(Wgr[ws][:, kc, fsl], xt[ls][:, kc, :]) for kc in range(16)], [B_W[ws], B_in[ls]], [PSB[bs + 3]])
                            p.act(lambda h, fs=fs, bs=bs: h.activation(out=sa[fs][:], in_=PS[bs + 1][:], func=AF.Sigmoid), writes=[PSB[bs + 1], B_sa[fs]])
                            p.act(lambda h, fs=fs, bs=bs: h.activation(out=sr[fs][:], in_=PS[bs + 3][:], func=AF.Sigmoid), writes=[PSB[bs + 3], B_sr[fs]])
                            p.dve(lambda h, fs=fs, bs=bs: h.tensor_tensor(out=m1[fs][:], in0=PS[bs + 0][:], in1=sa[fs][:], op=ALU.mult),
                                  reads=[B_sa[fs]], writes=[PSB[bs + 0], B_m1[fs]])
                            p.dve(lambda h, fs=fs, bs=bs: h.tensor_tensor(out=m2[fs][:], in0=PS[bs + 2][:], in1=sr[fs][:], op=ALU.mult),
                                  reads=[B_sr[fs]], writes=[PSB[bs + 2], B_m2[fs]])
                            p.pool(lambda h, fs=fs, ls=ls, fb=fb: h.tensor_tensor(out=mo[ls][:, fb, :], in0=m1[fs][:], in1=m2[fs][:], op=ALU.add),
                                   reads=[B_m1[fs], B_m2[fs]], writes=[B_mo[ls]])
                        p.dma("sp", S_mo[ls], [(mT_d[:, 4 * g:4 * g + 4, tsl], mo[ls][:])], reads=[B_mo[ls]],
                              writes=[gbuf(B_mT, (g, j), "mT")])
                p.emit_stage()

        if upto >= 5:
            with contextlib.ExitStack() as s5:
                Wo = sb(s5, "s5wo", [128, 16, D], BF16)
                B_Wo = p.buf("s5wo")
                S_Wo = p.dsem("s5wo")
                g2 = sb(s5, "s5g2", [128, D], F32)
                B_g2 = p.buf("s5g2")
                S_g2 = p.dsem("s5g2")
                NB5 = 3
                mt = [sb(s5, "s5m%d" % i, [128, 16, 128], BF16) for i in range(NB5)]
                xr = [sb(s5, "s5x%d" % i, [128, D], F32) for i in range(NB5)]
                B_in = [p.buf("s5in%d" % i) for i in range(NB5)]
                S_in = [p.dsem("s5in%d" % i) for i in range(NB5)]
                h1 = [sb(s5, "s5h%d" % i, [128, D], F32) for i in range(NB5)]
                B_h1t = [p.buf("s5h%d" % i) for i in range(NB5)]
                S_h1 = [p.dsem("s5h%d" % i) for i in range(NB5)]
                xn = [sb(s5, "s5n%d" % i, [128, D], BF16) for i in range(NB5)]
                B_xn = [p.buf("s5n%d" % i) for i in range(NB5)]
                st4 = [sb(s5, "s5s%d" % i, [128, 4], F32) for i in range(NB5)]
                B_st4 = [p.buf("s5s%d" % i) for i in range(NB5)]
                hT = [sb(s5, "s5T%d" % i, [128, 16, 128], BF16) for i in range(NB5)]
                B_hT = [p.buf("s5T%d" % i) for i in range(NB5)]
                S_hT = [p.dsem("s5T%d" % i) for i in range(NB5)]
                p.dma("pool", S_Wo, [(Wo[:, :, 512 * i:512 * (i + 1)], w_o_v[:, :, 512 * i:512 * (i + 1)]) for i in range(4)], writes=[B_Wo])
                p.dma("sp", S_g2, [(g2[:], gb_d[:, 1, :])], writes=[B_g2])
                def ld5(c):
                    s = c % NB5
                    p.dma("sp", S_in[s], [(mt[s][:], mT_d[:, :, c * 128:(c + 1) * 128]),
                                          (xr[s][:], xin[NPRE + c * 128:NPRE + (c + 1) * 128, :])],
                          reads=[B_mT[(g, c // 4)] for g in range(4)], writes=[B_in[s]])
                for c in range(NB5 - 1):
                    ld5(c)
                pend5 = None
                for c in range(16):
                    s = c % NB5
                    j = c // 4
                    if c + NB5 - 1 < 16:
                        ld5(c + NB5 - 1)
                    if c % 3 == 0:
                        pre_cast_one()
                    for nb in range(4):
                        bk = 4 + (c * 4 + nb) % 4
                        nsl = slice(nb * 512, (nb + 1) * 512)
                        mm_group(p, PS[bk][:], [(mt[s][:, kc, :], Wo[:, kc, nsl]) for kc in range(16)], [B_Wo, B_in[s]], [PSB[bk]])
                        p.dve(lambda h, s=s, bk=bk, nsl=nsl: h.tensor_tensor(out=h1[s][:, nsl], in0=PS[bk][:], in1=xr[s][:, nsl], op=ALU.add),
                              reads=[B_in[s]], writes=[PSB[bk], B_h1t[s]])
                    p.dma("sp", S_h1[s], [(h1_d[c * 128:(c + 1) * 128, :], h1[s][:])], reads=[B_h1t[s]], writes=[B_h1[c]])
                    part2 = norm_transpose(h1[s][:], B_h1t[s], g2[:], B_g2, xn[s], B_xn[s], st4[s], B_st4[s], hT[s], B_hT[s],
                                           (c % 2) * 2, (c % 2) * 2 + 1, defer=True)
                    if pend5 is not None:
                        pend5()

                    def pend5(part2=part2, s=s, c=c):
                        part2()
                        p.dma("sp", S_hT[s], [(hnT_d[:, :, c * 128:(c + 1) * 128], hT[s][:])], reads=[B_hT[s]], writes=[B_hnT[c]])
                pend5()
                p.emit_stage()

        if upto >= 6:
            with contextlib.ExitStack() as s6:
                NWB = 4
                wb = [sb(s6, "s6w%d" % i, [128, 16, 512], BF16) for i in range(NWB)]
                B_wb = [p.buf("s6w%d" % i) for i in range(NWB)]
                S_wb = [p.dsem("s6w%d" % i) for i in range(NWB)]
                gf = sb(s6, "s6gf", [128, D], F32)
                B_gf = p.buf("s6gf")
                S_gf = p.dsem("s6gf")
                ht = sb(s6, "s6ht", [128, 16, 512], BF16)
                B_ht = p.buf("s6ht")
                S_ht = p.dsem("s6ht")
                uT = sb(s6, "s6uT", [128, 64, 512], BF16)
                B_uT = [p.buf("s6uT%d" % i) for i in range(16)]
                rl = [sb(s6, "s6rl%d" % i, [128, 512], F32) for i in range(2)]
                B_rl = [p.buf("s6rl%d" % i) for i in range(2)]
                hh = sb(s6, "s6hh", [128, 4, D], F32)
                B_hh = [p.buf("s6hh%d" % i) for i in range(4)]
                S_hh = p.dsem("s6hh")
                st6 = sb(s6, "s6st", [128, 4, 4], F32)
                B_st6 = [p.buf("s6st%d" % i) for i in range(4)]
                S_yo = [p.dsem("s6yo%d" % i) for i in range(4)]
                p.dma("sp", S_gf, [(gf[:], gb_d[:, 2, :])], writes=[B_gf])
                wi = 0
                ri = 0
                yi = 0
                for j in range(4):
                    if j == 0:
                        p.dma("sp", S_ht, [(ht[:], hnT_d[:, :, j * 512:(j + 1) * 512])],
                              reads=[B_hnT[4 * j + i] for i in range(4)], writes=[B_ht])
                    p.dma("sp", S_hh, [(hh[:, m, :], h1_d[(4 * j + m) * 128:(4 * j + m + 1) * 128, :]) for m in range(4)],
                          reads=[B_h1[4 * j + m] for m in range(4)], writes=B_hh)
                    for ub in range(16):
                        s = wi % NWB
                        wi += 1
                        p.dma("pool", S_wb[s], [(wb[s][:], wupb_v[:, :, ub * 512:(ub + 1) * 512])], reads=[B_wpre], writes=[B_wb[s]])
                        for q in range(4):
                            bk = (ub * 4 + q) % 4
                            r_ = ri % 2
                            ri += 1
                            mm_group(p, PS[bk][:], [(wb[s][:, kc, q * 128:(q + 1) * 128], ht[:, kc, :]) for kc in range(16)],
                                     [B_wb[s], B_ht], [PSB[bk]])
                            p.act(lambda h, r_=r_, bk=bk: h.activation(out=rl[r_][:], in_=PS[bk][:], func=AF.Relu),
                                  writes=[PSB[bk], B_rl[r_]])
                            p.dve(lambda h, r_=r_, ub=ub, q=q: h.tensor_tensor(out=uT[:, ub * 4 + q, :], in0=rl[r_][:], in1=rl[r_][:], op=ALU.mult),
                                  reads=[B_rl[r_]], writes=[B_uT[ub]])
                    if j + 1 < 4:
                        p.dma("sp", S_ht, [(ht[:], hnT_d[:, :, (j + 1) * 512:(j + 2) * 512])],
                              reads=[B_hnT[4 * (j + 1) + i] for i in range(4)], writes=[B_ht])
                    for nb in range(4):
                        nsl = slice(nb * 512, (nb + 1) * 512)
                        for gq in range(4):
                            s = wi % NWB
                            wi += 1
                            p.dma("pool", S_wb[s], [(wb[s][:], wdnb_v[:, gq * 16:(gq + 1) * 16, nsl])], reads=[B_wpre], writes=[B_wb[s]])
                            for m in range(4):
                                mm_group(p, PS[4 + m][:], [(uT[:, gq * 16 + f, m * 128:(m + 1) * 128], wb[s][:, f, :]) for f in range(16)],
                                         [B_wb[s]] + B_uT[gq * 4:(gq + 1) * 4], [PSB[4 + m]], start=(gq == 0), stop=(gq == 3))
                        for m in range(4):
                            p.dve(lambda h, m=m, nsl=nsl: h.tensor_tensor(out=hh[:, m, nsl], in0=PS[4 + m][:], in1=hh[:, m, nsl], op=ALU.add),
                                  writes=[PSB[4 + m], B_hh[m]])
                    for m in range(4):
                        ys = yi % 2
                        yi += 1
                        p.act(lambda h, m=m: h.activation(out=uT[:, 0:4, :], in_=hh[:, m, :].rearrange("p (a b) -> p a b", a=4),
                                                          func=AF.Square, accum_out=st6[:, m, 0:1]),
                              reads=[B_hh[m]], writes=[B_uT[0], B_st6[m]])
                        p.dve(lambda h, m=m: h.tensor_scalar(out=st6[:, m, 1:2], in0=st6[:, m, 0:1], scalar1=1.0 / D, scalar2=EPS,
                                                             op0=ALU.mult, op1=ALU.add), reads=[B_st6[m]], writes=[B_st6[m]])
                        p.pool(lambda h, m=m: h.tensor_tensor(out=st6[:, m, 2:3], in0=st6[:, m, 1:2], in1=negh, op=ALU.pow),
                               reads=[B_st6[m], B_cst], writes=[B_st6[m]])
                        p.dve(lambda h, m=m: h.scalar_tensor_tensor(out=hh[:, m, :], in0=hh[:, m, :], scalar=st6[:, m, 2:3], in1=gf[:],
                                                                    op0=ALU.mult, op1=ALU.mult),
                              reads=[B_st6[m], B_gf], writes=[B_hh[m]])
                        c = 4 * j + m
                        p.dma("sp", S_yo[m], [(y[c * 128:(c + 1) * 128, :], hh[:, m, :])], reads=[B_hh[m]], writes=[B_y[c]])
                p.op("sp", lambda h: h.nop(), reads=B_y)
                p.emit_stage()
        else:
            allb = [b for b in p.bufs if (b.w is not None and b.w.is_dma)]
            p.op("sp", lambda h: h.nop(), reads=allb)
            p.emit_stage()
    return nc


def _consts():
    c = np.zeros((128, NCST), np.float32)
    c[:, C_ID:C_ID + 128] = np.eye(128, dtype=np.float32)
    pm = np.zeros((128, 128), np.float32)
    for i in range(128):
        pm[(i + 64) % 128, i] = 1.0
    c[:, C_PM:C_PM + 128] = pm
    kk = np.arange(128)[:, None]
    qq = np.arange(128)[None, :]
    c[:, C_MK:C_MK + 128] = (qq >= kk).astype(np.float32)
    lg = np.log(np.float32(1.0) - np.float32(2.0) ** (-5.0 - np.arange(4, dtype=np.float32))).astype(np.float32)
    for h in range(4):
        rel = (qq - kk).astype(np.float32)
        dm = np.where(rel >= 0, np.exp(lg[h] * np.maximum(rel, 0.0)), 0.0).astype(np.float32)
        c[:, C_DT + 128 * h:C_DT + 128 * (h + 1)] = dm * np.float32(256.0 ** -0.5)
        xi = np.exp(lg[h] * (np.arange(128, dtype=np.float32) + 1.0)).astype(np.float32)
        c[:, C_XI + 512 * h:C_XI + 512 * (h + 1)] = np.tile(xi, 4)[None, :]
        ze = np.exp(lg[h] * (127.0 - np.arange(128, dtype=np.float32))).astype(np.float32)
        c[:, C_ZE + h] = ze * np.float32(256.0 ** -0.5)
    c[:, C_NH] = -0.5
    return c


def _rope_tables(pos):
    inv_a = (np.float32(10000.0) ** (-np.arange(64, dtype=np.float32) / np.float32(64))).astype(np.float32)
    ang = pos.astype(np.float32)[:, None] * inv_a[None, :]
    ca = np.cos(ang).astype(np.float32).T
    sa = np.sin(ang).astype(np.float32).T
    da = np.zeros((128, 2, NTOK), np.float32)
    da[0:64, 0] = ca
    da[64:128, 0] = ca
    da[0:64, 1] = -sa
    da[64:128, 1] = sa
    inv_r = (np.float32(10000.0) ** (-np.arange(128, dtype=np.float32) / np.float32(128))).astype(np.float32)
    angr = pos.astype(np.float32)[:, None] * inv_r[None, :]
    rr = np.zeros((128, 2, NTOK), np.float32)
    rr[:, 0] = np.cos(angr).astype(np.float32).T
    rr[:, 1] = np.sin(angr).astype(np.float32).T
    return da, rr


def _prep(x, meta_tokens, norm1_g, w_in, lam_q1, lam_k1, lam_q2, lam_k2, da_subln_g,
          w_pa, w_pr, w_o, norm2_g, w_up, w_down, normf_g):
    f = np.float32
    base = _consts()
    gb = np.zeros((128, 3, D), f)
    gb[:, 0] = np.asarray(norm1_g, f).reshape(1, D)
    gb[:, 1] = np.asarray(norm2_g, f).reshape(1, D)
    gb[:, 2] = np.asarray(normf_g, f).reshape(1, D)
    shared = {
        "gb": gb,
        "w_in": np.ascontiguousarray(np.asarray(w_in, f).reshape(D, INW)),
        "w_pa": np.ascontiguousarray(np.asarray(w_pa, f).reshape(1024, D)),
        "w_pr": np.ascontiguousarray(np.asarray(w_pr, f).reshape(1024, D)),
        "w_o": np.ascontiguousarray(np.asarray(w_o, f).reshape(D, D)),
        "w_up": np.ascontiguousarray(np.asarray(w_up, f).reshape(D, DFF)),
        "w_down": np.ascontiguousarray(np.asarray(w_down, f).reshape(DFF, D)),
    }
    x = np.asarray(x, f)
    meta = np.asarray(meta_tokens, f)
    in_maps = []
    for c in range(8):
        b, half = c // 2, c % 2
        xin = np.zeros((NTOK, D), f)
        pos = np.zeros((NTOK,), f)
        valid = np.zeros((NTOK,), f)
        if half == 0:
            xin[NPRE - 16:NPRE] = meta
            pos[NPRE - 16:NPRE] = np.arange(16)
            valid[NPRE - 16:NPRE] = 1
            xin[NPRE:] = x[b, 0:NOWN]
            pos[NPRE:] = 16 + np.arange(NOWN)
            valid[NPRE:] = 1
        else:
            xin[112:128] = meta
            pos[112:128] = np.arange(16)
            valid[112:128] = 1
            xin[128:NPRE] = x[b, 0:NOWN]
            pos[128:NPRE] = 16 + np.arange(NOWN)
            valid[128:NPRE] = 1
            xin[NPRE:] = x[b, NOWN:2 * NOWN]
            pos[NPRE:] = 16 + NOWN + np.arange(NOWN)
            valid[NPRE:] = 1
        cst = base.copy()
        cst[:, C_VA:C_VA + NCH] = valid.reshape(NCH, 128).T
        cst[:, C_GS:C_GS + 256] = np.asarray(da_subln_g, f).reshape(1, 256)
        for i, v in enumerate((lam_q1, lam_k1, lam_q2, lam_k2)):
            cst[:, C_LV + 128 * i:C_LV + 128 * (i + 1)] = np.asarray(v, f).reshape(1, 128)
        da, rr = _rope_tables(pos)
        m = {"xin": xin, "ropeda": da, "roper": rr, "cst": cst}
        m.update(shared)
        in_maps.append(m)
    return in_maps


def kernel(**inputs):
    in_maps = _prep(**inputs)
    nc = build()
    res = run_bass_kernel_spmd(nc, in_maps, core_ids=list(range(8)))
    out = np.zeros((4, 2 * NOWN, D), np.float32)
    for c in range(8):
        b, half = c // 2, c % 2
        out[b, half * NOWN:(half + 1) * NOWN] = res.results[c]["y"]
    return out
```

```python
import contextlib
import math
import numpy as np
import concourse.bass as bass
import concourse.mybir as mybir
from concourse.bass_utils import run_bass_kernel_spmd

F32 = mybir.dt.float32
BF16 = mybir.dt.bfloat16
AF = mybir.ActivationFunctionType
ALU = mybir.AluOpType

D = 2048
NPRE = 2176
NOWN = 2048
NTOK = NPRE + NOWN
NCH = NTOK // 128
NPCH = NPRE // 128
DFF = 8192
INW = 11264
EPS = 1e-6
LAMBDA_INIT = 0.8 - 0.6 * math.exp(-0.0)
EPOCH = 24000

C_ID, C_PM, C_MK = 0, 128, 256
C_DT = 384
C_XI = C_DT + 512
C_ZE = C_XI + 2048
C_VA = C_ZE + 4
C_GS = C_VA + 33
C_LV = C_GS + 256
C_NH = C_LV + 512
NCST = C_NH + 1

ENGS = ("pe", "act", "dve", "pool", "sp")
BLK = {"pe": "tensor", "act": "scalar", "dve": "vector", "pool": "gpsimd", "sp": "sync"}


class Buf:
    __slots__ = ("name", "w", "r")

    def __init__(self, name):
        self.name = name
        self.w = None
        self.r = {}


class Op:
    __slots__ = ("eng", "fn", "idx", "deps", "signal", "sigcount", "is_dma", "sem", "val")


class DSem:
    def __init__(self, h):
        self.h = h
        self.count = 0
        self.last = None


class Prog:
    def __init__(self, nc, stack):
        self.nc = nc
        self.stack = stack
        self.streams = {e: [] for e in ENGS}
        self.sigtotal = {e: 0 for e in ENGS}
        self.engsems = {e: [] for e in ENGS}
        self.waited = {e: {} for e in ENGS}
        self.bufs = []
        self.nsem = 0

    def new_sem(self, name):
        self.nsem += 1
        return self.stack.enter_context(self.nc.semaphore(name))

    def dsem(self, name):
        return DSem(self.new_sem("d_" + name))

    def buf(self, name):
        b = Buf(name)
        self.bufs.append(b)
        return b

    def op(self, eng, fn, reads=(), writes=(), dsem=None, ndma=1):
        o = Op()
        o.eng = eng
        o.fn = fn
        o.idx = len(self.streams[eng])
        o.signal = False
        o.sigcount = None
        o.is_dma = dsem is not None
        o.sem = None
        o.val = None
        deps = []

        def add(t, raw):
            if t is None:
                return
            if (not t.is_dma) and t.eng == eng and not o.is_dma:
                if eng == "pe":
                    return
            deps.append(t)

        for b in reads:
            add(b.w, True)
        for b in writes:
            add(b.w, False)
            for t in b.r.values():
                add(t, False)
        if dsem is not None:
            add(dsem.last, False)
            dsem.count += 16 * ndma
            o.sem = dsem
            o.val = dsem.count
            dsem.last = o
        for b in reads:
            b.r[("dma", id(o)) if o.is_dma else eng] = o
        for b in writes:
            b.w = o
            b.r = {}
        uniq = []
        for t in deps:
            if not any(t is u for u in uniq):
                uniq.append(t)
                if not t.is_dma:
                    t.signal = True
        o.deps = uniq
        self.streams[eng].append(o)
        return o

    def pe(self, fn, reads=(), writes=()):
        return self.op("pe", fn, reads, writes)

    def act(self, fn, reads=(), writes=()):
        return self.op("act", fn, reads, writes)

    def dve(self, fn, reads=(), writes=()):
        return self.op("dve", fn, reads, writes)

    def pool(self, fn, reads=(), writes=()):
        return self.op("pool", fn, reads, writes)

    def dma(self, eng, dsem, pairs, reads=(), writes=(), **kw):
        h16 = dsem.h

        def fn(h):
            ins = None
            for (o_, i_) in pairs:
                ins = h.dma_start(out=o_, in_=i_, **kw).then_inc(h16, 16)
            return ins
        return self.op(eng, fn, reads, writes, dsem=dsem, ndma=len(pairs))

    def _engsem(self, e, epoch):
        lst = self.engsems[e]
        while len(lst) <= epoch:
            lst.append(self.new_sem("e_%s_%d" % (e, len(lst))))
        return lst[epoch]

    def _emit_stream(self, h, e, ops):
        waited = self.waited[e]
        for o in ops:
            for t in o.deps:
                if t.is_dma:
                    key = ("d", id(t.sem))
                    g = t.val
                    sem, val = t.sem.h, t.val
                else:
                    key = ("e", t.eng)
                    g = t.sigcount
                    ep = (g - 1) // EPOCH
                    sem, val = self._engsem(t.eng, ep), (g - 1) % EPOCH + 1
                if waited.get(key, 0) >= g:
                    continue
                waited[key] = g
                h.wait_ge(sem, val)
            ins = o.fn(h)
            if (not o.is_dma) and o.signal:
                g = o.sigcount
                ins.then_inc(self._engsem(e, (g - 1) // EPOCH), 1)

    def emit_stage(self):
        last = {}
        for e in ENGS:
            for o in self.streams[e]:
                if o.is_dma:
                    last[id(o.sem)] = o
        if last:
            fin = Op()
            fin.eng = "sp"
            fin.fn = lambda h: h.nop()
            fin.idx = len(self.streams["sp"])
            fin.signal = False
            fin.sigcount = None
            fin.is_dma = False
            fin.sem = None
            fin.val = None
            fin.deps = list(last.values())
            self.streams["sp"].append(fin)
        for e in ENGS:
            for o in self.streams[e]:
                if o.signal and not o.is_dma:
                    self.sigtotal[e] += 1
                    o.sigcount = self.sigtotal[e]
        for e in ENGS:
            if self.sigtotal[e] > 0:
                self._engsem(e, (self.sigtotal[e] - 1) // EPOCH)
        streams = self.streams
        with self.nc.Block() as block:
            for e in ENGS:
                ops = streams[e]
                if not ops:
                    continue

                def body(h, e=e, ops=ops):
                    self._emit_stream(h, e, ops)
                getattr(block, BLK[e])(body)
        self.streams = {e: [] for e in ENGS}
        for b in self.bufs:
            if b.w is not None and not b.w.is_dma:
                b.w = None
            b.r = {k: t for k, t in b.r.items() if t.is_dma}


def mm_group(p, out_ap, pairs, reads, writes, start=True, stop=True):
    def fn(h):
        ins = None
        n = len(pairs)
        for i, (l, r) in enumerate(pairs):
            ins = h.matmul(out_ap, l, r, start=(start and i == 0), stop=(stop and i == n - 1))
        return ins
    return p.pe(fn, reads, writes)


def build(debug=False, upto=99):
    nc = bass.Bass("TRN2", target_bir_lowering=False)

    def dram(name, shape, dt, kind):
        return nc.dram_tensor(name, shape, dt, kind=kind).ap()

    xin = dram("xin", [NTOK, D], F32, "ExternalInput")
    ropeda = dram("ropeda", [128, 2, NTOK], F32, "ExternalInput")
    roper = dram("roper", [128, 2, NTOK], F32, "ExternalInput")
    cst_d = dram("cst", [128, NCST], F32, "ExternalInput")
    gb_d = dram("gb", [128, 3, D], F32, "ExternalInput")
    w_in = dram("w_in", [D, INW], F32, "ExternalInput")
    w_pa = dram("w_pa", [1024, D], F32, "ExternalInput")
    w_pr = dram("w_pr", [1024, D], F32, "ExternalInput")
    w_o = dram("w_o", [D, D], F32, "ExternalInput")
    w_up = dram("w_up", [D, DFF], F32, "ExternalInput")
    w_down = dram("w_down", [DFF, D], F32, "ExternalInput")
    y = dram("y", [NOWN, D], F32, "ExternalOutput")
    sk = "ExternalOutput" if debug else "Internal"
    xnT_d = dram("xnT_d", [128, 16, NTOK], BF16, sk)
    oaT_d = dram("oaT_d", [128, 8, NOWN], BF16, sk)
    orrT_d = dram("orrT_d", [128, 8, NOWN], BF16, sk)
    mT_d = dram("mT_d", [128, 16, NOWN], BF16, sk)
    h1_d = dram("h1_d", [NOWN, D], F32, sk)
    hnT_d = dram("hnT_d", [128, 16, NOWN], BF16, sk)
    wupb_d = dram("wupb_d", [D, DFF], BF16, "Internal")
    wdnb_d = dram("wdnb_d", [DFF, D], BF16, "Internal")
    wupb_v = wupb_d.rearrange("(kc p) c -> p kc c", p=128)
    wdnb_v = wdnb_d.rearrange("(kc p) c -> p kc c", p=128)
    if debug:
        dbgKT = dram("dbgKT", [128, 2, NTOK], BF16, "ExternalOutput")
        dbgV = dram("dbgV", [128, NCH, 257], BF16, "ExternalOutput")
        dbgQT = dram("dbgQT", [128, 2, 512], BF16, "ExternalOutput")
        dbgO = dram("dbgO", [128, 4, 256], F32, "ExternalOutput")
        dbgS = dram("dbgS", [128, 4, 8], F32, "ExternalOutput")

    w_in_v = w_in.rearrange("(kc p) c -> p kc c", p=128)
    w_pa_v = w_pa.rearrange("(kc p) c -> p kc c", p=128)
    w_pr_v = w_pr.rearrange("(kc p) c -> p kc c", p=128)
    w_o_v = w_o.rearrange("(kc p) c -> p kc c", p=128)
    w_up_v = w_up.rearrange("(kc p) c -> p kc c", p=128)
    w_down_v = w_down.rearrange("(kc p) c -> p kc c", p=128)

    with contextlib.ExitStack() as gs:
        p = Prog(nc, gs)

        def sb(stack, name, shape, dt):
            return stack.enter_context(nc.sbuf_tensor("sb_" + name, shape, dt))

        PS = [gs.enter_context(nc.psum_tensor("ps%d" % i, [128, 512], F32)) for i in range(8)]
        PSB = [p.buf("ps%d" % i) for i in range(8)]
        cst = sb(gs, "cst", [128, NCST], F32)
        cstb = sb(gs, "cstb", [128, 384], BF16)
        neglam = sb(gs, "neglam", [128, 1], F32)
        gs8 = sb(gs, "gs8", [128, 256], F32)
        B_cst = p.buf("cst")
        B_cstb = p.buf("cstb")
        B_neglam = p.buf("neglam")
        B_gs8 = p.buf("gs8")
        ident = cstb[:, C_ID:C_ID + 128]
        permm = cstb[:, C_PM:C_PM + 128]
        maskT = cstb[:, C_MK:C_MK + 128]
        negh = cst[:, C_NH:C_NH + 1]

        S_c = p.dsem("cst")
        S_pre = p.dsem("pre")
        B_wpre = p.buf("wpre")
        S_cb = p.dsem("cstb")

        B_xnT = [p.buf("xnT%d" % i) for i in range(NCH)]
        B_oaT = {}
        B_orrT = {}
        B_mT = {}
        B_h1 = [p.buf("h1_%d" % i) for i in range(16)]
        B_hnT = [p.buf("hnT%d" % i) for i in range(16)]
        B_y = [p.buf("y%d" % i) for i in range(16)]

        def gbuf(dct, key, name):
            if key not in dct:
                dct[key] = p.buf("%s_%s" % (name, key))
            return dct[key]

        with contextlib.ExitStack() as s0:
            tmp = sb(gs, "l_tmp", [128, 128], F32)
            s12 = sb(gs, "l_s12", [128, 4], F32)
            B_tmp = p.buf("l_tmp")
            B_s12 = p.buf("l_s12")
            p.dma("sp", S_c, [(cst[:], cst_d[:, :])], writes=[B_cst])
            p.dma("pool", S_cb, [(cstb[:], cst_d[:, 0:384])], writes=[B_cstb])
            for i in range(2):
                p.dve(lambda h, i=i: h.scalar_tensor_tensor(
                    out=tmp[:], in0=cst[:, C_LV + 256 * i:C_LV + 256 * i + 128], scalar=1.0,
                    in1=cst[:, C_LV + 256 * i + 128:C_LV + 256 * i + 256],
                    op0=ALU.mult, op1=ALU.mult, accum_out=s12[:, i:i + 1]),
                    reads=[B_cst], writes=[B_tmp, B_s12])
            p.act(lambda h: h.activation(out=s12[:, 2:4], in_=s12[:, 0:2], func=AF.Exp),
                  reads=[B_s12], writes=[B_s12])
            p.dve(lambda h: h.tensor_tensor(out=neglam[:], in0=s12[:, 3:4], in1=s12[:, 2:3], op=ALU.subtract),
                  reads=[B_s12], writes=[B_neglam])
            p.dve(lambda h: h.tensor_scalar(out=neglam[:], in0=neglam[:], scalar1=-LAMBDA_INIT, scalar2=None, op0=ALU.add),
                  reads=[B_neglam], writes=[B_neglam])
            p.dve(lambda h: h.tensor_scalar(out=gs8[:], in0=cst[:, C_GS:C_GS + 256], scalar1=1.0 - LAMBDA_INIT, scalar2=None, op0=ALU.mult),
                  reads=[B_cst], writes=[B_gs8])
            if upto < 1:
                p.emit_stage()

        def norm_transpose(src, B_src, gbt, B_gbt, xn, B_xn, st4, B_st4, dstT, B_dstT, pa, pb, defer=False):
            p.act(lambda h: h.activation(out=xn[:], in_=src, func=AF.Square, accum_out=st4[:, 0:1]),
                  reads=[B_src], writes=[B_xn, B_st4])
            p.dve(lambda h: h.tensor_scalar(out=st4[:, 1:2], in0=st4[:, 0:1], scalar1=1.0 / D, scalar2=EPS,
                                            op0=ALU.mult, op1=ALU.add), reads=[B_st4], writes=[B_st4])
            p.pool(lambda h: h.tensor_tensor(out=st4[:, 2:3], in0=st4[:, 1:2], in1=negh, op=ALU.pow),
                   reads=[B_st4, B_cst], writes=[B_st4])
            p.dve(lambda h: h.scalar_tensor_tensor(out=xn[:], in0=src, scalar=st4[:, 2:3], in1=gbt,
                                                   op0=ALU.mult, op1=ALU.mult),
                  reads=[B_src, B_st4, B_gbt], writes=[B_xn])
            def part2():
              for half, pi in ((0, pa), (1, pb)):
                pv = PS[pi][:].bitcast(BF16)

                def fn(h, half=half, pv=pv):
                    ins = None
                    for k in range(8):
                        kc = half * 8 + k
                        ins = h.transpose(pv[:, k * 128:(k + 1) * 128], xn[:, kc * 128:(kc + 1) * 128], ident)
                    return ins
                p.pe(fn, reads=[B_xn, B_cstb], writes=[PSB[pi]])
                dst = dstT[:, half * 8:(half + 1) * 8, :]
                if half == 0:
                    p.act(lambda h, pv=pv, dst=dst: h.activation(out=dst, in_=pv.rearrange("p (k t) -> p k t", k=8), func=AF.Copy),
                          reads=[], writes=[PSB[pi], B_dstT])
                else:
                    p.dve(lambda h, pv=pv, dst=dst: h.tensor_copy(out=dst, in_=pv.rearrange("p (k t) -> p k t", k=8)),
                          reads=[], writes=[PSB[pi], B_dstT])
            if defer:
                return part2
            part2()
            return None

        if upto >= 1:
            with contextlib.ExitStack() as s1:
                gbt = sb(s1, "g1", [128, D], F32)
                B_gbt = p.buf("g1")
                S_g = p.dsem("g1")
                p.dma("sp", S_g, [(gbt[:], gb_d[:, 0, :])], writes=[B_gbt])
                NB1 = 4
                xt = [sb(s1, "s1x%d" % i, [128, D], F32) for i in range(NB1)]
                B_xt = [p.buf("s1x%d" % i) for i in range(NB1)]
                S_xt = [p.dsem("s1x%d" % i) for i in range(NB1)]
                xn = [sb(s1, "s1n%d" % i, [128, D], BF16) for i in range(NB1)]
                B_xn = [p.buf("s1n%d" % i) for i in range(NB1)]
                st4 = [sb(s1, "s1s%d" % i, [128, 4], F32) for i in range(NB1)]
                B_st4 = [p.buf("s1s%d" % i) for i in range(NB1)]
                xT = [sb(s1, "s1T%d" % i, [128, 16, 128], BF16) for i in range(NB1)]
                B_xT = [p.buf("s1T%d" % i) for i in range(NB1)]
                S_xT = [p.dsem("s1T%d" % i) for i in range(NB1)]
                for c in range(min(NB1 - 1, NCH)):
                    p.dma("sp", S_xt[c % NB1], [(xt[c % NB1][:], xin[c * 128:(c + 1) * 128, :])], writes=[B_xt[c % NB1]])
                for c in range(NCH):
                    s = c % NB1
                    cn = c + NB1 - 1
                    if cn < NCH:
                        p.dma("sp", S_xt[cn % NB1], [(xt[cn % NB1][:], xin[cn * 128:(cn + 1) * 128, :])], writes=[B_xt[cn % NB1]])
                    norm_transpose(xt[s][:], B_xt[s], gbt[:], B_gbt, xn[s], B_xn[s], st4[s], B_st4[s],
                                   xT[s], B_xT[s], (c % 4) * 2, (c % 4) * 2 + 1)
                    p.dma("sp", S_xT[s], [(xnT_d[:, :, c * 128:(c + 1) * 128], xT[s][:])],
                          reads=[B_xT[s]], writes=[B_xnT[c]])
                p.emit_stage()

        pre_pairs = [(wupb_v[:, a, :], w_up_v[:, a, :]) for a in range(16)] + \
                    [(wdnb_v[:, 4 * a:4 * a + 4, :], w_down_v[:, 4 * a:4 * a + 4, :]) for a in range(16)]

        def pre_cast_one():
            if pre_pairs:
                p.dma("pool", S_pre, [pre_pairs.pop(0)], writes=[B_wpre])

        tiles = [(0, 512), (512, 512), (1024, 512), (1536, 512), (2048, 128)] + \
                [(NPRE + 512 * j, 512) for j in range(4)]

        if upto >= 2:
            with contextlib.ExitStack() as s2:
                NT = len(tiles)
                W = [sb(s2, "s2w%d" % i, [128, 16, 768], BF16) for i in range(2)]
                B_W = [p.buf("s2w%d" % i) for i in range(2)]
                S_W = [p.dsem("s2w%d" % i) for i in range(2)]
                xt = [sb(s2, "s2x%d" % i, [128, 16, 512], BF16) for i in range(2)]
                B_xt = [p.buf("s2x%d" % i) for i in range(2)]
                S_xt = [p.dsem("s2x%d" % i) for i in range(2)]
                rt = [sb(s2, "s2r%d" % i, [128, 2, 512], F32) for i in range(2)]
                B_rt = [p.buf("s2r%d" % i) for i in range(2)]
                S_rt = [p.dsem("s2r%d" % i) for i in range(2)]
                KT = [sb(s2, "s2KT%d" % i, [128, 2, NTOK], BF16) for i in range(2)]
                B_KT = [[p.buf("s2KT%d_%d" % (i, t)) for t in range(NT)] for i in range(2)]
                V = [sb(s2, "s2V%d" % i, [128, NCH, 257], BF16) for i in range(2)]
                B_V = [[p.buf("s2V%d_%d" % (i, t)) for t in range(NT)] for i in range(2)]
                QT = [sb(s2, "s2QT%d" % i, [128, 2, 512], BF16) for i in range(2)]
                B_QT = [p.buf("s2QT%d" % i) for i in range(2)]
                ksb = [sb(s2, "s2ksb%d" % i, [128, 512], BF16) for i in range(2)]
                B_ksb = [p.buf("s2ksb%d" % i) for i in range(2)]
                t1 = [sb(s2, "s2t1%d" % i, [128, 512], F32) for i in range(2)]
                B_t1 = [p.buf("s2t1%d" % i) for i in range(2)]
                t2 = [sb(s2, "s2t2%d" % i, [128, 512], F32) for i in range(2)]
                B_t2 = [p.buf("s2t2%d" % i) for i in range(2)]
                pt = [sb(s2, "s2pt%d" % i, [128, 512], BF16) for i in range(3)]
                B_pt = [p.buf("s2pt%d" % i) for i in range(3)]
                o1 = sb(s2, "s2o1", [128, 4, 256], F32)
                B_o1 = [p.buf("s2o1_%d" % i) for i in range(4)]
                sq = sb(s2, "s2sq", [128, 256], F32)
                B_sq = p.buf("s2sq")
                stt_ = sb(s2, "s2st", [128, 4, 8], F32)
                B_st = [p.buf("s2st%d" % i) for i in range(4)]
                oab = [sb(s2, "s2oab%d" % i, [128, 256], BF16) for i in range(2)]
                B_oab = [p.buf("s2oab%d" % i) for i in range(2)]
                oaT = [sb(s2, "s2oaT%d" % i, [128, 2, 512], BF16) for i in range(2)]
                B_oaTt = [p.buf("s2oaT%d" % i) for i in range(2)]
                S_oaT = [p.dsem("s2oaT%d" % i) for i in range(2)]

                for i in range(2):
                    p.dve(lambda h, i=i: h.tensor_copy(out=V[i][:, :, 256], in_=cst[:, C_VA:C_VA + NCH]),
                          reads=[B_cst], writes=B_V[i])

                def load_W(hd):
                    s = hd % 2
                    pairs = [(W[s][:, :, 256 * i:256 * (i + 1)],
                              w_in_v[:, :, 1024 * i + 256 * hd:1024 * i + 256 * (hd + 1)]) for i in range(3)]
                    p.dma("pool", S_W[s], pairs, writes=[B_W[s]])

                rr = {"proj": 0, "ks": 0, "pt": 0, "oab": 0, "oaT": 0}
                scale = 1.0 / math.sqrt(128.0)

                def tile_of(ch):
                    if ch < 16:
                        return ch // 4
                    if ch == 16:
                        return 4
                    return 5 + (ch - NPCH) // 4

                def rope_steps(Wt, B_Wt, col0, ls, n, dst, B_dst):
                    pb = rr["proj"] % 2
                    rr["proj"] += 1
                    pb2 = 1 - pb
                    ki = rr["ks"] % 2
                    rr["ks"] += 1
                    steps = []
                    for g in range(8):
                        def st(g=g):
                            pairs = [(Wt[:, kc, col0:col0 + 128], xt[ls][:, kc, 0:n]) for kc in (2 * g, 2 * g + 1)]
                            mm_group(p, PS[pb][:, 0:n], pairs, [B_Wt, B_xt[ls]], [PSB[pb]], start=(g == 0), stop=(g == 7))
                            if g == 7:
                                p.act(lambda h: h.activation(out=ksb[ki][:, 0:n], in_=PS[pb][:, 0:n], func=AF.Copy),
                                      writes=[PSB[pb], B_ksb[ki]])
                                p.dve(lambda h: h.tensor_tensor(out=t1[ki][:, 0:n], in0=PS[pb][:, 0:n], in1=rt[ls][:, 0, 0:n], op=ALU.mult),
                                      reads=[B_rt[ls]], writes=[PSB[pb], B_t1[ki]])
                        steps.append(st)

                    def st_perm():
                        mm_group(p, PS[pb2][:, 0:n], [(permm, ksb[ki][:, 0:n])], [B_cstb, B_ksb[ki]], [PSB[pb2]])
                        p.dve(lambda h: h.tensor_tensor(out=t2[ki][:, 0:n], in0=PS[pb2][:, 0:n], in1=rt[ls][:, 1, 0:n], op=ALU.mult),
                              reads=[B_rt[ls]], writes=[PSB[pb2], B_t2[ki]])
                        p.pool(lambda h: h.tensor_tensor(out=dst, in0=t1[ki][:, 0:n], in1=t2[ki][:, 0:n], op=ALU.add),
                               reads=[B_t1[ki], B_t2[ki]], writes=[B_dst])
                    steps.append(st_perm)
                    return steps

                def proj_steps(hd, ti, lidx, qp):
                    s0_, n = tiles[ti]
                    own = s0_ >= NPRE
                    hp = hd % 2
                    ws = hd % 2
                    ls = lidx % 2
                    c0 = s0_ // 128
                    nchk = n // 128
                    steps = []

                    def st_load():
                        if ti == 0 and hd + 1 < 4:
                            load_W(hd + 1)
                        p.dma("sp", S_xt[ls], [(xt[ls][:, :, 0:n], xnT_d[:, :, s0_:s0_ + n])],
                              reads=[B_xnT[c0 + i] for i in range(nchk)], writes=[B_xt[ls]])
                        p.dma("sp", S_rt[ls], [(rt[ls][:, :, 0:n], ropeda[:, :, s0_:s0_ + n])], writes=[B_rt[ls]])
                    steps.append(st_load)
                    for c in range(2):
                        steps += rope_steps(W[ws], B_W[ws], 256 + 128 * c, ls, n, KT[hp][:, c, s0_:s0_ + n], B_KT[hp][ti])
                    for cc in range(nchk):
                        pb = rr["proj"] % 2
                        rr["proj"] += 1
                        for g in range(8):
                            def st(g=g, cc=cc, pb=pb):
                                pairs = [(xt[ls][:, kc, cc * 128:(cc + 1) * 128], W[ws][:, kc, 512:768]) for kc in (2 * g, 2 * g + 1)]
                                mm_group(p, PS[pb][:, 0:256], pairs, [B_W[ws], B_xt[ls]], [PSB[pb]], start=(g == 0), stop=(g == 7))
                                if g == 7:
                                    p.dve(lambda h: h.tensor_copy(out=V[hp][:, c0 + cc, 0:256], in_=PS[pb][:, 0:256]),
                                          writes=[PSB[pb], B_V[hp][ti]])
                            steps.append(st)
                    if own:
                        for c in range(2):
                            steps += rope_steps(W[ws], B_W[ws], 128 * c, ls, n, QT[qp][:, c, :], B_QT[qp])
                    return steps

                queue = []
                lidx = 0
                qcnt = 0
                qp_of = {}
                for hd in range(4):
                    for ti in range(NT):
                        own = tiles[ti][0] >= NPRE
                        qp = qcnt % 2
                        if own:
                            qp_of[(hd, ti)] = qp
                            qcnt += 1
                        for stp in proj_steps(hd, ti, lidx, qp):
                            queue.append((hd * NT + ti, stp))
                        lidx += 1
                qpos = {"i": 0}

                def ensure(tag):
                    while qpos["i"] < len(queue) and queue[qpos["i"]][0] <= tag:
                        queue[qpos["i"]][1]()
                        qpos["i"] += 1

                def pull(k, limit_tag):
                    for _ in range(k):
                        if qpos["i"] < len(queue) and queue[qpos["i"]][0] <= limit_tag:
                            queue[qpos["i"]][1]()
                            qpos["i"] += 1

                def attention(hd, ti):
                    s0_, n = tiles[ti]
                    hp = hd % 2
                    qp = qp_of[(hd, ti)]
                    j = (s0_ - NPRE) // 512
                    limit_tag = hd * NT + ti + 1 if ti + 1 < NT else (hd + 1) * NT + 5
                    budget = {"steps": sum(1 for k_ in range(qpos["i"], len(queue)) if queue[k_][0] <= limit_tag),
                              "iters": 0}
                    nfull = NPCH + 4 * j
                    seq = [(ch, None) for ch in range(nfull)] + [(nfull + k, k) for k in range(4)]
                    budget["iters"] = 2 * len(seq)
                    for c in range(2):
                        slots = {}

                        def QK(i, c=c):
                            ch, dg = seq[i]
                            sbk = 2 + (i % 2)
                            q0 = 0 if dg is None else dg * 128
                            mm_group(p, PS[sbk][:, q0:512], [(KT[hp][:, c, ch * 128:(ch + 1) * 128], QT[qp][:, c, q0:512])],
                                     [B_KT[hp][tile_of(ch)], B_QT[qp]], [PSB[sbk]])
                            ps_ = rr["pt"] % 3
                            rr["pt"] += 1
                            slots[i] = ps_
                            p.act(lambda h: h.activation(out=pt[ps_][:, q0:512], in_=PS[sbk][:, q0:512], func=AF.Exp, scale=scale),
                                  writes=[PSB[sbk], B_pt[ps_]])
                            if dg is not None:
                                p.pool(lambda h: h.tensor_tensor(out=pt[ps_][:, q0:q0 + 128], in0=pt[ps_][:, q0:q0 + 128], in1=maskT, op=ALU.mult),
                                       reads=[B_cstb], writes=[B_pt[ps_]])

                        def AV(i, c=c):
                            ch, dg = seq[i]
                            ps_ = slots[i]
                            qb0 = 0 if dg is None else dg
                            for qb in range(qb0, 4):
                                last = (dg is not None and dg == qb)
                                mm_group(p, PS[4 + qb][:, 0:257], [(pt[ps_][:, qb * 128:(qb + 1) * 128], V[hp][:, ch, :])],
                                         [B_pt[ps_], B_V[hp][tile_of(ch)]], [PSB[4 + qb]], start=(i == 0), stop=last)
                        QK(0)
                        for i in range(len(seq)):
                            if i + 1 < len(seq):
                                QK(i + 1)
                            take = -(-budget["steps"] // max(budget["iters"], 1))
                            if c == 1 and i == 0:
                                take += 8
                            take = min(take, budget["steps"], 12)
                            pull(take, limit_tag)
                            budget["steps"] -= take
                            budget["iters"] -= 1
                            AV(i)
                        for qb in range(4):
                            ob = PS[4 + qb]
                            if c == 0:
                                p.dve(lambda h, ob=ob, qb=qb: h.reciprocal(out=stt_[:, qb, 0:1], in_=ob[:, 256:257]),
                                      writes=[PSB[4 + qb], B_st[qb]])
                                p.dve(lambda h, ob=ob, qb=qb: h.tensor_scalar(out=o1[:, qb, :], in0=ob[:, 0:256], scalar1=stt_[:, qb, 0:1],
                                                                               scalar2=None, op0=ALU.mult),
                                      reads=[B_st[qb]], writes=[PSB[4 + qb], B_o1[qb]])
                            else:
                                p.dve(lambda h, ob=ob, qb=qb: h.reciprocal(out=stt_[:, qb, 1:2], in_=ob[:, 256:257]),
                                      writes=[PSB[4 + qb], B_st[qb]])
                                p.dve(lambda h, qb=qb: h.tensor_tensor(out=stt_[:, qb, 2:3], in0=stt_[:, qb, 1:2], in1=neglam[:], op=ALU.mult),
                                      reads=[B_neglam, B_st[qb]], writes=[B_st[qb]])
                                p.dve(lambda h, ob=ob, qb=qb: h.scalar_tensor_tensor(out=o1[:, qb, :], in0=ob[:, 0:256], scalar=stt_[:, qb, 2:3],
                                                                                      in1=o1[:, qb, :], op0=ALU.mult, op1=ALU.add),
                                      reads=[B_st[qb]], writes=[PSB[4 + qb], B_o1[qb]])
                    os_ = rr["oaT"] % 2
                    rr["oaT"] += 1
                    tb = 2
                    pv = PS[tb][:].bitcast(BF16)
                    for qb in range(4):
                        ab = rr["oab"] % 2
                        rr["oab"] += 1
                        p.dve(lambda h, qb=qb: h.scalar_tensor_tensor(out=sq[:], in0=o1[:, qb, :], scalar=1.0, in1=o1[:, qb, :],
                                                                      op0=ALU.mult, op1=ALU.mult, accum_out=stt_[:, qb, 3:4]),
                              reads=[B_o1[qb]], writes=[B_sq, B_st[qb]])
                        p.dve(lambda h, qb=qb: h.tensor_scalar(out=stt_[:, qb, 4:5], in0=stt_[:, qb, 3:4], scalar1=1.0 / 256, scalar2=EPS,
                                                               op0=ALU.mult, op1=ALU.add), reads=[B_st[qb]], writes=[B_st[qb]])
                        p.pool(lambda h, qb=qb: h.tensor_tensor(out=stt_[:, qb, 5:6], in0=stt_[:, qb, 4:5], in1=negh, op=ALU.pow),
                               reads=[B_st[qb], B_cst], writes=[B_st[qb]])
                        p.dve(lambda h, qb=qb, ab=ab: h.scalar_tensor_tensor(out=oab[ab][:], in0=o1[:, qb, :], scalar=stt_[:, qb, 5:6], in1=gs8[:],
                                                                             op0=ALU.mult, op1=ALU.mult),
                              reads=[B_o1[qb], B_st[qb], B_gs8], writes=[B_oab[ab]])
                        pull(2, limit_tag)

                        def tfn(h, qb=qb, ab=ab, pv=pv):
                            ins = None
                            for b in range(2):
                                ins = h.transpose(pv[:, b * 512 + qb * 128:b * 512 + (qb + 1) * 128], oab[ab][:, b * 128:(b + 1) * 128], ident)
                            return ins
                        p.pe(tfn, reads=[B_oab[ab], B_cstb], writes=[PSB[tb]])
                    p.act(lambda h, os_=os_, pv=pv: h.activation(out=oaT[os_][:], in_=pv.rearrange("p (b t) -> p b t", b=2), func=AF.Copy),
                          writes=[PSB[tb], B_oaTt[os_]])
                    p.dma("sp", S_oaT[os_], [(oaT_d[:, 2 * hd:2 * hd + 2, j * 512:(j + 1) * 512], oaT[os_][:])],
                          reads=[B_oaTt[os_]], writes=[gbuf(B_oaT, (hd, j), "oaT")])

                load_W(0)
                for hd in range(4):
                    for ti in range(NT):
                        if tiles[ti][0] < NPRE:
                            continue
                        ensure(hd * NT + ti)
                        attention(hd, ti)
                ensure(10 ** 9)
                if debug:
                    S_dbg = p.dsem("dbg")
                    B_dbg = p.buf("dbg")
                    p.dma("sp", S_dbg, [(dbgKT[:, :, :], KT[1][:]), (dbgV[:, :, :], V[1][:]), (dbgQT[:, :, :], QT[1][:]), (dbgO[:, :, :], o1[:]), (dbgS[:, :, :], stt_[:])],
                          reads=B_KT[1] + B_V[1] + [B_QT[1]] + B_o1 + B_st, writes=[B_dbg])
                p.emit_stage()

        if upto >= 3:
            with contextlib.ExitStack() as s3:
                W = [sb(s3, "s3w%d" % i, [128, 16, 1024], BF16) for i in range(2)]
                B_W = [p.buf("s3w%d" % i) for i in range(2)]
                S_W = [p.dsem("s3w%d" % i) for i in range(2)]
                xt = [sb(s3, "s3x%d" % i, [128, 16, 512], BF16) for i in range(2)]
                B_xt = [p.buf("s3x%d" % i) for i in range(2)]
                S_xt = [p.dsem("s3x%d" % i) for i in range(2)]
                rt = [sb(s3, "s3r%d" % i, [128, 2, 512], F32) for i in range(2)]
                B_rt = [p.buf("s3r%d" % i) for i in range(2)]
                S_rt = [p.dsem("s3r%d" % i) for i in range(2)]
                ta = [sb(s3, "s3ta%d" % i, [128, 512], F32) for i in range(2)]
                B_ta = [p.buf("s3ta%d" % i) for i in range(2)]
                tb_ = [sb(s3, "s3tb%d" % i, [128, 512], F32) for i in range(2)]
                B_tb = [p.buf("s3tb%d" % i) for i in range(2)]
                rkT = [sb(s3, "s3rkT%d" % i, [128, 2, 512], BF16) for i in range(2)]
                B_rkT = [p.buf("s3rkT%d" % i) for i in range(2)]
                rqT = [sb(s3, "s3rqT%d" % i, [128, 2, 512], BF16) for i in range(2)]
                B_rqT = [p.buf("s3rqT%d" % i) for i in range(2)]
                rqX = [sb(s3, "s3rqX%d" % i, [128, 2, 512], BF16) for i in range(2)]
                B_rqX = [p.buf("s3rqX%d" % i) for i in range(2)]
                rkt = [sb(s3, "s3rkt%d" % i, [128, 4, 256], BF16) for i in range(2)]
                B_rkt = [p.buf("s3rkt%d" % i) for i in range(2)]
                rv = [sb(s3, "s3rv%d" % i, [128, 4, 256], BF16) for i in range(2)]
                B_rv = [p.buf("s3rv%d" % i) for i in range(2)]
                rvz = [sb(s3, "s3rvz%d" % i, [128, 4, 256], BF16) for i in range(2)]
                B_rvz = [p.buf("s3rvz%d" % i) for i in range(2)]
                sg = [sb(s3, "s3sg%d" % i, [128, 4, 256], BF16) for i in range(2)]
                B_sg = [p.buf("s3sg%d" % i) for i in range(2)]
                R32 = sb(s3, "s3R32", [128, 512], F32)
                B_R32 = p.buf("s3R32")
                Rb = sb(s3, "s3Rb", [128, 512], BF16)
                B_Rb = p.buf("s3Rb")
                Rb2 = sb(s3, "s3Rb2", [128, 512], BF16)
                B_Rb2 = p.buf("s3Rb2")
                sm = [sb(s3, "s3sm%d" % i, [128, 128], BF16) for i in range(2)]
                B_sm = [p.buf("s3sm%d" % i) for i in range(2)]
                osb = [sb(s3, "s3osb%d" % i, [128, 256], F32) for i in range(2)]
                B_osb = [p.buf("s3osb%d" % i) for i in range(2)]
                sq = sb(s3, "s3sq", [128, 256], F32)
                B_sq = p.buf("s3sq")
                st3 = [sb(s3, "s3st%d" % i, [128, 4], F32) for i in range(2)]
                B_st3 = [p.buf("s3st%d" % i) for i in range(2)]
                onb = [sb(s3, "s3onb%d" % i, [128, 256], BF16) for i in range(2)]
                B_onb = [p.buf("s3onb%d" % i) for i in range(2)]
                orT = [sb(s3, "s3orT%d" % i, [128, 2, 512], BF16) for i in range(2)]
                B_orTt = [p.buf("s3orT%d" % i) for i in range(2)]
                S_orT = [p.dsem("s3orT%d" % i) for i in range(2)]
                gam = [1.0 - 2.0 ** (-5.0 - hh) for hh in range(4)]
                lg = [float(np.log(np.float32(g))) for g in gam]

                def load_W3(hd):
                    s = hd % 2
                    offs = [3072, 4096, 5120, 6144]
                    pairs = [(W[s][:, :, 256 * i:256 * (i + 1)],
                              w_in_v[:, :, offs[i] + 256 * hd:offs[i] + 256 * (hd + 1)]) for i in range(4)]
                    p.dma("pool", S_W[s], pairs, writes=[B_W[s]])

                rr = {"ld": 0, "tl": 0, "ch": 0, "orT": 0}

                def rope2(Wt, B_Wt, col0, ls, n, dstT, B_dst):
                    for b in range(2):
                        pairs = [(Wt[:, kc, col0 + 128 * b:col0 + 128 * (b + 1)], xt[ls][:, kc, 0:n]) for kc in range(16)]
                        mm_group(p, PS[b][:, 0:n], pairs, [B_Wt, B_xt[ls]], [PSB[b]])
                    cr = rt[ls][:, 0, 0:n]
                    sr = rt[ls][:, 1, 0:n]
                    p.dve(lambda h: h.tensor_tensor(out=ta[0][:, 0:n], in0=PS[0][:, 0:n], in1=cr, op=ALU.mult),
                          reads=[B_rt[ls]], writes=[PSB[0], B_ta[0]])
                    p.dve(lambda h: h.tensor_tensor(out=tb_[0][:, 0:n], in0=PS[1][:, 0:n], in1=sr, op=ALU.mult),
                          reads=[B_rt[ls]], writes=[PSB[1], B_tb[0]])
                    p.pool(lambda h: h.tensor_tensor(out=dstT[:, 0, 0:n], in0=ta[0][:, 0:n], in1=tb_[0][:, 0:n], op=ALU.subtract),
                           reads=[B_ta[0], B_tb[0]], writes=[B_dst])
                    p.dve(lambda h: h.tensor_tensor(out=ta[1][:, 0:n], in0=PS[1][:, 0:n], in1=cr, op=ALU.mult),
                          reads=[B_rt[ls]], writes=[PSB[1], B_ta[1]])
                    p.dve(lambda h: h.tensor_tensor(out=tb_[1][:, 0:n], in0=PS[0][:, 0:n], in1=sr, op=ALU.mult),
                          reads=[B_rt[ls]], writes=[PSB[0], B_tb[1]])
                    p.pool(lambda h: h.tensor_tensor(out=dstT[:, 1, 0:n], in0=ta[1][:, 0:n], in1=tb_[1][:, 0:n], op=ALU.add),
                           reads=[B_ta[1], B_tb[1]], writes=[B_dst])

                Rbs = [Rb, Rb2]
                B_Rbs = [B_Rb, B_Rb2]
                cds = [float(np.exp(np.float32(lg[hh_]) * np.float32(128.0))) for hh_ in range(4)]

                def proj_units(hd, ti, idx):
                    s0_, n = tiles[ti]
                    own = s0_ >= NPRE
                    ls = idx % 2
                    tl = idx % 2
                    ws = hd % 2
                    c0 = s0_ // 128
                    nchk = n // 128
                    U = []

                    def u_load():
                        if ti == 0 and hd + 1 < 4:
                            load_W3(hd + 1)
                        p.dma("sp", S_xt[ls], [(xt[ls][:, :, 0:n], xnT_d[:, :, s0_:s0_ + n])],
                              reads=[B_xnT[c0 + i] for i in range(nchk)], writes=[B_xt[ls]])
                        p.dma("sp", S_rt[ls], [(rt[ls][:, :, 0:n], roper[:, :, s0_:s0_ + n])], writes=[B_rt[ls]])
                    U.append(u_load)
                    U.append(lambda: rope2(W[ws], B_W[ws], 256, ls, n, rkT[tl], B_rkT[tl]))

                    def u_tr():
                        pv2 = PS[2][:].bitcast(BF16)

                        def tfn(h):
                            ins = None
                            for cc in range(nchk):
                                for b in range(2):
                                    ins = h.transpose(pv2[:, cc * 256 + b * 128:cc * 256 + (b + 1) * 128],
                                                      rkT[tl][:, b, cc * 128:(cc + 1) * 128], ident)
                            return ins
                        p.pe(tfn, reads=[B_rkT[tl], B_cstb], writes=[PSB[2]])
                        p.act(lambda h: h.activation(
                            out=rkt[tl][:, 0:nchk, :], in_=pv2[:, 0:nchk * 256].rearrange("p (c d) -> p c d", c=nchk), func=AF.Copy),
                            writes=[PSB[2], B_rkt[tl]])
                    U.append(u_tr)

                    def u_rv(cc):
                        pairs = [(xt[ls][:, kc, cc * 128:(cc + 1) * 128], W[ws][:, kc, 512:768]) for kc in range(16)]
                        mm_group(p, PS[3][:, 0:256], pairs, [B_W[ws], B_xt[ls]], [PSB[3]])
                        if own:
                            p.act(lambda h: h.activation(out=rv[tl][:, cc, :], in_=PS[3][:, 0:256], func=AF.Copy),
                                  writes=[PSB[3], B_rv[tl]])
                        p.act(lambda h: h.activation(out=rvz[tl][:, cc, :], in_=PS[3][:, 0:256], func=AF.Copy,
                                                     scale=cst[:, C_ZE + hd:C_ZE + hd + 1]),
                              reads=[B_cst], writes=[PSB[3], B_rvz[tl]])
                    for cc in range(nchk):
                        U.append(lambda cc=cc: u_rv(cc))
                    if own:
                        U.append(lambda: rope2(W[ws], B_W[ws], 0, ls, n, rqT[tl], B_rqT[tl]))

                        def u_rqx():
                            for b in range(2):
                                p.pool(lambda h, b=b: h.tensor_tensor(
                                    out=rqX[tl][:, b, :], in0=rqT[tl][:, b, :],
                                    in1=cst[:, C_XI + 512 * hd:C_XI + 512 * (hd + 1)], op=ALU.mult),
                                    reads=[B_rqT[tl], B_cst], writes=[B_rqX[tl]])
                        U.append(u_rqx)

                        def u_rg(cc):
                            pairs = [(xt[ls][:, kc, cc * 128:(cc + 1) * 128], W[ws][:, kc, 768:1024]) for kc in range(16)]
                            mm_group(p, PS[3][:, 0:256], pairs, [B_W[ws], B_xt[ls]], [PSB[3]])
                            p.act(lambda h: h.activation(out=sg[tl][:, cc, :], in_=PS[3][:, 0:256], func=AF.Silu),
                                  writes=[PSB[3], B_sg[tl]])
                        for cc in range(nchk):
                            U.append(lambda cc=cc: u_rg(cc))
                    return U

                gch = {"n": 0}

                def rec_steps(hd, ti, idx):
                    s0_, n = tiles[ti]
                    own = s0_ >= NPRE
                    tl = idx % 2
                    nchk = n // 128
                    cd = cds[hd]
                    j = (s0_ - NPRE) // 512 if own else None
                    os_ = None
                    if own:
                        os_ = rr["orT"] % 2
                        rr["orT"] += 1
                    pv7 = PS[7][:].bitcast(BF16)
                    out = []
                    for cc in range(nchk):
                        g = gch["n"]
                        gch["n"] += 1
                        ci = g % 2
                        first = (ti == 0 and cc == 0)
                        csl = slice(cc * 128, (cc + 1) * 128)
                        Rprev, B_Rprev = Rbs[(g + 1) % 2], B_Rbs[(g + 1) % 2]
                        Rnew, B_Rnew = Rbs[g % 2], B_Rbs[g % 2]
                        st = {}

                        def A(ci=ci, csl=csl):
                            mm_group(p, PS[4][:, 0:128], [(rkT[tl][:, b, csl], rqT[tl][:, b, csl]) for b in range(2)],
                                     [B_rkT[tl], B_rqT[tl]], [PSB[4]])
                            p.dve(lambda h: h.tensor_tensor(out=sm[ci][:], in0=PS[4][:, 0:128],
                                                            in1=cst[:, C_DT + 128 * hd:C_DT + 128 * (hd + 1)], op=ALU.mult),
                                  reads=[B_cst], writes=[PSB[4], B_sm[ci]])

                        def Dst(cc=cc, first=first, Rnew=Rnew, B_Rnew=B_Rnew):
                            if first:
                                p.pool(lambda h: h.memset(R32[:], 0.0), writes=[B_R32])
                            for b in range(2):
                                mm_group(p, PS[6][:, b * 256:(b + 1) * 256], [(rkt[tl][:, cc, b * 128:(b + 1) * 128], rvz[tl][:, cc, :])],
                                         [B_rkt[tl], B_rvz[tl]], [PSB[6]])
                            p.dve(lambda h: h.scalar_tensor_tensor(out=R32[:], in0=R32[:], scalar=cd, in1=PS[6][:],
                                                                   op0=ALU.mult, op1=ALU.add),
                                  writes=[PSB[6], B_R32])
                            p.pool(lambda h: h.tensor_copy(out=Rnew[:], in_=R32[:]), reads=[B_R32], writes=[B_Rnew])

                        def Bst(ci=ci, cc=cc, csl=csl, first=first, Rprev=Rprev, B_Rprev=B_Rprev):
                            pairs = [(sm[ci][:], rv[tl][:, cc, :])]
                            rds = [B_sm[ci], B_rv[tl]]
                            if not first:
                                pairs += [(rqX[tl][:, b, csl], Rprev[:, b * 256:(b + 1) * 256]) for b in range(2)]
                                rds += [B_rqX[tl], B_Rprev]
                            mm_group(p, PS[5][:, 0:256], pairs, rds, [PSB[5]])
                            p.act(lambda h: h.activation(out=osb[ci][:], in_=PS[5][:, 0:256], func=AF.Copy),
                                  writes=[PSB[5], B_osb[ci]])
                            p.dve(lambda h: h.scalar_tensor_tensor(out=sq[:], in0=osb[ci][:], scalar=1.0, in1=osb[ci][:],
                                                                   op0=ALU.mult, op1=ALU.mult, accum_out=st3[ci][:, 0:1]),
                                  reads=[B_osb[ci]], writes=[B_sq, B_st3[ci]])
                            p.dve(lambda h: h.tensor_scalar(out=st3[ci][:, 1:2], in0=st3[ci][:, 0:1], scalar1=1.0 / 256, scalar2=EPS,
                                                            op0=ALU.mult, op1=ALU.add), reads=[B_st3[ci]], writes=[B_st3[ci]])
                            p.pool(lambda h: h.tensor_tensor(out=st3[ci][:, 2:3], in0=st3[ci][:, 1:2], in1=negh, op=ALU.pow),
                                   reads=[B_st3[ci], B_cst], writes=[B_st3[ci]])
                            p.dve(lambda h: h.scalar_tensor_tensor(out=onb[ci][:], in0=osb[ci][:], scalar=st3[ci][:, 2:3],
                                                                   in1=sg[tl][:, cc, :], op0=ALU.mult, op1=ALU.mult),
                                  reads=[B_osb[ci], B_st3[ci], B_sg[tl]], writes=[B_onb[ci]])

                        def Cst(ci=ci, cc=cc, last=(cc == nchk - 1)):
                            def tfn2(h):
                                ins = None
                                for b in range(2):
                                    ins = h.transpose(pv7[:, b * 512 + cc * 128:b * 512 + (cc + 1) * 128], onb[ci][:, b * 128:(b + 1) * 128], ident)
                                return ins
                            p.pe(tfn2, reads=[B_onb[ci], B_cstb], writes=[PSB[7]])
                            if last:
                                p.act(lambda h: h.activation(out=orT[os_][:], in_=pv7.rearrange("p (b t) -> p b t", b=2), func=AF.Copy),
                                      writes=[PSB[7], B_orTt[os_]])
                                p.dma("sp", S_orT[os_], [(orrT_d[:, 2 * hd:2 * hd + 2, j * 512:(j + 1) * 512], orT[os_][:])],
                                      reads=[B_orTt[os_]], writes=[gbuf(B_orrT, (hd, j), "orrT")])
                        st["A"] = A if own else None
                        st["D"] = Dst
                        st["B"] = Bst if own else None
                        st["C"] = Cst if own else None
                        out.append(st)
                    return out

                seqL = [(hd, ti) for hd in range(4) for ti in range(len(tiles))]
                load_W3(0)
                for u in proj_units(seqL[0][0], seqL[0][1], 0):
                    u()
                pendC = None
                for i, (hd, ti) in enumerate(seqL):
                    if i < 20:
                        pre_cast_one()
                    R_ = rec_steps(hd, ti, i)
                    P_ = proj_units(seqL[i + 1][0], seqL[i + 1][1], i + 1) if i + 1 < len(seqL) else []
                    pi = 0
                    nslots = 2 * len(R_)
                    slot = 0

                    for k, stp in enumerate(R_):
                        if stp["A"] is not None:
                            stp["A"]()
                        stp["D"]()
                        for half in range(2):
                            cnt = -(-(len(P_) - pi) // (nslots - slot))
                            slot += 1
                            for _ in range(cnt):
                                P_[pi]()
                                pi += 1
                            if half == 0:
                                if stp["B"] is not None:
                                    stp["B"]()
                            else:
                                if pendC is not None:
                                    pendC()
                                    pendC = None
                                pendC = stp["C"]
                    while pi < len(P_):
                        P_[pi]()
                        pi += 1
                if pendC is not None:
                    pendC()
                    pendC = None
                p.emit_stage()

        if upto >= 4:
            with contextlib.ExitStack() as s4:
                Wa = [sb(s4, "s4wa%d" % i, [128, 8, 512], BF16) for i in range(2)]
                Wr = [sb(s4, "s4wr%d" % i, [128, 8, 512], BF16) for i in range(2)]
                Wga = [sb(s4, "s4wga%d" % i, [128, 16, 512], BF16) for i in range(2)]
                Wgr = [sb(s4, "s4wgr%d" % i, [128, 16, 512], BF16) for i in range(2)]
                B_W = [p.buf("s4w%d" % i) for i in range(2)]
                S_W = [p.dsem("s4w%d" % i) for i in range(2)]
                xt = [sb(s4, "s4x%d" % i, [128, 16, 512], BF16) for i in range(2)]
                at = [sb(s4, "s4a%d" % i, [128, 8, 512], BF16) for i in range(2)]
                rt_ = [sb(s4, "s4r%d" % i, [128, 8, 512], BF16) for i in range(2)]
                B_in = [p.buf("s4in%d" % i) for i in range(2)]
                S_in = [p.dsem("s4in%d" % i) for i in range(2)]
                sa = [sb(s4, "s4sa%d" % i, [128, 512], BF16) for i in range(2)]
                sr = [sb(s4, "s4sr%d" % i, [128, 512], BF16) for i in range(2)]
                B_sa = [p.buf("s4sa%d" % i) for i in range(2)]
                B_sr = [p.buf("s4sr%d" % i) for i in range(2)]
                m1 = [sb(s4, "s4m1%d" % i, [128, 512], F32) for i in range(2)]
                m2 = [sb(s4, "s4m2%d" % i, [128, 512], F32) for i in range(2)]
                B_m1 = [p.buf("s4m1%d" % i) for i in range(2)]
                B_m2 = [p.buf("s4m2%d" % i) for i in range(2)]
                mo = [sb(s4, "s4mo%d" % i, [128, 4, 512], BF16) for i in range(2)]
                B_mo = [p.buf("s4mo%d" % i) for i in range(2)]
                S_mo = [p.dsem("s4mo%d" % i) for i in range(2)]

                def load_W4(g):
                    s = g % 2
                    pairs = [(Wa[s][:], w_pa_v[:, :, 512 * g:512 * (g + 1)]),
                             (Wr[s][:], w_pr_v[:, :, 512 * g:512 * (g + 1)]),
                             (Wga[s][:], w_in_v[:, :, 7168 + 512 * g:7168 + 512 * (g + 1)]),
                             (Wgr[s][:], w_in_v[:, :, 9216 + 512 * g:9216 + 512 * (g + 1)])]
                    p.dma("pool", S_W[s], pairs, writes=[B_W[s]])
                load_W4(0)
                it = 0
                fbc = 0

                def ld4(itx):
                    ls_ = itx % 2
                    j_ = itx % 4
                    tsl_ = slice(j_ * 512, (j_ + 1) * 512)
                    p.dma("sp", S_in[ls_], [(xt[ls_][:], xnT_d[:, :, NPRE + j_ * 512:NPRE + (j_ + 1) * 512]),
                                            (at[ls_][:], oaT_d[:, :, tsl_]), (rt_[ls_][:], orrT_d[:, :, tsl_])],
                          reads=[B_xnT[NPCH + 4 * j_ + i] for i in range(4)] + [B_oaT[(hd, j_)] for hd in range(4)] + [B_orrT[(hd, j_)] for hd in range(4)],
                          writes=[B_in[ls_]])
                ld4(0)
                for g in range(4):
                    ws = g % 2
                    if g + 1 < 4:
                        load_W4(g + 1)
                    for j in range(4):
                        ls = it % 2
                        if it + 1 < 16:
                            ld4(it + 1)
                        it += 1
                        if it <= 8:
                            pre_cast_one()
                        tsl = slice(j * 512, (j + 1) * 512)
                        for fb in range(4):
                            bs = (fbc % 2) * 4
                            fs = fbc % 2
                            fbc += 1
                            fsl = slice(fb * 128, (fb + 1) * 128)
                            mm_group(p, PS[bs + 0][:], [(Wa[ws][:, kc, fsl], at[ls][:, kc, :]) for kc in range(8)], [B_W[ws], B_in[ls]], [PSB[bs + 0]])
                            mm_group(p, PS[bs + 1][:], [(Wga[ws][:, kc, fsl], xt[ls][:, kc, :]) for kc in range(16)], [B_W[ws], B_in[ls]], [PSB[bs + 1]])
                            mm_group(p, PS[bs + 2][:], [(Wr[ws][:, kc, fsl], rt_[ls][:, kc, :]) for kc in range(8)], [B_W[ws], B_in[ls]], [PSB[bs + 2]])
                            mm_group(p, PS[bs + 3][:], [(Wgr[ws][:, kc, fsl], xt[ls][:, kc, :]) for kc in range(16)], [B_W[ws], B_in[ls]], [PSB[bs + 3]])
                            p.act(lambda h, fs=fs, bs=bs: h.activation(out=sa[fs][:], in_=PS[bs + 1][:], func=AF.Sigmoid), writes=[PSB[bs + 1], B_sa[fs]])
                            p.act(lambda h, fs=fs, bs=bs: h.activation(out=sr[fs][:], in_=PS[bs + 3][:], func=AF.Sigmoid), writes=[PSB[bs + 3], B_sr[fs]])
                            p.dve(lambda h, fs=fs, bs=bs: h.tensor_tensor(out=m1[fs][:], in0=PS[bs + 0][:], in1=sa[fs][:], op=ALU.mult),
                                  reads=[B_sa[fs]], writes=[PSB[bs + 0], B_m1[fs]])
                            p.dve(lambda h, fs=fs, bs=bs: h.tensor_tensor(out=m2[fs][:], in0=PS[bs + 2][:], in1=sr[fs][:], op=ALU.mult),
                                  reads=[B_sr[fs]], writes=[PSB[bs + 2], B_m2[fs]])
                            p.pool(lambda h, fs=fs, ls=ls, fb=fb: h.tensor_tensor(out=mo[ls][:, fb, :], in0=m1[fs][:], in1=m2[fs][:], op=ALU.add),
                                   reads=[B_m1[fs], B_m2[fs]], writes=[B_mo[ls]])
                        p.dma("sp", S_mo[ls], [(mT_d[:, 4 * g:4 * g + 4, tsl], mo[ls][:])], reads=[B_mo[ls]],
                              writes=[gbuf(B_mT, (g, j), "mT")])
                p.emit_stage()

        if upto >= 5:
            with contextlib.ExitStack() as s5:
                Wo = sb(s5, "s5wo", [128, 16, D], BF16)
                B_Wo = p.buf("s5wo")
                S_Wo = p.dsem("s5wo")
                g2 = sb(s5, "s5g2", [128, D], F32)
                B_g2 = p.buf("s5g2")
                S_g2 = p.dsem("s5g2")
                NB5 = 3
                mt = [sb(s5, "s5m%d" % i, [128, 16, 128], BF16) for i in range(NB5)]
                xr = [sb(s5, "s5x%d" % i, [128, D], F32) for i in range(NB5)]
                B_in = [p.buf("s5in%d" % i) for i in range(NB5)]
                S_in = [p.dsem("s5in%d" % i) for i in range(NB5)]
                h1 = [sb(s5, "s5h%d" % i, [128, D], F32) for i in range(NB5)]
                B_h1t = [p.buf("s5h%d" % i) for i in range(NB5)]
                S_h1 = [p.dsem("s5h%d" % i) for i in range(NB5)]
                xn = [sb(s5, "s5n%d" % i, [128, D], BF16) for i in range(NB5)]
                B_xn = [p.buf("s5n%d" % i) for i in range(NB5)]
                st4 = [sb(s5, "s5s%d" % i, [128, 4], F32) for i in range(NB5)]
                B_st4 = [p.buf("s5s%d" % i) for i in range(NB5)]
                hT = [sb(s5, "s5T%d" % i, [128, 16, 128], BF16) for i in range(NB5)]
                B_hT = [p.buf("s5T%d" % i) for i in range(NB5)]
                S_hT = [p.dsem("s5T%d" % i) for i in range(NB5)]
                p.dma("pool", S_Wo, [(Wo[:, :, 512 * i:512 * (i + 1)], w_o_v[:, :, 512 * i:512 * (i + 1)]) for i in range(4)], writes=[B_Wo])
                p.dma("sp", S_g2, [(g2[:], gb_d[:, 1, :])], writes=[B_g2])
                def ld5(c):
                    s = c % NB5
                    p.dma("sp", S_in[s], [(mt[s][:], mT_d[:, :, c * 128:(c + 1) * 128]),
                                          (xr[s][:], xin[NPRE + c * 128:NPRE + (c + 1) * 128, :])],
                          reads=[B_mT[(g, c // 4)] for g in range(4)], writes=[B_in[s]])
                for c in range(NB5 - 1):
                    ld5(c)
                pend5 = None
                for c in range(16):
                    s = c % NB5
                    j = c // 4
                    if c + NB5 - 1 < 16:
                        ld5(c + NB5 - 1)
                    if c % 3 == 0:
                        pre_cast_one()
                    for nb in range(4):
                        bk = 4 + (c * 4 + nb) % 4
                        nsl = slice(nb * 512, (nb + 1) * 512)
                        mm_group(p, PS[bk][:], [(mt[s][:, kc, :], Wo[:, kc, nsl]) for kc in range(16)], [B_Wo, B_in[s]], [PSB[bk]])
                        p.dve(lambda h, s=s, bk=bk, nsl=nsl: h.tensor_tensor(out=h1[s][:, nsl], in0=PS[bk][:], in1=xr[s][:, nsl], op=ALU.add),
                              reads=[B_in[s]], writes=[PSB[bk], B_h1t[s]])
                    p.dma("sp", S_h1[s], [(h1_d[c * 128:(c + 1) * 128, :], h1[s][:])], reads=[B_h1t[s]], writes=[B_h1[c]])
                    part2 = norm_transpose(h1[s][:], B_h1t[s], g2[:], B_g2, xn[s], B_xn[s], st4[s], B_st4[s], hT[s], B_hT[s],
                                           (c % 2) * 2, (c % 2) * 2 + 1, defer=True)
                    if pend5 is not None:
                        pend5()

                    def pend5(part2=part2, s=s, c=c):
                        part2()
                        p.dma("sp", S_hT[s], [(hnT_d[:, :, c * 128:(c + 1) * 128], hT[s][:])], reads=[B_hT[s]], writes=[B_hnT[c]])
                pend5()
                p.emit_stage()

        if upto >= 6:
            with contextlib.ExitStack() as s6:
                NWB = 4
                wb = [sb(s6, "s6w%d" % i, [128, 16, 512], BF16) for i in range(NWB)]
                B_wb = [p.buf("s6w%d" % i) for i in range(NWB)]
                S_wb = [p.dsem("s6w%d" % i) for i in range(NWB)]
                gf = sb(s6, "s6gf", [128, D], F32)
                B_gf = p.buf("s6gf")
                S_gf = p.dsem("s6gf")
                ht = sb(s6, "s6ht", [128, 16, 512], BF16)
                B_ht = p.buf("s6ht")
                S_ht = p.dsem("s6ht")
                uT = sb(s6, "s6uT", [128, 64, 512], BF16)
                B_uT = [p.buf("s6uT%d" % i) for i in range(16)]
                rl = [sb(s6, "s6rl%d" % i, [128, 512], F32) for i in range(2)]
                B_rl = [p.buf("s6rl%d" % i) for i in range(2)]
                hh = sb(s6, "s6hh", [128, 4, D], F32)
                B_hh = [p.buf("s6hh%d" % i) for i in range(4)]
                S_hh = p.dsem("s6hh")
                st6 = sb(s6, "s6st", [128, 4, 4], F32)
                B_st6 = [p.buf("s6st%d" % i) for i in range(4)]
                S_yo = [p.dsem("s6yo%d" % i) for i in range(4)]
                p.dma("sp", S_gf, [(gf[:], gb_d[:, 2, :])], writes=[B_gf])
                wi = 0
                ri = 0
                yi = 0
                for j in range(4):
                    if j == 0:
                        p.dma("sp", S_ht, [(ht[:], hnT_d[:, :, j * 512:(j + 1) * 512])],
                              reads=[B_hnT[4 * j + i] for i in range(4)], writes=[B_ht])
                    p.dma("sp", S_hh, [(hh[:, m, :], h1_d[(4 * j + m) * 128:(4 * j + m + 1) * 128, :]) for m in range(4)],
                          reads=[B_h1[4 * j + m] for m in range(4)], writes=B_hh)
                    for ub in range(16):
                        s = wi % NWB
                        wi += 1
                        p.dma("pool", S_wb[s], [(wb[s][:], wupb_v[:, :, ub * 512:(ub + 1) * 512])], reads=[B_wpre], writes=[B_wb[s]])
                        for q in range(4):
                            bk = (ub * 4 + q) % 4
                            r_ = ri % 2
                            ri += 1
                            mm_group(p, PS[bk][:], [(wb[s][:, kc, q * 128:(q + 1) * 128], ht[:, kc, :]) for kc in range(16)],
                                     [B_wb[s], B_ht], [PSB[bk]])
                            p.act(lambda h, r_=r_, bk=bk: h.activation(out=rl[r_][:], in_=PS[bk][:], func=AF.Relu),
                                  writes=[PSB[bk], B_rl[r_]])
                            p.dve(lambda h, r_=r_, ub=ub, q=q: h.tensor_tensor(out=uT[:, ub * 4 + q, :], in0=rl[r_][:], in1=rl[r_][:], op=ALU.mult),
                                  reads=[B_rl[r_]], writes=[B_uT[ub]])
                    if j + 1 < 4:
                        p.dma("sp", S_ht, [(ht[:], hnT_d[:, :, (j + 1) * 512:(j + 2) * 512])],
                              reads=[B_hnT[4 * (j + 1) + i] for i in range(4)], writes=[B_ht])
                    for nb in range(4):
                        nsl = slice(nb * 512, (nb + 1) * 512)
                        for gq in range(4):
                            s = wi % NWB
                            wi += 1
                            p.dma("pool", S_wb[s], [(wb[s][:], wdnb_v[:, gq * 16:(gq + 1) * 16, nsl])], reads=[B_wpre], writes=[B_wb[s]])
                            for m in range(4):
                                mm_group(p, PS[4 + m][:], [(uT[:, gq * 16 + f, m * 128:(m + 1) * 128], wb[s][:, f, :]) for f in range(16)],
                                         [B_wb[s]] + B_uT[gq * 4:(gq + 1) * 4], [PSB[4 + m]], start=(gq == 0), stop=(gq == 3))
                        for m in range(4):
                            p.dve(lambda h, m=m, nsl=nsl: h.tensor_tensor(out=hh[:, m, nsl], in0=PS[4 + m][:], in1=hh[:, m, nsl], op=ALU.add),
                                  writes=[PSB[4 + m], B_hh[m]])
                    for m in range(4):
                        ys = yi % 2
                        yi += 1
                        p.act(lambda h, m=m: h.activation(out=uT[:, 0:4, :], in_=hh[:, m, :].rearrange("p (a b) -> p a b", a=4),
                                                          func=AF.Square, accum_out=st6[:, m, 0:1]),
                              reads=[B_hh[m]], writes=[B_uT[0], B_st6[m]])
                        p.dve(lambda h, m=m: h.tensor_scalar(out=st6[:, m, 1:2], in0=st6[:, m, 0:1], scalar1=1.0 / D, scalar2=EPS,
                                                             op0=ALU.mult, op1=ALU.add), reads=[B_st6[m]], writes=[B_st6[m]])
                        p.pool(lambda h, m=m: h.tensor_tensor(out=st6[:, m, 2:3], in0=st6[:, m, 1:2], in1=negh, op=ALU.pow),
                               reads=[B_st6[m], B_cst], writes=[B_st6[m]])
                        p.dve(lambda h, m=m: h.scalar_tensor_tensor(out=hh[:, m, :], in0=hh[:, m, :], scalar=st6[:, m, 2:3], in1=gf[:],
                                                                    op0=ALU.mult, op1=ALU.mult),
                              reads=[B_st6[m], B_gf], writes=[B_hh[m]])
                        c = 4 * j + m
                        p.dma("sp", S_yo[m], [(y[c * 128:(c + 1) * 128, :], hh[:, m, :])], reads=[B_hh[m]], writes=[B_y[c]])
                p.op("sp", lambda h: h.nop(), reads=B_y)
                p.emit_stage()
        else:
            allb = [b for b in p.bufs if (b.w is not None and b.w.is_dma)]
            p.op("sp", lambda h: h.nop(), reads=allb)
            p.emit_stage()
    return nc


def _consts():
    c = np.zeros((128, NCST), np.float32)
    c[:, C_ID:C_ID + 128] = np.eye(128, dtype=np.float32)
    pm = np.zeros((128, 128), np.float32)
    for i in range(128):
        pm[(i + 64) % 128, i] = 1.0
    c[:, C_PM:C_PM + 128] = pm
    kk = np.arange(128)[:, None]
    qq = np.arange(128)[None, :]
    c[:, C_MK:C_MK + 128] = (qq >= kk).astype(np.float32)
    lg = np.log(np.float32(1.0) - np.float32(2.0) ** (-5.0 - np.arange(4, dtype=np.float32))).astype(np.float32)
    for h in range(4):
        rel = (qq - kk).astype(np.float32)
        dm = np.where(rel >= 0, np.exp(lg[h] * np.maximum(rel, 0.0)), 0.0).astype(np.float32)
        c[:, C_DT + 128 * h:C_DT + 128 * (h + 1)] = dm * np.float32(256.0 ** -0.5)
        xi = np.exp(lg[h] * (np.arange(128, dtype=np.float32) + 1.0)).astype(np.float32)
        c[:, C_XI + 512 * h:C_XI + 512 * (h + 1)] = np.tile(xi, 4)[None, :]
        ze = np.exp(lg[h] * (127.0 - np.arange(128, dtype=np.float32))).astype(np.float32)
        c[:, C_ZE + h] = ze * np.float32(256.0 ** -0.5)
    c[:, C_NH] = -0.5
    return c


def _rope_tables(pos):
    inv_a = (np.float32(10000.0) ** (-np.arange(64, dtype=np.float32) / np.float32(64))).astype(np.float32)
    ang = pos.astype(np.float32)[:, None] * inv_a[None, :]
    ca = np.cos(ang).astype(np.float32).T
    sa = np.sin(ang).astype(np.float32).T
    da = np.zeros((128, 2, NTOK), np.float32)
    da[0:64, 0] = ca
    da[64:128, 0] = ca
    da[0:64, 1] = -sa
    da[64:128, 1] = sa
    inv_r = (np.float32(10000.0) ** (-np.arange(128, dtype=np.float32) / np.float32(128))).astype(np.float32)
    angr = pos.astype(np.float32)[:, None] * inv_r[None, :]
    rr = np.zeros((128, 2, NTOK), np.float32)
    rr[:, 0] = np.cos(angr).astype(np.float32).T
    rr[:, 1] = np.sin(angr).astype(np.float32).T
    return da, rr


def _prep(x, meta_tokens, norm1_g, w_in, lam_q1, lam_k1, lam_q2, lam_k2, da_subln_g,
          w_pa, w_pr, w_o, norm2_g, w_up, w_down, normf_g):
    f = np.float32
    base = _consts()
    gb = np.zeros((128, 3, D), f)
    gb[:, 0] = np.asarray(norm1_g, f).reshape(1, D)
    gb[:, 1] = np.asarray(norm2_g, f).reshape(1, D)
    gb[:, 2] = np.asarray(normf_g, f).reshape(1, D)
    shared = {
        "gb": gb,
        "w_in": np.ascontiguousarray(np.asarray(w_in, f).reshape(D, INW)),
        "w_pa": np.ascontiguousarray(np.asarray(w_pa, f).reshape(1024, D)),
        "w_pr": np.ascontiguousarray(np.asarray(w_pr, f).reshape(1024, D)),
        "w_o": np.ascontiguousarray(np.asarray(w_o, f).reshape(D, D)),
        "w_up": np.ascontiguousarray(np.asarray(w_up, f).reshape(D, DFF)),
        "w_down": np.ascontiguousarray(np.asarray(w_down, f).reshape(DFF, D)),
    }
    x = np.asarray(x, f)
    meta = np.asarray(meta_tokens, f)
    in_maps = []
    for c in range(8):
        b, half = c // 2, c % 2
        xin = np.zeros((NTOK, D), f)
        pos = np.zeros((NTOK,), f)
        valid = np.zeros((NTOK,), f)
        if half == 0:
            xin[NPRE - 16:NPRE] = meta
            pos[NPRE - 16:NPRE] = np.arange(16)
            valid[NPRE - 16:NPRE] = 1
            xin[NPRE:] = x[b, 0:NOWN]
            pos[NPRE:] = 16 + np.arange(NOWN)
            valid[NPRE:] = 1
        else:
            xin[112:128] = meta
            pos[112:128] = np.arange(16)
            valid[112:128] = 1
            xin[128:NPRE] = x[b, 0:NOWN]
            pos[128:NPRE] = 16 + np.arange(NOWN)
            valid[128:NPRE] = 1
            xin[NPRE:] = x[b, NOWN:2 * NOWN]
            pos[NPRE:] = 16 + NOWN + np.arange(NOWN)
            valid[NPRE:] = 1
        cst = base.copy()
        cst[:, C_VA:C_VA + NCH] = valid.reshape(NCH, 128).T
        cst[:, C_GS:C_GS + 256] = np.asarray(da_subln_g, f).reshape(1, 256)
        for i, v in enumerate((lam_q1, lam_k1, lam_q2, lam_k2)):
            cst[:, C_LV + 128 * i:C_LV + 128 * (i + 1)] = np.asarray(v, f).reshape(1, 128)
        da, rr = _rope_tables(pos)
        m = {"xin": xin, "ropeda": da, "roper": rr, "cst": cst}
        m.update(shared)
        in_maps.append(m)
    return in_maps


def kernel(**inputs):
    in_maps = _prep(**inputs)
    nc = build()
    res = run_bass_kernel_spmd(nc, in_maps, core_ids=list(range(8)))
    out = np.zeros((4, 2 * NOWN, D), np.float32)
    for c in range(8):
        b, half = c // 2, c % 2
        out[b, half * NOWN:(half + 1) * NOWN] = res.results[c]["y"]
    return out
```
